# Optimizing a Trainium2 kernel written in Bass

```python
import jax, jax.numpy as jnp
from jax import lax
import numpy as np

D_MODEL = 1024
BATCH = 4
SEQ = 8192
DEPTH = 2

GRID_W = 64
HEAD_DIM = 64
NA_HEADS = D_MODEL // 256
NA_WIDTH = NA_HEADS * HEAD_DIM
NA_KR_MAX = 8
NA_KC = 16
MLA_HEADS = D_MODEL // 128
MLA_NOPE = 64
MLA_ROPE = 32
MLA_V = 64
MLA_Q_RANK = 384
MLA_KV_RANK = 256
MLA_WIDTH = MLA_HEADS * MLA_V
ROPE_THETA = 10000.0
Q_BLOCK = 128
CONV_WIDTH = D_MODEL // 4
CONV_K = 3
D_MIX = NA_WIDTH + MLA_WIDTH + CONV_WIDTH
D_IN = 3 * NA_WIDTH + MLA_Q_RANK + MLA_KV_RANK + MLA_ROPE + 3 * CONV_WIDTH
OUT_GROUPS = D_MIX // HEAD_DIM
D_FF = -(-8 * D_MODEL // (3 * 256)) * 256
EPS = 1e-6

kernel_name = 'hybrid_na_mla_shortconv_encoder'


def rms_norm(x, g):
    xf = x.astype(jnp.float32)
    y = xf * lax.rsqrt(jnp.mean(xf * xf, axis=-1, keepdims=True) + EPS)
    return (y * g.astype(jnp.float32)).astype(x.dtype)


def ada_norm(x, g, shift, scale):
    return rms_norm(x, g) * (1 + scale) + shift


def rope_tables(positions):
    inv = ROPE_THETA ** (-jnp.arange(0, MLA_ROPE, 2, dtype=jnp.float32) / MLA_ROPE)
    ang = positions.astype(jnp.float32)[..., None] * inv
    return jnp.cos(ang), jnp.sin(ang)


def apply_rope(x, cos, sin):
    xf = x.astype(jnp.float32)
    x1, x2 = jnp.split(xf, 2, axis=-1)
    return jnp.concatenate([x1 * cos - x2 * sin, x2 * cos + x1 * sin], axis=-1).astype(x.dtype)


def neighbourhood_attention(q, k, v, rpb):
    b, t, h, dh = q.shape
    rows = t // GRID_W
    kr = min(NA_KR_MAX, rows)
    kc = NA_KC
    qg = q.reshape(b, rows, GRID_W, h, dh)
    kg = k.reshape(b, rows, GRID_W, h, dh)
    vg = v.reshape(b, rows, GRID_W, h, dh)
    r = jnp.arange(rows)
    row_start = jnp.clip(r - kr // 2, 0, rows - kr)
    row_idx = row_start[:, None] + jnp.arange(kr)[None, :]
    k_band = kg[:, row_idx]
    v_band = vg[:, row_idx]
    s = jnp.einsum('brqhd,brjwhd->bhrqjw', qg, k_band).astype(jnp.float32) * (dh ** -0.5)
    cols = jnp.arange(GRID_W)
    col_start = jnp.clip(cols - kc // 2, 0, GRID_W - kc)
    col_ok = (cols[None, :] >= col_start[:, None]) & (cols[None, :] < col_start[:, None] + kc)
    dr_i = row_idx - r[:, None] + NA_KR_MAX - 1
    dc_i = jnp.clip(cols[None, :] - cols[:, None] + kc - 1, 0, 2 * kc - 2)
    bias = rpb[:, dr_i[:, None, :, None], dc_i[None, :, None, :]]
    s = jnp.where(col_ok[:, None, :], s + bias.astype(jnp.float32)[None], -jnp.inf)
    p = jax.nn.softmax(s.reshape(b, h, rows, GRID_W, kr * GRID_W), axis=-1)
    p = p.reshape(s.shape).astype(v.dtype)
    o = jnp.einsum('bhrqjw,brjwhd->brqhd', p, v_band)
    return o.reshape(b, t, h * dh)


def latent_attention(c_q, c_kv, k_rope, q_a_g, kv_a_g, w_uq, w_ukv, qn_g, kn_g, qr_g, kr_g, cos, sin):
    b, t, _ = c_q.shape
    q = (rms_norm(c_q, q_a_g) @ w_uq).reshape(b, t, MLA_HEADS, MLA_NOPE + MLA_ROPE)
    kv = (rms_norm(c_kv, kv_a_g) @ w_ukv).reshape(b, t, MLA_HEADS, MLA_NOPE + MLA_V)
    q_nope = rms_norm(q[..., :MLA_NOPE], qn_g)
    q_rope = apply_rope(rms_norm(q[..., MLA_NOPE:], qr_g), cos[:, :, None], sin[:, :, None])
    k_nope = rms_norm(kv[..., :MLA_NOPE], kn_g)
    v = kv[..., MLA_NOPE:]
    k_r = apply_rope(rms_norm(k_rope, kr_g), cos, sin)
    scale = (MLA_NOPE + MLA_ROPE) ** -0.5
    nb = t // Q_BLOCK

    def to_blocks(a):
        return jnp.moveaxis(a.reshape(b, nb, Q_BLOCK, *a.shape[2:]), 1, 0)

    def block(qs):
        qn, qr = qs
        s = jnp.einsum('bqhd,bkhd->bhqk', qn, k_nope) + jnp.einsum('bqhr,bkr->bhqk', qr, k_r)
        p = jax.nn.softmax(s.astype(jnp.float32) * scale, axis=-1).astype(v.dtype)
        return jnp.einsum('bhqk,bkhd->bqhd', p, v)

    o = lax.map(block, (to_blocks(q_nope), to_blocks(q_rope)))
    return jnp.moveaxis(o, 0, 1).reshape(b, t, MLA_WIDTH)


def short_conv(x_in, gate_b, gate_c, conv_w, conv_b):
    u = gate_c * x_in
    y = lax.conv_general_dilated(u, conv_w[:, None, :], window_strides=(1,),
                                 padding=((CONV_K // 2, CONV_K // 2),),
                                 dimension_numbers=('NWC', 'WIO', 'NWC'),
                                 feature_group_count=CONV_WIDTH) + conv_b
    return gate_b * y


def setup_inputs(seed: int = 0) -> dict:
    key = jax.random.key(seed)
    ks = jax.random.split(key, 24)
    f32 = jnp.float32

    def dense(k, shape, fan_in, mult=1.0):
        return jax.random.normal(k, shape, f32) * (mult * fan_in ** -0.5)

    def gain(k, shape):
        return 1.0 + 0.05 * jax.random.normal(k, shape, f32)

    def small(k, shape, s):
        return s * jax.random.normal(k, shape, f32)

    L = DEPTH
    return {
        'x': jax.random.normal(ks[0], (BATCH, SEQ, D_MODEL), f32),
        'c': jax.random.normal(ks[1], (BATCH, D_MODEL), f32),
        'positions': jnp.broadcast_to(jnp.arange(SEQ, dtype=jnp.int32), (BATCH, SEQ)),
        'norm1_g': gain(ks[2], (L, D_MODEL)),
        'norm2_g': gain(ks[3], (L, D_MODEL)),
        'w_ada': dense(ks[4], (L, D_MODEL, 6 * D_MODEL), D_MODEL, 0.5),
        'b_ada': small(ks[5], (L, 6 * D_MODEL), 0.02),
        'w_in': dense(ks[6], (L, D_MODEL, D_IN), D_MODEL),
        'na_q_g': gain(ks[7], (L, HEAD_DIM)),
        'na_k_g': gain(ks[8], (L, HEAD_DIM)),
        'na_rpb': small(ks[9], (L, NA_HEADS, 2 * NA_KR_MAX - 1, 2 * NA_KC - 1), 0.05),
        'mla_q_a_g': gain(ks[10], (L, MLA_Q_RANK)),
        'mla_kv_a_g': gain(ks[11], (L, MLA_KV_RANK)),
        'mla_w_uq': dense(ks[12], (L, MLA_Q_RANK, MLA_HEADS * (MLA_NOPE + MLA_ROPE)), MLA_Q_RANK),
        'mla_w_ukv': dense(ks[13], (L, MLA_KV_RANK, MLA_HEADS * (MLA_NOPE + MLA_V)), MLA_KV_RANK),
        'mla_qn_g': gain(ks[14], (L, MLA_NOPE)),
        'mla_kn_g': gain(ks[15], (L, MLA_NOPE)),
        'mla_qr_g': gain(ks[16], (L, MLA_ROPE)),
        'mla_kr_g': gain(ks[17], (L, MLA_ROPE)),
        'conv_w': dense(ks[18], (L, CONV_K, CONV_WIDTH), CONV_K),
        'conv_b': small(ks[19], (L, CONV_WIDTH), 0.02),
        'out_norm_g': gain(ks[20], (L, D_MIX)),
        'w_out': dense(ks[21], (L, D_MIX, D_MODEL), D_MIX),
        'w_gu': dense(ks[22], (L, D_MODEL, 2 * D_FF), D_MODEL),
        'w_down': dense(ks[23], (L, D_FF, D_MODEL), D_FF),
    }


def reference(x, c, positions, norm1_g, norm2_g, w_ada, b_ada, w_in, na_q_g, na_k_g, na_rpb,
              mla_q_a_g, mla_kv_a_g, mla_w_uq, mla_w_ukv, mla_qn_g, mla_kn_g, mla_qr_g, mla_kr_g,
              conv_w, conv_b, out_norm_g, w_out, w_gu, w_down):
    b, t, _ = x.shape
    cos, sin = rope_tables(positions)
    c_act = jax.nn.silu(c)
    i0 = 3 * NA_WIDTH
    i1 = i0 + MLA_Q_RANK
    i2 = i1 + MLA_KV_RANK
    i3 = i2 + MLA_ROPE
    for l in range(DEPTH):
        mod = c_act @ w_ada[l] + b_ada[l]
        sh1, sc1, g1, sh2, sc2, g2 = jnp.split(mod[:, None, :], 6, axis=-1)
        h = ada_norm(x, norm1_g[l], sh1, sc1)
        proj = h @ w_in[l]
        na_qkv, c_q, c_kv, k_rope, conv_in = jnp.split(proj, [i0, i1, i2, i3], axis=-1)
        q, k, v = jnp.split(na_qkv, 3, axis=-1)
        q = rms_norm(q.reshape(b, t, NA_HEADS, HEAD_DIM), na_q_g[l])
        k = rms_norm(k.reshape(b, t, NA_HEADS, HEAD_DIM), na_k_g[l])
        v = v.reshape(b, t, NA_HEADS, HEAD_DIM)
        y_na = neighbourhood_attention(q, k, v, na_rpb[l])
        y_mla = latent_attention(c_q, c_kv, k_rope, mla_q_a_g[l], mla_kv_a_g[l], mla_w_uq[l],
                                 mla_w_ukv[l], mla_qn_g[l], mla_kn_g[l], mla_qr_g[l], mla_kr_g[l],
                                 cos, sin)
        x_in, gate_b, gate_c = jnp.split(conv_in, 3, axis=-1)
        y_conv = short_conv(x_in, gate_b, gate_c, conv_w[l], conv_b[l])
        mixed = jnp.concatenate([y_na, y_mla, y_conv], axis=-1)
        mixed = rms_norm(mixed.reshape(b, t, OUT_GROUPS, HEAD_DIM),
                         out_norm_g[l].reshape(OUT_GROUPS, HEAD_DIM)).reshape(b, t, D_MIX)
        x = x + g1 * (mixed @ w_out[l])
        h2 = ada_norm(x, norm2_g[l], sh2, sc2)
        gt, up = jnp.split(h2 @ w_gu[l], 2, axis=-1)
        x = x + g2 * ((jax.nn.silu(gt) * up) @ w_down[l])
    return x
```

```python
from contextlib import ExitStack
import math
import numpy as np
import ml_dtypes
import concourse.bass as bass
import concourse.mybir as mybir
from concourse.bass_utils import run_bass_kernel_spmd

F32 = mybir.dt.float32
BF16 = mybir.dt.bfloat16
I32 = mybir.dt.int32
ALU = mybir.AluOpType
AF = mybir.ActivationFunctionType

ENGS = ("pe", "act", "dve", "pool", "sp")


class Res:
    __slots__ = ("name", "w", "r")

    def __init__(self, name):
        self.name = name
        self.w = {}
        self.r = []


class Prog:
    def __init__(self, nc, es):
        self.nc = nc
        self.es = es
        self.q = {k: [] for k in ENGS}
        self.sem = {}
        self.cnt = {}
        self.seen = {k: {} for k in ENGS}
        self.cur = {}
        self.pe_pending = False
        self.epoch = 0
        self.new_epoch()

    def mksem(self, name):
        if name in self.sem:
            return name
        h = self.es.enter_context(self.nc.semaphore(name))
        self.sem[name] = h
        self.cnt[name] = 0
        return name

    def new_epoch(self):
        self.epoch += 1
        for k in ("pe", "act", "dve", "pool"):
            self.cur[k] = self.mksem(f"s_{k}_{self.epoch}")

    def _waits(self, eng, reads, writes, pwrites=(), guards=()):
        need = {}

        def add(s, v):
            if need.get(s, 0) < v:
                need[s] = v

        for r in reads:
            for s, v in r.w.items():
                add(s, v)
        for w in tuple(writes) + tuple(guards):
            for s, v in w.w.items():
                add(s, v)
            for s, v in w.r:
                add(s, v)
        for w in pwrites:
            for s, v in w.r:
                add(s, v)
        out = []
        seen = self.seen[eng]
        for s, v in need.items():
            if seen.get(s, 0) >= v:
                continue
            seen[s] = v
            out.append((s, v))
        return out

    def op(self, eng, meth, reads=(), writes=(), sig=True, pwrites=(), guards=(), **kw):
        waits = self._waits(eng, reads, writes, pwrites, guards)
        if eng == "pe":
            waits = [(s, v) for (s, v) in waits if s != self.cur["pe"]]
        s = self.cur[eng]
        if sig:
            self.cnt[s] += 1
            ev = (s, self.cnt[s])
            if eng == "pe":
                self.pe_pending = False
        else:
            assert eng == "pe"
            ev = (s, self.cnt[s] + 1)
            self.pe_pending = True
        wl = [(self.sem[a], b) for a, b in waits]
        sh = self.sem[s]

        def run(e, meth=meth, kw=kw, wl=wl, sig=sig, sh=sh):
            for h, v in wl:
                e.wait_ge(h, v)
            ins = getattr(e, meth)(**kw)
            if sig:
                ins.then_inc(sh, 1)

        self.q[eng].append(run)
        for r in reads:
            r.r.append(ev)
        for w in writes:
            w.w = {ev[0]: ev[1]}
            w.r = []
        for w in pwrites:
            w.w[ev[0]] = ev[1]
        return ev

    def dma(self, qeng, semname, out, in_, reads=(), writes=(), pwrites=(), guards=(), **kw):
        self.mksem(semname)
        waits = self._waits(qeng, reads, writes, pwrites, guards)
        self.cnt[semname] += 16
        ev = (semname, self.cnt[semname])
        wl = [(self.sem[a], b) for a, b in waits]
        sh = self.sem[semname]

        def run(e, wl=wl, sh=sh, out=out, in_=in_, kw=kw):
            for h, v in wl:
                e.wait_ge(h, v)
            e.dma_start(out=out, in_=in_, **kw).then_inc(sh, 16)

        self.q[qeng].append(run)
        for r in reads:
            r.r.append(ev)
        for w in writes:
            w.w = {ev[0]: ev[1]}
            w.r = []
        for w in pwrites:
            w.w[ev[0]] = ev[1]
        return ev

    def final_wait(self, eng, resources):
        waits = self._waits(eng, resources, ())
        wl = [(self.sem[a], b) for a, b in waits]

        def run(e, wl=wl):
            for h, v in wl:
                e.wait_ge(h, v)

        self.q[eng].append(run)

    def replay(self):
        assert not self.pe_pending, "unsignalled PE group at end"
        q = self.q
        with self.nc.Block() as block:
            @block.tensor
            def _(e):
                for f in q["pe"]:
                    f(e)

            @block.scalar
            def _(e):
                for f in q["act"]:
                    f(e)

            @block.vector
            def _(e):
                for f in q["dve"]:
                    f(e)

            @block.gpsimd
            def _(e):
                for f in q["pool"]:
                    f(e)

            @block.sync
            def _(e):
                for f in q["sp"]:
                    f(e)
        self.q = {k: [] for k in ENGS}


D = 1024
NT = 4096
NB = 512
NBLK = NT // NB
SEQ = 8192
DFF = 2816
NJ = DFF // 128
EPS = 1e-6
NAQ, NAK, NAV, CQ, CKV, KR, XIN, GB, GC, WIN = 0, 256, 512, 768, 1152, 1408, 1472, 1728, 1984, 2240
NPAR = 96
P_G1, P_G2, P_NAQ, P_NAK, P_CQ, P_CKV, P_QH, P_KN, P_KR, P_CW, P_CB, P_ON, P_ONC, P_BADA = \
    0, 8, 16, 17, 18, 21, 23, 24, 25, 26, 32, 34, 46, 48
SC_NA = 64 ** -0.5
SC_MLA = 96 ** -0.5
TWO_PI = 2.0 * math.pi
CW1 = 6.28125
CW2 = TWO_PI - CW1
NHALO = 256
DBG = {}


class B:
    def __init__(self, nc, es):
        self.nc = nc
        self.es = es
        self.P = Prog(nc, es)
        self.ps = []
        self.psr = []
        self.psbig = []
        for i in range(4):
            big = es.enter_context(nc.psum_tensor(f"psb{i}", [128, 1024], F32))
            self.psbig.append(big)
            for j in range(2):
                self.ps.append(big[:, j * 512:(j + 1) * 512])
                self.psr.append(Res(f"ps{2 * i + j}"))
        self.psi = 0
        self.uid = 0
        self.rr = {}

    def psum(self):
        i = self.psi
        self.psi = (self.psi + 1) % 8
        return self.ps[i], self.psr[i]

    def sb(self, es, name, shape, dt, nres=1):
        self.uid += 1
        t = es.enter_context(self.nc.sbuf_tensor(f"t{self.uid}_{name}", shape, dt))
        if nres == 1:
            return t, Res(name)
        return t, [Res(f"{name}{i}") for i in range(nres)]

    def pool(self, es, name, n, shape, dt):
        self.rr[name] = [0, [(self.sb(es, f"{name}{i}", shape, dt)) for i in range(n)]]

    def nxt(self, name):
        st = self.rr[name]
        i = st[0]
        st[0] = (i + 1) % len(st[1])
        t, r = st[1][i]
        return t, r, f"{name}{i}"

    def mm(self, out_ap, out_res, items):
        n = len(items)
        for i, (l, r, rd) in enumerate(items):
            first, last = i == 0, i == n - 1
            self.P.op("pe", "matmul", reads=rd,
                      writes=[out_res] if last else (),
                      guards=[out_res] if (first and not last) else (),
                      sig=last, out=out_ap, lhsT=l, rhs=r, start=first, stop=last)


def ext(nc, name, shape, dt, kind):
    return nc.dram_tensor(name, list(shape), dt, kind=kind).ap()


def prep_common(b, es, d, nl=2):
    P = b.P
    cm, r_cm = b.sb(es, "cmats", [128, 6, 128], BF16)
    P.dma("pool", "l_cm", cm[:], d["cmats"][:, :, :], writes=[r_cm])
    one, r_one = b.sb(es, "one", [128, 2], F32)
    P.op("dve", "memset", writes=[r_one], ap=one[:], constant=1.0)
    ccol, r_cc = b.sb(es, "ccol", [128, 8], F32)
    P.dma("sp", "l_cc", ccol[:], d["ccol"][:, :], writes=[r_cc])
    cact, r_ca = b.sb(es, "cact", [128, 8], F32)
    P.op("act", "activation", reads=[r_cc], writes=[r_ca], out=cact[:], in_=ccol[:], func=AF.Silu)
    out = []
    pars = []
    for l in range(nl):
        par, r_par = b.sb(es, f"par{l}", [128, NPAR], F32)
        P.dma("sp", f"l_par{l}", par[:], d["params"][l, :, :], writes=[r_par])
        modc, r_mc = b.sb(es, f"modc{l}", [128, 48], F32)
        pars.append((par, r_par, modc, r_mc))
    with ExitStack() as es2:
        modrow, r_mr = b.sb(es2, "modrow", [1, 6144], F32)
        wa = [b.sb(es2, f"wa{i}", [128, 8, 512], F32) for i in range(2)]
        n = 0
        for l in range(nl):
            par, r_par, modc, r_mc = pars[l]
            w_ada = d["w_ada"][l].rearrange("(kc p) n -> p kc n", p=128)
            for jc in range(12):
                wt, wr = wa[n % 2]
                P.dma("sp", f"l_wa{n % 2}", wt[:], w_ada[:, :, jc * 512:(jc + 1) * 512], writes=[wr])
                n += 1
                ps, pr = b.psum()
                b.mm(ps[0:1, :], pr, [(cact[:, kc:kc + 1], wt[:, kc, :], [r_ca, wr]) for kc in range(8)])
                P.op("dve", "tensor_copy", reads=[pr], pwrites=[r_mr], guards=[r_mr] if jc == 0 else (),
                     out=modrow[0:1, jc * 512:(jc + 1) * 512], in_=ps[0:1, :])
            ps, pr = b.psum()
            for j in range(48):
                P.op("pe", "matmul", reads=[r_mr, r_one], writes=[pr] if j == 47 else (), guards=[pr] if j == 0 else (),
                     sig=(j == 47), out=ps[:, j:j + 1], lhsT=modrow[0:1, j * 128:(j + 1) * 128], rhs=one[0:1, 0:1],
                     start=True, stop=True)
            P.op("dve", "tensor_tensor", reads=[pr, r_par], writes=[r_mc], out=modc[:], in0=ps[:, 0:48],
                 in1=par[:, P_BADA:P_BADA + 48], op=ALU.add)
            out.append(dict(par=par, r_par=r_par, cm=cm, r_cm=r_cm, modc=modc, r_mc=r_mc))
        P.replay()
    return out


A_IN = [("xT", [D, NT], F32), ("params", [128, NPAR], F32), ("cmats", [128, 6, 128], F32),
        ("ccol", [128, 8], F32), ("w_ada", [D, 6144], F32), ("rc", [128, 4], F32),
        ("pos", [1, NT], I32), ("w_in_p", [D, WIN], F32), ("w_uq_p", [384, 1024], F32),
        ("w_ukv_k", [256, 512], F32), ("w_ukv_v", [256, 512], F32)]
A_OUT = [("qT", [8, 128, NT], BF16), ("kT", [8, 128, NT], BF16), ("vE", [NT, 520], BF16),
         ("naqT", [256, NT], BF16), ("nakT", [256, NT], BF16), ("navE", [NT, 260], BF16),
         ("uT", [256, NT], F32), ("gbT", [256, NT], F32)]


def phase1(b, es, d, o, R=None):
    P = b.P
    t = d["_prep"]
    par, r_par, cm, r_cm, modc, r_mc = t["par"], t["r_par"], t["cm"], t["r_cm"], t["modc"], t["r_mc"]
    ONES, B64, BQ, BK, DUP = (cm[:, i, :] for i in range(5))
    a1, r_a1 = b.sb(es, "a1", [128, 8], F32)
    P.op("dve", "scalar_tensor_tensor", reads=[r_mc, r_par], writes=[r_a1], out=a1[:], in0=modc[:, 8:16], scalar=1.0,
         in1=par[:, P_G1:P_G1 + 8], op0=ALU.add, op1=ALU.mult)
    gs, r_gs = b.sb(es, "gs", [128, 2], F32)
    P.op("dve", "tensor_scalar", reads=[r_par], writes=[r_gs], out=gs[:, 0:1], in0=par[:, P_NAQ:P_NAQ + 1],
         scalar1=SC_NA, scalar2=None, op0=ALU.mult)
    P.op("dve", "tensor_scalar", reads=[r_par], pwrites=[r_gs], out=gs[:, 1:2], in0=par[:, P_QH:P_QH + 1],
         scalar1=SC_MLA, scalar2=None, op0=ALU.mult)
    rc, r_rc = b.sb(es, "rc", [128, 4], F32)
    P.dma("sp", "l_rc", rc[:], d["rc"][:, :], writes=[r_rc])
    TT, r_TT = b.sb(es, "TT", [128, NT], F32)
    TK, r_TK = b.sb(es, "TK", [64, NT], F32)
    with ExitStack() as es2:
        posi, r_pi = b.sb(es2, "posi", [128, NT], I32)
        posf, r_pf = b.sb(es2, "posf", [128, NT], F32)
        y, r_y = b.sb(es2, "ry", [128, NT], F32)
        kf, r_kf = b.sb(es2, "rkf", [128, NT], F32)
        ki, r_ki = b.sb(es2, "rki", [128, NT], I32)
        P.dma("sp", "l_pos", posi[:], d["pos"][0:1, :].partition_broadcast(128), writes=[r_pi])
        P.op("dve", "tensor_copy", reads=[r_pi], writes=[r_pf], out=posf[:], in_=posi[:])
        for (tab, r_tab, n_, c0) in ((TT, r_TT, 128, 0), (TK, r_TK, 64, 2)):
            P.op("dve", "tensor_scalar", reads=[r_pf, r_rc], writes=[r_y], out=y[0:n_, :], in0=posf[0:n_, :],
                 scalar1=rc[0:n_, c0:c0 + 1], scalar2=rc[0:n_, c0 + 1:c0 + 2], op0=ALU.mult, op1=ALU.add)
            P.op("dve", "tensor_scalar", reads=[r_y], writes=[r_kf], out=kf[0:n_, :], in0=y[0:n_, :],
                 scalar1=1.0 / TWO_PI, scalar2=None, op0=ALU.mult)
            P.op("dve", "tensor_copy", reads=[r_kf], writes=[r_ki], out=ki[0:n_, :], in_=kf[0:n_, :])
            P.op("dve", "tensor_copy", reads=[r_ki], writes=[r_kf], out=kf[0:n_, :], in_=ki[0:n_, :])
            P.op("dve", "scalar_tensor_tensor", reads=[r_kf, r_y], writes=[r_y], out=y[0:n_, :], in0=kf[0:n_, :],
                 scalar=-CW1, in1=y[0:n_, :], op0=ALU.mult, op1=ALU.add)
            P.op("dve", "scalar_tensor_tensor", reads=[r_kf, r_y], writes=[r_y], out=y[0:n_, :], in0=kf[0:n_, :],
                 scalar=-CW2, in1=y[0:n_, :], op0=ALU.mult, op1=ALU.add)
            P.op("dve", "tensor_scalar", reads=[r_y], writes=[r_y], out=y[0:n_, :], in0=y[0:n_, :],
                 scalar1=-math.pi, scalar2=math.pi, op0=ALU.max, op1=ALU.min)
            P.op("act", "activation", reads=[r_y], writes=[r_tab], out=tab[0:n_, :], in_=y[0:n_, :], func=AF.Sin)
        P.replay()
    if DBG.get("stop") == "rope":
        b.r_out = {"TT": r_TT, "TK": r_TK}
        b.dbg = (TT, TK)
        return
    win, r_win = b.sb(es, "win", [128, 8, WIN], BF16)
    for kc in range(8):
        P.dma("pool", "l_win", win[:, kc, :], d["w_in_p"][kc * 128:(kc + 1) * 128, :],
              writes=[r_win] if kc == 0 else (), pwrites=[r_win] if kc else (), max_dma_last_dim=4096)
    wuq, r_wuq = b.sb(es, "wuq", [128, 3, 1024], BF16)
    P.dma("pool", "l_wuq", wuq[:], d["w_uq_p"].rearrange("(kc p) n -> p kc n", p=128), writes=[r_wuq], max_dma_last_dim=4096)
    wkk, r_wkk = b.sb(es, "wkk", [128, 2, 512], BF16)
    P.dma("pool", "l_wkk", wkk[:], d["w_ukv_k"].rearrange("(kc p) n -> p kc n", p=128), writes=[r_wkk], max_dma_last_dim=4096)
    wkv, r_wkv = b.sb(es, "wkv", [128, 2, 512], BF16)
    P.dma("pool", "l_wkv", wkv[:], d["w_ukv_v"].rearrange("(kc p) n -> p kc n", p=128), writes=[r_wkv], max_dma_last_dim=4096)
    if "_after_w" in d:
        d["_after_w"]()
    if DBG.get("stop") == "w":
        b.r_out = {"a": r_win, "b": r_wuq, "c": r_wkk, "d": r_wkv}
        return
    xs = [b.sb(es, f"x{i}", [128, 8, NB], F32) for i in range(2)]
    sq, r_sq = b.sb(es, "sq", [128, 8, NB], BF16, nres=8)
    hs = [b.sb(es, f"h{i}", [128, 8, NB], BF16, nres=8) for i in range(2)]
    b.pool(es, "rs", 3, [128, NB], F32)
    b.pool(es, "tmp", 3, [128, NB], F32)
    b.pool(es, "sqc", 4, [128, NB], BF16)
    b.pool(es, "ob", 4, [128, NB], BF16)
    b.pool(es, "of", 3, [128, NB], F32)
    cqn, r_cqn = b.sb(es, "cqn", [128, 3, NB], BF16, nres=3)
    ckvn, r_ckvn = b.sb(es, "ckvn", [128, 2, NB], BF16, nres=2)
    vst = [b.sb(es, f"vst{i}", [128, 4, 8, 65], BF16) for i in range(2)]
    nvst = [b.sb(es, f"nvst{i}", [128, 4, 4, 65], BF16) for i in range(2)]
    for (tl, rr) in vst + nvst:
        P.op("pool", "memset", writes=[rr], ap=tl[:], constant=1.0)

    def rstd_from(ps_ap, ps_res, n_, scale, eps=EPS):
        rt, rr, _ = b.nxt("rs")
        P.op("act", "activation", reads=[ps_res], writes=[rr], out=rt[0:n_, :], in_=ps_ap, func=AF.Ln, scale=scale, bias=eps)
        P.op("act", "activation", reads=[rr], writes=[rr], out=rt[0:n_, :], in_=rt[0:n_, :], func=AF.Exp, scale=-0.5)
        return rt, rr

    r_out = R if R is not None else {k: Res("o_" + k) for k in o}
    b.r_out = r_out

    def store(dst_ap, src_ap, src_res, key, nm):
        P.dma("sp", "st_" + nm, dst_ap, src_ap, reads=[src_res], pwrites=[r_out[key]])

    r_x = R["x_in"] if R is not None else Res("xT")
    xT = d["xT"].rearrange("(c p) n -> p c n", p=128)

    def load_x(blk):
        xt, xr = xs[blk % 2]
        P.dma("sp", f"l_x{blk % 2}", xt[:], xT[:, :, blk * NB:(blk + 1) * NB], reads=[r_x], writes=[xr])

    def head_norm(ps, pr, n_, bmat, gain_ap, gain_res, out_ap, out_res, mult=None, mult_res=None):
        st, sr, _ = b.nxt("sqc")
        P.op("act", "activation", reads=[pr], writes=[sr], out=st[0:n_, :], in_=ps[0:n_, :], func=AF.Square)
        ps2, pr2 = b.psum()
        b.mm(ps2[0:n_, :], pr2, [(bmat, st[0:n_, :], [r_cm, sr])])
        rt, rr = rstd_from(ps2[0:n_, :], pr2, n_, 1.0)
        if mult is None:
            P.op("dve", "scalar_tensor_tensor", reads=[pr, rr, gain_res], writes=[out_res], out=out_ap, in0=ps[0:n_, :],
                 scalar=gain_ap, in1=rt[0:n_, :], op0=ALU.mult, op1=ALU.mult)
        else:
            tt, tr, _ = b.nxt("tmp")
            P.op("dve", "scalar_tensor_tensor", reads=[pr, rr, gain_res], writes=[tr], out=tt[0:n_, :], in0=ps[0:n_, :],
                 scalar=gain_ap, in1=rt[0:n_, :], op0=ALU.mult, op1=ALU.mult)
            P.op("pool", "tensor_tensor", reads=[tr, mult_res], writes=[out_res], out=out_ap, in0=tt[0:n_, :], in1=mult, op=ALU.mult)

    def norm1(blk):
        xt, xr = xs[blk % 2]
        h, r_h = hs[blk % 2]
        for c in range(8):
            P.op("pool", "tensor_tensor", reads=[xr], writes=[r_sq[c]], out=sq[:, c, :], in0=xt[:, c, :], in1=xt[:, c, :], op=ALU.mult)
        ps, pr = b.psum()
        b.mm(ps[:], pr, [(ONES, sq[:, c, :], [r_cm, r_sq[c]]) for c in range(8)])
        rt, rr = rstd_from(ps[:], pr, 128, 1.0 / D)
        for c in range(8):
            tt, tr, _ = b.nxt("tmp")
            P.op("dve", "tensor_tensor", reads=[xr, rr], writes=[tr], out=tt[:], in0=xt[:, c, :], in1=rt[:], op=ALU.mult)
            P.op("act", "activation", reads=[tr, r_a1, r_mc], writes=[r_h[c]], out=h[:, c, :], in_=tt[:], func=AF.Identity,
                 scale=a1[:, c:c + 1], bias=modc[:, c:c + 1])

    nblk1 = DBG.get("nblk", NBLK)
    load_x(0)
    if nblk1 > 1:
        load_x(1)
    norm1(0)
    for blk in range(nblk1):
        tok = slice(blk * NB, (blk + 1) * NB)
        h, r_h = hs[blk % 2]

        def proj(col0, ncol):
            ps, pr = b.psum()
            b.mm(ps[0:ncol, :], pr, [(win[:, k, col0:col0 + ncol], h[:, k, :], [r_win, r_h[k]]) for k in range(8)])
            return ps, pr

        for (col0, gain_ap, gain_res, key) in ((NAQ, gs[:, 0:1], r_gs, "naqT"), (NAK, par[:, P_NAK:P_NAK + 1], r_par, "nakT")):
            for m in range(2):
                ps, pr = proj(col0 + m * 128, 128)
                ot, orr, nm = b.nxt("ob")
                head_norm(ps, pr, 128, B64, gain_ap, gain_res, ot[:], orr)
                store(o[key][m * 128:(m + 1) * 128, tok], ot[:], orr, key, nm)
        nvt, nvr = nvst[blk % 2]
        for tt_ in range(4):
            ps, pr = b.psum()
            b.mm(ps[:, 0:256], pr, [(h[:, k, tt_ * 128:(tt_ + 1) * 128], win[:, k, NAV:NAV + 256], [r_win, r_h[k]]) for k in range(8)])
            P.op("dve", "tensor_copy", reads=[pr], pwrites=[nvr], guards=[nvr] if tt_ == 0 else (),
                 out=nvt[:, tt_, :, 0:64], in_=ps[:, 0:256].rearrange("p (h d) -> p h d", h=4))
        P.dma("sp", f"st_nv{blk % 2}", o["navE"][tok, :].rearrange("(t p) n -> p t n", p=128),
              nvt[:].rearrange("p t h d -> p t (h d)"), reads=[nvr], pwrites=[r_out["navE"]])
        pss = [proj(CQ + m * 128, 128) for m in range(3)]
        sts = []
        for m in range(3):
            st, sr, _ = b.nxt("sqc")
            P.op("act", "activation", reads=[pss[m][1]], writes=[sr], out=st[:], in_=pss[m][0][:], func=AF.Square)
            sts.append((st, sr))
        ps2, pr2 = b.psum()
        b.mm(ps2[:], pr2, [(ONES, sts[m][0][:], [r_cm, sts[m][1]]) for m in range(3)])
        rt, rr = rstd_from(ps2[:], pr2, 128, 1.0 / 384)
        for m in range(3):
            P.op("dve", "scalar_tensor_tensor", reads=[pss[m][1], rr, r_par], writes=[r_cqn[m]], out=cqn[:, m, :],
                 in0=pss[m][0][:], scalar=par[:, P_CQ + m:P_CQ + m + 1], in1=rt[:], op0=ALU.mult, op1=ALU.mult)
        for hh in range(8):
            ps, pr = b.psum()
            b.mm(ps[:], pr, [(wuq[:, m, hh * 128:(hh + 1) * 128], cqn[:, m, :], [r_wuq, r_cqn[m]]) for m in range(3)])
            ot, orr, nm = b.nxt("ob")
            head_norm(ps, pr, 128, BQ, gs[:, 1:2], r_gs, ot[:], orr, mult=TT[:, tok], mult_res=r_TT)
            store(o["qT"][hh, :, tok], ot[:], orr, "qT", nm)
        if blk + 1 < nblk1:
            norm1(blk + 1)
        pss = [proj(CKV + m * 128, 128) for m in range(2)]
        sts = []
        for m in range(2):
            st, sr, _ = b.nxt("sqc")
            P.op("act", "activation", reads=[pss[m][1]], writes=[sr], out=st[:], in_=pss[m][0][:], func=AF.Square)
            sts.append((st, sr))
        ps2, pr2 = b.psum()
        b.mm(ps2[:], pr2, [(ONES, sts[m][0][:], [r_cm, sts[m][1]]) for m in range(2)])
        rt, rr = rstd_from(ps2[:], pr2, 128, 1.0 / 256)
        for m in range(2):
            P.op("dve", "scalar_tensor_tensor", reads=[pss[m][1], rr, r_par], writes=[r_ckvn[m]], out=ckvn[:, m, :],
                 in0=pss[m][0][:], scalar=par[:, P_CKV + m:P_CKV + m + 1], in1=rt[:], op0=ALU.mult, op1=ALU.mult)
        for c in range(4):
            ps, pr = b.psum()
            b.mm(ps[:], pr, [(wkk[:, m, c * 128:(c + 1) * 128], ckvn[:, m, :], [r_wkk, r_ckvn[m]]) for m in range(2)])
            ot, orr, nm = b.nxt("ob")
            head_norm(ps, pr, 128, B64, par[:, P_KN:P_KN + 1], r_par, ot[:], orr)
            store(o["kT"][2 * c, 0:64, tok], ot[0:64, :], orr, "kT", nm)
            store(o["kT"][2 * c + 1, 0:64, tok], ot[64:128, :], orr, "kT", nm)
        vt, vr = vst[blk % 2]
        for tt_ in range(4):
            ps, pr = b.psum()
            b.mm(ps[:], pr, [(ckvn[:, m, tt_ * 128:(tt_ + 1) * 128], wkv[:, m, :], [r_wkv, r_ckvn[m]]) for m in range(2)])
            P.op("dve", "tensor_copy", reads=[pr], pwrites=[vr], guards=[vr] if tt_ == 0 else (),
                 out=vt[:, tt_, :, 0:64], in_=ps[:].rearrange("p (h d) -> p h d", h=8))
        P.dma("sp", f"st_v{blk % 2}", o["vE"][tok, :].rearrange("(t p) n -> p t n", p=128),
              vt[:].rearrange("p t h d -> p t (h d)"), reads=[vr], pwrites=[r_out["vE"]])
        ps, pr = proj(KR, 64)
        kt_, ktr, _ = b.nxt("ob")
        head_norm(ps, pr, 64, BK[0:64, 0:64], par[0:64, P_KR:P_KR + 1], r_par, kt_[0:64, :], ktr, mult=TK[:, tok], mult_res=r_TK)
        ps2, pr2 = b.psum()
        b.mm(ps2[0:64, :], pr2, [(DUP[0:64, 0:64], kt_[0:64, :], [r_cm, ktr])])
        ot, orr, nm = b.nxt("ob")
        P.op("act", "activation", reads=[pr2], writes=[orr], out=ot[0:64, :], in_=ps2[0:64, :], func=AF.Copy)
        for hh in range(8):
            store(o["kT"][hh, 64:128, tok], ot[0:64, :], orr, "kT", nm)
        for m in range(2):
            psx, prx = proj(XIN + m * 128, 128)
            tt, tr, _ = b.nxt("tmp")
            P.op("act", "activation", reads=[prx], writes=[tr], out=tt[:], in_=psx[:], func=AF.Copy)
            psc, prc = proj(GC + m * 128, 128)
            ot, orr, nm = b.nxt("of")
            P.op("dve", "tensor_tensor", reads=[prc, tr], writes=[orr], out=ot[:], in0=psc[:], in1=tt[:], op=ALU.mult)
            if "uTh" in o:
                store(o["uTh"][m * 128:(m + 1) * 128, 1 + blk * NB:1 + (blk + 1) * NB], ot[:], orr, "uTh", nm)
            else:
                store(o["uT"][m * 128:(m + 1) * 128, tok], ot[:], orr, "uT", nm)
            psb, prb = proj(GB + m * 128, 128)
            ot, orr, nm = b.nxt("of")
            P.op("act", "activation", reads=[prb], writes=[orr], out=ot[:], in_=psb[:], func=AF.Copy)
            store(o["gbT"][m * 128:(m + 1) * 128, tok], ot[:], orr, "gbT", nm)
        if blk + 2 < nblk1:
            load_x(blk + 2)


def const_mats():
    cm = np.zeros((128, 6, 128), np.float32)
    cm[:, 0, :] = 1.0
    for g in range(2):
        cm[g * 64:(g + 1) * 64, 1, g * 64:(g + 1) * 64] = 1.0 / 64
    cm[0:64, 2, 0:64] = 1.0 / 64
    cm[64:96, 2, 64:96] = 1.0 / 32
    cm[96:128, 2, 96:128] = 1.0 / 32
    cm[0:32, 3, 0:32] = 1.0 / 32
    cm[32:64, 3, 32:64] = 1.0 / 32
    for i in range(64):
        cm[i, 4, i % 32] = 1.0
        cm[i, 4, 32 + i % 32] = 1.0
    cm[0:64, 5, 0:64] = 1.0 / 64
    cm[64, 5, 0:64] = EPS
    return cm


def rope_consts():
    inv = (10000.0 ** (-np.arange(0, 32, 2, dtype=np.float32) / 32)).astype(np.float32)
    rc = np.zeros((128, 4), np.float32)
    rc[:, 1] = math.pi / 2
    for p in range(64, 128):
        rc[p, 0] = inv[(p - 64) % 16]
        rc[p, 1] = math.pi / 2 if p < 96 else (math.pi if p < 112 else 0.0)
    for p in range(64):
        rc[p, 2] = inv[p % 16]
        rc[p, 3] = math.pi / 2 if p < 32 else (math.pi if p < 48 else 0.0)
    return rc


def layer_host(inp, l):
    f = lambda k: np.asarray(inp[k][l], np.float32)
    w_in = f("w_in")
    kr = w_in[:, 1408:1440]
    w_in_p = np.concatenate([w_in[:, 0:1440], kr[:, 16:32], kr[:, 0:16], w_in[:, 1440:2208]], axis=1)
    wuq = f("mla_w_uq").reshape(384, 8, 96)
    w_uq_p = np.concatenate([wuq[:, :, 0:96], wuq[:, :, 80:96], wuq[:, :, 64:80]], axis=2).reshape(384, 1024)
    wukv = f("mla_w_ukv").reshape(256, 8, 128)
    w_ukv_k = wukv[:, :, 0:64].reshape(256, 512)
    w_ukv_v = wukv[:, :, 64:128].reshape(256, 512)
    par = np.zeros((128, NPAR), np.float32)
    par[:, P_G1:P_G1 + 8] = f("norm1_g").reshape(8, 128).T
    par[:, P_G2:P_G2 + 8] = f("norm2_g").reshape(8, 128).T
    par[:, P_NAQ] = np.tile(f("na_q_g"), 2)
    par[:, P_NAK] = np.tile(f("na_k_g"), 2)
    par[:, P_CQ:P_CQ + 3] = f("mla_q_a_g").reshape(3, 128).T
    par[:, P_CKV:P_CKV + 2] = f("mla_kv_a_g").reshape(2, 128).T
    qr = f("mla_qr_g")
    par[:, P_QH] = np.concatenate([f("mla_qn_g"), qr, qr[16:32], qr[0:16]])
    par[:, P_KN] = np.tile(f("mla_kn_g"), 2)
    krg = f("mla_kr_g")
    par[0:64, P_KR] = np.concatenate([krg, krg[16:32], krg[0:16]])
    cw = f("conv_w")
    for k in range(3):
        for c in range(2):
            par[:, P_CW + k * 2 + c] = cw[k, c * 128:(c + 1) * 128]
    par[:, P_CB:P_CB + 2] = f("conv_b").reshape(2, 128).T
    og = f("out_norm_g")
    par[0:64, P_ON:P_ON + 12] = og[0:768].reshape(12, 64).T
    par[:, P_ONC:P_ONC + 2] = og[768:1024].reshape(2, 128).T
    par[:, P_BADA:P_BADA + 48] = f("b_ada").reshape(48, 128).T
    rpb = f("na_rpb")
    wq = np.arange(64)[None, :]
    wk = np.arange(64)[:, None]
    cs = np.clip(wq - 8, 0, 48)
    col_ok = (wk >= cs) & (wk < cs + 16)
    dc = np.clip(wk - wq + 15, 0, 30)
    tb = np.full((2, 64, 4, 22, 64), -30000.0, np.float32)
    for jr in range(2):
        for s in range(22):
            dr = 10 + jr - s
            if -7 <= dr <= 7:
                for hh in range(4):
                    g = rpb[hh, dr + 7][dc]
                    tb[jr, :, hh, s, :] = np.where(col_ok, g, np.float32(-30000.0))
    return dict(w_in_p=np.ascontiguousarray(w_in_p), w_uq_p=np.ascontiguousarray(w_uq_p),
                w_ukv_k=np.ascontiguousarray(w_ukv_k), w_ukv_v=np.ascontiguousarray(w_ukv_v),
                params=par, w_ada=f("w_ada"), TB2=tb.reshape(128, 4 * 22 * 64),
                w_out=f("w_out"), w_gu=f("w_gu"), w_down=f("w_down"))


def row_mask(half):
    m = np.zeros((128, 3, 8, 8), np.float32)
    for ty, qb in enumerate((0, 1, 7)):
        for kt in range(8):
            for jr in range(2):
                for qr in range(8):
                    qrow = half * 64 + qb * 8 + qr
                    krow = half * 64 + qb * 8 - 4 + 2 * kt + jr
                    rs = min(max(qrow - 4, 0), 120)
                    ok = (0 <= krow < 128) and (rs <= krow < rs + 8)
                    m[jr * 64:(jr + 1) * 64, ty, kt, qr] = 1.0 if ok else 0.0
    return m.reshape(128, 192)


def core_static(inp, core):
    bi, half = core // 2, core % 2
    x = np.asarray(inp["x"], np.float32)
    sl = slice(half * NT, (half + 1) * NT)
    return dict(xT=np.ascontiguousarray(x[bi, sl, :].T),
                ccol=np.ascontiguousarray(np.asarray(inp["c"], np.float32)[bi].reshape(8, 128).T),
                pos=np.ascontiguousarray(np.asarray(inp["positions"], np.int32)[bi, sl].reshape(1, NT)),
                rowmask=row_mask(half))


NTH = NT + 2 * NHALO
B_IN = [("xT", [D, NT], F32), ("params", [128, NPAR], F32), ("cmats", [128, 6, 128], F32),
        ("ccol", [128, 8], F32), ("w_ada", [D, 6144], F32),
        ("qT", [8, 128, NT], BF16), ("kTf", [8, 128, SEQ], BF16), ("vEf", [SEQ, 520], BF16),
        ("naqT", [256, NT], BF16), ("nakTh", [256, NTH], BF16), ("navEh", [NTH, 260], BF16),
        ("uTh", [256, NT + 2], F32), ("gbT", [256, NT], F32), ("TB2", [128, 5632], F32),
        ("rowmask", [128, 192], F32), ("w_out", [D, D], F32), ("w_gu", [D, 2 * DFF], F32),
        ("w_down", [DFF, D], F32)]
B_OUT = [("xoT", [D, NT], F32)]


def prep_ffn_weights(b, d, scr, r):
    P = b.P
    for j in range(NJ):
        for half in range(2):
            P.dma("pool", "c_wgu", scr["wgu_s"][j, :, half * 1024:(half + 1) * 1024].rearrange("p (kc n) -> p kc n", kc=8),
                  d["w_gu"][:, half * DFF + j * 128: half * DFF + (j + 1) * 128].rearrange("(kc p) n -> p kc n", p=128),
                  pwrites=[r["wgus"]])
    for oc in range(8):
        P.dma("pool", "c_wd", scr["wd_s"][oc].rearrange("p (j n) -> p j n", j=NJ),
              d["w_down"][:, oc * 128:(oc + 1) * 128].rearrange("(j p) n -> p j n", p=128), pwrites=[r["wds"]])


def attn_finish(b, st8, psO, prO, g, tok, scr, r, t, nbank=None):
    P = b.P
    par, r_par, cm, r_cm = t["par"], t["r_par"], t["cm"], t["r_cm"]
    i = st8["f"]
    st8["f"] += 1
    ot, orr = st8["osb"][i % 2]
    P.op("dve", "tensor_copy", reads=[prO], writes=[orr], out=ot[:], in_=psO[0:65, :])
    sq, sr = st8["osq"][i % 2]
    P.op("pool", "tensor_tensor", reads=[orr], writes=[sr], out=sq[:], in0=ot[:], in1=ot[:], op=ALU.mult)

    def part_b():
        nb_ = (6 + i % 2) if nbank is None else nbank
        ps2, pr2 = b.ps[nb_], b.psr[nb_]
        b.mm(ps2[0:64, :], pr2, [(cm[0:65, 5, 0:64], sq[0:65, :], [r_cm, sr])])
        rt, rr = st8["ors"][i % 2]
        P.op("act", "activation", reads=[pr2], writes=[rr], out=rt[:], in_=ps2[0:64, :], func=AF.Ln, scale=2.0 ** -20)
        P.op("act", "activation", reads=[rr], writes=[rr], out=rt[:], in_=rt[:], func=AF.Exp, scale=-0.5, bias=-10.0 * math.log(2.0))
        mt, mr = st8["mixo"][i % 3]
        P.op("dve", "scalar_tensor_tensor", reads=[orr, rr, r_par], writes=[mr], out=mt[:], in0=ot[0:64, :],
             scalar=par[0:64, P_ON + g:P_ON + g + 1], in1=rt[:], op0=ALU.mult, op1=ALU.mult)
        P.dma("sp", f"st_mixo{i % 3}", scr["mixT"][g, :, tok], mt[:], reads=[mr], pwrites=[r["mix"]])

    return part_b


def attn_state(b, es):
    return dict(f=0, s=0, p=0, it=0,
                osb=[b.sb(es, f"osb{i}", [65, NB], F32) for i in range(2)],
                osq=[b.sb(es, f"osq{i}", [65, NB], BF16) for i in range(2)],
                ors=[b.sb(es, f"ors{i}", [64, NB], F32) for i in range(2)],
                mixo=[b.sb(es, f"mixo{i}", [64, NB], BF16) for i in range(3)])


def phase2_mla(b, d, scr, r, t, R):
    P = b.P
    with ExitStack() as es:
        st8 = attn_state(b, es)
        V, r_V = b.sb(es, "Vall", [128, 64, 520], BF16)
        first = True
        for cch in range(4):
            for rk in range(2):
                t0 = rk * 32 + cch * 8
                P.dma("sp", "l_V", V[:, t0:t0 + 8, :], d["vEg"][cch, rk].rearrange("(t p) n -> p t n", p=128), reads=[R["vEg"]],
                      writes=[r_V] if first else (), pwrites=() if first else [r_V])
                first = False
        Kh = [b.sb(es, f"Kh{i}", [128, SEQ], BF16) for i in range(2)]
        Qh = [b.sb(es, f"Qh{i}", [128, NT], BF16) for i in range(2)]
        pT = [b.sb(es, f"pT{i}", [128, 2 * NB], BF16) for i in range(4)]

        def load_h(hh):
            P.dma("sp", f"l_K{hh % 2}", Kh[hh % 2][0][:, 0:NT], d["kTg"][hh // 2, 0, hh % 2, :, :], reads=[R["kTg"]], writes=[Kh[hh % 2][1]])
            P.dma("sp", f"l_K{hh % 2}", Kh[hh % 2][0][:, NT:SEQ], d["kTg"][hh // 2, 1, hh % 2, :, :], reads=[R["kTg"]], pwrites=[Kh[hh % 2][1]])
            P.dma("sp", f"l_Q{hh % 2}", Qh[hh % 2][0][:], d["qT"][hh, :, :], reads=[R["qT"]], writes=[Qh[hh % 2][1]])

        nh = DBG.get("mla_heads", 8)
        deferred = []
        load_h(0)
        for hh in range(nh):
            if hh + 1 < nh:
                load_h(hh + 1)
            kt_, kr_ = Kh[hh % 2]
            qt_, qr_ = Qh[hh % 2]
            for qb in range(NBLK):
                tok = slice(qb * NB, (qb + 1) * NB)
                psO, prO = b.ps[4], b.psr[4]

                def S2(j):
                    pb = (0, 1, 3)[st8["s"] % 3]
                    st8["s"] += 1
                    for u in range(2):
                        kt = 2 * j + u
                        P.op("pe", "matmul", reads=[kr_, qr_], writes=[b.psr[2 * pb + u]], out=b.ps[2 * pb + u],
                             lhsT=kt_[:, kt * 128:(kt + 1) * 128], rhs=qt_[:, tok], start=True, stop=True)
                    return pb

                pend = {0: S2(0), 1: S2(1), 2: S2(2)}
                for j in range(32):
                    if j == 3 and deferred:
                        deferred.pop()()
                    pb = pend.pop(j)
                    pt, ptr = pT[st8["p"] % 4]
                    st8["p"] += 1
                    P.op("act", "activation", reads=[b.psr[2 * pb], b.psr[2 * pb + 1]], writes=[ptr], out=pt[:],
                         in_=b.psbig[pb][:], func=AF.Exp)
                    for u in range(2):
                        kt = 2 * j + u
                        P.op("pe", "matmul", reads=[r_V, ptr], writes=[prO] if kt == 63 else (), guards=[prO] if kt == 0 else (),
                             sig=(kt == 63), out=psO[0:65, :], lhsT=V[:, kt, hh * 65:(hh + 1) * 65], rhs=pt[:, u * NB:(u + 1) * NB],
                             start=(kt == 0), stop=(kt == 63))
                    if j + 3 < 32:
                        pend[j + 3] = S2(j + 3)
                deferred.append(attn_finish(b, st8, psO, prO, 4 + hh, tok, scr, r, t, nbank=5))
        while deferred:
            deferred.pop()()
        P.replay()


def phase2_na(b, d, scr, r, t, R):
    P = b.P
    with ExitStack() as es:
        st8 = attn_state(b, es)
        naq, r_naq = b.sb(es, "naqz", [128, 4, NT], BF16)
        P.op("pool", "memset", writes=[r_naq], ap=naq[:], constant=0.0)
        for hh in range(4):
            po = (hh % 2) * 64
            P.dma("sp", "l_naq", naq[po:po + 64, hh, :], d["naqT"][hh * 64:(hh + 1) * 64, :], reads=[R["naqT"]],
                  pwrites=[r_naq], guards=[r_naq])
        nak, r_nak = b.sb(es, "nak", [128, 2, NTH], BF16)
        P.dma("sp", "l_nak", nak[:, :, NHALO:NHALO + NT], d["nakT"].rearrange("(c p) n -> p c n", p=128), reads=[R["nakT"]], writes=[r_nak])
        P.dma("sp", "l_nak", nak[:, :, 0:NHALO], d["nakg"][0, :, NHALO:2 * NHALO].rearrange("(c p) n -> p c n", p=128),
              reads=[R["nakg"]], pwrites=[r_nak])
        P.dma("sp", "l_nak", nak[:, :, NHALO + NT:NTH], d["nakg"][1, :, 0:NHALO].rearrange("(c p) n -> p c n", p=128),
              reads=[R["nakg"]], pwrites=[r_nak])
        nav, r_nav = b.sb(es, "nav", [128, NTH // 128, 260], BF16)
        P.dma("sp", "l_nav", nav[:, 2:2 + NT // 128, :], d["navE"].rearrange("(t p) n -> p t n", p=128), reads=[R["navE"]], writes=[r_nav])
        P.dma("sp", "l_nav", nav[:, 0:2, :], d["navg"][0, NHALO:2 * NHALO, :].rearrange("(t p) n -> p t n", p=128),
              reads=[R["navg"]], pwrites=[r_nav])
        P.dma("sp", "l_nav", nav[:, 2 + NT // 128:NTH // 128, :], d["navg"][1, 0:NHALO, :].rearrange("(t p) n -> p t n", p=128),
              reads=[R["navg"]], pwrites=[r_nav])
        rm, r_rm = b.sb(es, "rm", [128, 3, 8, 8], F32)
        P.dma("sp", "l_rm", rm[:].rearrange("p a b c -> p (a b c)"), d["rowmask"][:, :], writes=[r_rm])
        tbf, r_tbf = b.sb(es, "tbf", [128, 5632], F32)
        P.dma("sp", "l_tb", tbf[:], d["TB2"][:, :], writes=[r_tbf])
        Et, r_Et = b.sb(es, "Etab", [128, 4, 22, 64], BF16)
        P.op("act", "activation", reads=[r_tbf], writes=[r_Et], out=Et[:].rearrange("p a b c -> p (a b c)"), in_=tbf[:], func=AF.Exp)
        e1 = [b.sb(es, f"e1_{i}", [128, 2 * NB], BF16) for i in range(3)]
        pp = [b.sb(es, f"pp_{i}", [128, 2 * NB], BF16) for i in range(3)]
        tE = [b.sb(es, f"tE_{i}", [128, 2, NB], BF16) for i in range(3)]
        EM, r_EM = b.sb(es, "EM", [128, 4, 8, NB], BF16)
        first = True
        for hh in range(4):
            for kt in range(8):
                P.op("pool", "tensor_tensor", reads=[r_Et, r_rm], writes=[r_EM] if first else (), pwrites=() if first else [r_EM],
                     out=EM[:, hh, kt, :].rearrange("p (s w) -> p s w", s=8), in0=Et[:, hh, 14 - 2 * kt:22 - 2 * kt, :],
                     in1=rm[:, 1, kt, :].unsqueeze(2).to_broadcast([128, 8, 64]), op=ALU.mult)
                first = False
        seq = [(qb, hh, j) for qb in range(NBLK) for hh in range(4) for j in range(4)]
        LOOK = 2
        bufs = {}

        def S2(i):
            qb, hh, j = seq[i]
            c, po = hh // 2, (hh % 2) * 64
            pb = (0, 1, 3)[st8["s"] % 3]
            st8["s"] += 1
            for u in range(2):
                kt = 2 * j + u
                tok0 = (qb * 8 + 2 * kt) * 64
                P.op("pe", "matmul", reads=[r_nak, r_naq], writes=[b.psr[2 * pb + u]], out=b.ps[2 * pb + u],
                     lhsT=nak[:, c, tok0:tok0 + 128], rhs=naq[:, hh, qb * NB:(qb + 1) * NB], start=True, stop=True)
            bufs[i] = pb

        for i in range(LOOK):
            S2(i)
        deferred = []
        psO, prO = b.ps[4], b.psr[4]
        for i, (qb, hh, j) in enumerate(seq):
            tok = slice(qb * NB, (qb + 1) * NB)
            ty = 0 if qb == 0 else (2 if qb == NBLK - 1 else 1)
            pb = bufs.pop(i)
            a1_, a1r = e1[i % 3]
            a3_, a3r = pp[i % 3]
            if ty == 1:
                emul, emr = EM[:, hh, 2 * j:2 * j + 2, :].rearrange("p k n -> p (k n)"), r_EM
            else:
                te, ter = tE[i % 3]
                for u in range(2):
                    kt = 2 * j + u
                    P.op("pool", "tensor_tensor", reads=[r_Et, r_rm], writes=[ter] if u == 0 else (), pwrites=[ter] if u else (),
                         out=te[:, u, :].rearrange("p (s w) -> p s w", s=8), in0=Et[:, hh, 14 - 2 * kt:22 - 2 * kt, :],
                         in1=rm[:, ty, kt, :].unsqueeze(2).to_broadcast([128, 8, 64]), op=ALU.mult)
                emul, emr = te[:].rearrange("p k n -> p (k n)"), ter
            P.op("act", "activation", reads=[b.psr[2 * pb], b.psr[2 * pb + 1]], writes=[a1r], out=a1_[:], in_=b.psbig[pb][:], func=AF.Exp)
            P.op("dve", "tensor_tensor", reads=[a1r, emr], writes=[a3r], out=a3_[:], in0=a1_[:], in1=emul, op=ALU.mult)
            for u in range(2):
                kt = 2 * j + u
                P.op("pe", "matmul", reads=[r_nav, a3r], writes=[prO] if kt == 7 else (), guards=[prO] if kt == 0 else (),
                     sig=(kt == 7), out=psO[0:65, :], lhsT=nav[:, qb * 4 + kt, hh * 65:(hh + 1) * 65], rhs=a3_[:, u * NB:(u + 1) * NB],
                     start=(kt == 0), stop=(kt == 7))
            if i + LOOK < len(seq):
                S2(i + LOOK)
            if j == 3:
                deferred.append(attn_finish(b, st8, psO, prO, hh, tok, scr, r, t, nbank=5))
            if j == 1 and deferred:
                deferred.pop()()
        while deferred:
            deferred.pop()()
        P.replay()


def phase3(b, d, o, scr, r, t, R):
    P = b.P
    par, r_par, cm, r_cm, modc, r_mc = t["par"], t["r_par"], t["cm"], t["r_cm"], t["modc"], t["r_mc"]
    ONES, B64 = cm[:, 0, :], cm[:, 1, :]
    with ExitStack() as es:
        a2, r_a2 = b.sb(es, "a2", [128, 8], F32)
        P.op("dve", "scalar_tensor_tensor", reads=[r_mc, r_par], writes=[r_a2], out=a2[:], in0=modc[:, 32:40], scalar=1.0,
             in1=par[:, P_G2:P_G2 + 8], op0=ALU.add, op1=ALU.mult)
        woa, r_woa = b.sb(es, "woa", [128, 6, D], BF16)
        P.dma("pool", "l_woa", woa[:], d["w_out"][0:768, :].rearrange("(c p) n -> p c n", p=128), writes=[r_woa], max_dma_last_dim=4096)
        woc, r_woc = b.sb(es, "woc", [128, 2, D], BF16)
        P.dma("pool", "l_woc", woc[:], d["w_out"][768:1024, :].rearrange("(c p) n -> p c n", p=128), writes=[r_woc], max_dma_last_dim=4096)
        xs = [b.sb(es, f"x3_{i}", [128, 8, NB], F32, nres=8) for i in range(2)]
        mixs = [b.sb(es, f"mix{i}", [128, 6, NB], BF16) for i in range(2)]
        us = [b.sb(es, f"u{i}", [128, 2, NB + 2], F32) for i in range(2)]
        gbs = [b.sb(es, f"gb{i}", [128, 2, NB], F32) for i in range(2)]
        b.pool(es, "cv", 3, [128, NB], F32)
        b.pool(es, "rs3", 2, [128, NB], F32)
        b.pool(es, "sg", 2, [128, NB], F32)
        b.pool(es, "sqv", 2, [128, NB], BF16)
        mixc, r_mixc = b.sb(es, "mixc", [128, 2, NB], BF16, nres=2)
        sq, r_sq = b.sb(es, "sq3", [128, 8, NB], BF16, nres=8)
        h2, r_h2 = b.sb(es, "h2", [128, 8, NB], BF16, nres=8)
        aT, r_aT = b.sb(es, "aT", [128, NJ, NB], BF16, nres=NJ)
        wgu = [b.sb(es, f"wgu{i}", [128, 2048], BF16) for i in range(3)]
        wd = [b.sb(es, f"wd{i}", [128, NJ, 128], BF16) for i in range(2)]
        xT = d["xT"].rearrange("(c p) n -> p c n", p=128)
        xoT = o["xoT"].rearrange("(c p) n -> p c n", p=128)
        r_xin = R["x_in"]
        r_d = {"u": R["uTh"], "gb": R["gbT"]}
        edge, r_edge = b.sb(es, "edge", [128, 2], F32)
        P.dma("sp", "l_edge", edge[:], d["edge"][:, :], writes=[r_edge])
        nw = {"gu": 0, "d": 0}

        def rstd_from(ps_ap, ps_res, scale):
            rt, rr, _ = b.nxt("rs3")
            P.op("act", "activation", reads=[ps_res], writes=[rr], out=rt[:], in_=ps_ap, func=AF.Ln, scale=scale, bias=EPS)
            P.op("act", "activation", reads=[rr], writes=[rr], out=rt[:], in_=rt[:], func=AF.Exp, scale=-0.5)
            return rt, rr

        def loads(blk):
            tok = slice(blk * NB, (blk + 1) * NB)
            i = blk % 2
            P.dma("sp", f"l_x3{i}", xs[i][0][:], xT[:, :, tok], reads=[r_xin], writes=xs[i][1])
            P.dma("sp", f"l_mix{i}", mixs[i][0][:], scr["mixT"][:, :, tok].rearrange("(c g) p n -> (g p) c n", g=2), reads=[r["mix"]], writes=[mixs[i][1]])
            P.dma("sp", f"l_u{i}", us[i][0][:], d["uTh"][:, blk * NB:blk * NB + NB + 2].rearrange("(c p) n -> p c n", p=128),
                  reads=[r_d["u"]], writes=[us[i][1]])
            P.dma("sp", f"l_gb{i}", gbs[i][0][:], d["gbT"][:, tok].rearrange("(c p) n -> p c n", p=128), reads=[r_d["gb"]], writes=[gbs[i][1]])
            if blk == 0:
                P.op("dve", "tensor_scalar", reads=[r_edge], pwrites=[us[i][1]], guards=[us[i][1]], out=us[i][0][:, :, 0:1],
                     in0=us[i][0][:, :, 0:1], scalar1=edge[:, 0:1], scalar2=None, op0=ALU.mult)
            if blk == NBLK - 1:
                P.op("dve", "tensor_scalar", reads=[r_edge], pwrites=[us[i][1]], guards=[us[i][1]], out=us[i][0][:, :, NB + 1:NB + 2],
                     in0=us[i][0][:, :, NB + 1:NB + 2], scalar1=edge[:, 1:2], scalar2=None, op0=ALU.mult)

        nb3 = DBG.get("nblk3", NBLK)
        loads(0)
        for blk in range(nb3):
            tok = slice(blk * NB, (blk + 1) * NB)
            if blk + 1 < nb3:
                loads(blk + 1)
            i = blk % 2
            xt, xr = xs[i]
            mt, mr = mixs[i]
            ut, ur = us[i]
            gt, gr = gbs[i]
            for c in range(2):
                cv, cr, _ = b.nxt("cv")
                P.op("dve", "tensor_scalar", reads=[ur, r_par], writes=[cr], out=cv[:], in0=ut[:, c, 0:NB],
                     scalar1=par[:, P_CW + c:P_CW + c + 1], scalar2=None, op0=ALU.mult)
                P.op("dve", "scalar_tensor_tensor", reads=[ur, r_par, cr], writes=[cr], out=cv[:], in0=ut[:, c, 1:NB + 1],
                     scalar=par[:, P_CW + 2 + c:P_CW + 3 + c], in1=cv[:], op0=ALU.mult, op1=ALU.add)
                P.op("dve", "scalar_tensor_tensor", reads=[ur, r_par, cr], writes=[cr], out=cv[:], in0=ut[:, c, 2:NB + 2],
                     scalar=par[:, P_CW + 4 + c:P_CW + 5 + c], in1=cv[:], op0=ALU.mult, op1=ALU.add)
                P.op("dve", "scalar_tensor_tensor", reads=[gr, r_par, cr], writes=[cr], out=cv[:], in0=cv[:],
                     scalar=par[:, P_CB + c:P_CB + c + 1], in1=gt[:, c, :], op0=ALU.add, op1=ALU.mult)
                sv, svr, _ = b.nxt("sqv")
                P.op("pool", "tensor_tensor", reads=[cr], writes=[svr], out=sv[:], in0=cv[:], in1=cv[:], op=ALU.mult)
                ps2, pr2 = b.psum()
                b.mm(ps2[:], pr2, [(B64, sv[:], [r_cm, svr])])
                rt, rr = rstd_from(ps2[:], pr2, 1.0)
                P.op("dve", "scalar_tensor_tensor", reads=[cr, rr, r_par], writes=[r_mixc[c]], out=mixc[:, c, :], in0=cv[:],
                     scalar=par[:, P_ONC + c:P_ONC + c + 1], in1=rt[:], op0=ALU.mult, op1=ALU.mult)
            for oc in range(8):
                ps, pr = b.psum()
                items = [(woa[:, g, oc * 128:(oc + 1) * 128], mt[:, g, :], [r_woa, mr]) for g in range(6)]
                items += [(woc[:, c, oc * 128:(oc + 1) * 128], mixc[:, c, :], [r_woc, r_mixc[c]]) for c in range(2)]
                b.mm(ps[:], pr, items)
                P.op("dve", "scalar_tensor_tensor", reads=[pr, r_mc], writes=[xr[oc]], out=xt[:, oc, :], in0=ps[:],
                     scalar=modc[:, 16 + oc:17 + oc], in1=xt[:, oc, :], op0=ALU.mult, op1=ALU.add)
            for c in range(8):
                P.op("pool", "tensor_tensor", reads=[xr[c]], writes=[r_sq[c]], out=sq[:, c, :], in0=xt[:, c, :], in1=xt[:, c, :], op=ALU.mult)
            ps, pr = b.psum()
            b.mm(ps[:], pr, [(ONES, sq[:, c, :], [r_cm, r_sq[c]]) for c in range(8)])
            rt, rr = rstd_from(ps[:], pr, 1.0 / D)
            for c in range(8):
                cv, cr, _ = b.nxt("cv")
                P.op("dve", "tensor_tensor", reads=[xr[c], rr], writes=[cr], out=cv[:], in0=xt[:, c, :], in1=rt[:], op=ALU.mult)
                P.op("act", "activation", reads=[cr, r_a2, r_mc], writes=[r_h2[c]], out=h2[:, c, :], in_=cv[:], func=AF.Identity,
                     scale=a2[:, c:c + 1], bias=modc[:, 24 + c:25 + c])
            for j in range(NJ):
                wt, wr = wgu[nw["gu"] % 3]
                P.dma("sp", f"l_wgu{nw['gu'] % 3}", wt[:], scr["wgu_s"][j, :, :], reads=[r["wgus"]], writes=[wr])
                nw["gu"] += 1
                psg, prg = b.psum()
                b.mm(psg[:], prg, [(wt[:, k * 128:(k + 1) * 128], h2[:, k, :], [wr, r_h2[k]]) for k in range(8)])
                psu, pru = b.psum()
                b.mm(psu[:], pru, [(wt[:, 1024 + k * 128:1024 + (k + 1) * 128], h2[:, k, :], [wr, r_h2[k]]) for k in range(8)])
                sg, sgr, _ = b.nxt("sg")
                P.op("act", "activation", reads=[prg], writes=[sgr], out=sg[:], in_=psg[:], func=AF.Silu)
                P.op("dve", "tensor_tensor", reads=[sgr, pru], writes=[r_aT[j]], out=aT[:, j, :], in0=psu[:], in1=sg[:], op=ALU.mult)
            for oc in range(8):
                wt, wr = wd[nw["d"] % 2]
                P.dma("sp", f"l_wd{nw['d'] % 2}", wt[:].rearrange("p j n -> p (j n)"), scr["wd_s"][oc, :, :], reads=[r["wds"]], writes=[wr])
                nw["d"] += 1
                ps, pr = b.psum()
                b.mm(ps[:], pr, [(wt[:, j, :], aT[:, j, :], [wr, r_aT[j]]) for j in range(NJ)])
                P.op("dve", "scalar_tensor_tensor", reads=[pr, r_mc], writes=[xr[oc]], out=xt[:, oc, :], in0=ps[:],
                     scalar=modc[:, 40 + oc:41 + oc], in1=xt[:, oc, :], op0=ALU.mult, op1=ALU.add)
            P.dma("sp", f"st_x3{i}", xoT[:, :, tok], xt[:], reads=xr, pwrites=[R["x_out"]])
        P.replay()


PAIRS = [[0, 1], [2, 3], [4, 5], [6, 7]]
F_IN = [("x0T", [D, NT], F32), ("ccol", [128, 8], F32), ("pos", [1, NT], I32), ("rowmask", [128, 192], F32),
        ("edge", [128, 2], F32), ("cmats", [128, 6, 128], F32), ("rc", [128, 4], F32),
        ("params", [2, 128, NPAR], F32), ("w_ada", [2, D, 6144], F32), ("w_in_p", [2, D, WIN], F32),
        ("w_uq_p", [2, 384, 1024], F32), ("w_ukv_k", [2, 256, 512], F32), ("w_ukv_v", [2, 256, 512], F32),
        ("TB2", [2, 128, 5632], F32), ("w_out", [2, D, D], F32), ("w_gu", [2, D, 2 * DFF], F32),
        ("w_down", [2, DFF, D], F32)]
PER_LAYER = ("params", "w_ada", "w_in_p", "w_uq_p", "w_ukv_k", "w_ukv_v", "TB2", "w_out", "w_gu", "w_down")


def coll(P, semname, in_ap, out_ap, reads, writes, pw=()):
    P.mksem(semname)
    waits = P._waits("pool", reads, writes, pw)
    P.cnt[semname] += 1
    ev = (semname, P.cnt[semname])
    wl = [(P.sem[a], v) for a, v in waits]
    sh = P.sem[semname]

    def run(e, wl=wl, sh=sh):
        for h, v in wl:
            e.wait_ge(h, v)
        e.collective_compute("AllGather", ALU.bypass, replica_groups=PAIRS, ins=[in_ap], outs=[out_ap]).then_inc(sh)

    P.q["pool"].append(run)
    for r_ in reads:
        r_.r.append(ev)
    for w in writes:
        w.w = {ev[0]: ev[1]}
        w.r = []
    for w in pw:
        w.w[ev[0]] = ev[1]


def build_F():
    nc = bass.Bass("TRN2", target_bir_lowering=False)
    d = {n: ext(nc, n, s, dt, "ExternalInput") for n, s, dt in F_IN}
    xoT = ext(nc, "xoT", [D, NT], F32, "ExternalOutput")
    T = {}
    for n, s, dt in [("kT", [1024, NT], BF16), ("kTg", [2048, NT], BF16), ("vE", [NT, 520], BF16), ("vEg", [SEQ, 520], BF16),
                     ("nakp", [256, 2 * NHALO], BF16), ("nakg", [512, 2 * NHALO], BF16),
                     ("navp", [2 * NHALO, 260], BF16), ("navg", [4 * NHALO, 260], BF16),
                     ("up", [256, 2], F32), ("ug", [512, 2], F32)]:
        T[n] = nc.dram_tensor(n, s, dt)
    I = {}
    for n, s, dt in [("qT", [8, 128, NT], BF16), ("naqT", [256, NT], BF16), ("nakT", [256, NT], BF16), ("navE", [NT, 260], BF16),
                     ("uTh", [256, NT + 2], F32), ("gbT", [256, NT], F32), ("x1T", [D, NT], F32),
                     ("mixT", [12, 64, NT], BF16), ("wgu_s", [2, NJ, 128, 2048], BF16), ("wd_s", [2, 8, 128, NJ * 128], BF16)]:
        I[n] = ext(nc, n, s, dt, "Internal")
    R = {k: Res("R_" + k) for k in ("kT", "kTg", "vE", "vEg", "nakp", "nakg", "navp", "navg", "up", "ug", "qT", "naqT",
                                    "nakT", "navE", "uTh", "gbT", "x0", "x1", "xo", "mix")}
    with ExitStack() as es0:
        b = B(nc, es0)
        P = b.P
        rw = [dict(wgus=Res("wgus0"), wds=Res("wds0"), mix=R["mix"]), dict(wgus=Res("wgus1"), wds=Res("wds1"), mix=R["mix"])]
        preps = prep_common(b, es0, d)
        for l in range(2):
            dl = {k: (d[k][l] if k in PER_LAYER else d[k]) for k in d}
            dl["_prep"] = preps[l]
            if l == 0:
                dl["_after_w"] = lambda: prep_ffn_weights(b, {"w_gu": d["w_gu"][0], "w_down": d["w_down"][0]},
                                                          {"wgu_s": I["wgu_s"][0], "wd_s": I["wd_s"][0]}, rw[0])
            x_in, x_out = (d["x0T"], I["x1T"]) if l == 0 else (I["x1T"], xoT)
            R["x_in"], R["x_out"] = (R["x0"], R["x1"]) if l == 0 else (R["x1"], R["xo"])
            dl["xT"] = x_in
            o1 = dict(qT=I["qT"], kT=T["kT"].ap().rearrange("(h p) n -> h p n", p=128), vE=T["vE"].ap(), naqT=I["naqT"],
                      nakT=I["nakT"], navE=I["navE"], uTh=I["uTh"], gbT=I["gbT"])
            with ExitStack() as es:
                phase1(b, es, dl, o1, R)
                P.replay()
            if DBG.get("f_stop") == "p1":
                break
            P.dma("sp", "x_nakp", T["nakp"].ap()[:, 0:NHALO], I["nakT"][:, 0:NHALO], reads=[R["nakT"]], pwrites=[R["nakp"]])
            P.dma("sp", "x_nakp", T["nakp"].ap()[:, NHALO:2 * NHALO], I["nakT"][:, NT - NHALO:NT], reads=[R["nakT"]], pwrites=[R["nakp"]])
            P.dma("sp", "x_navp", T["navp"].ap()[0:NHALO, :], I["navE"][0:NHALO, :], reads=[R["navE"]], pwrites=[R["navp"]])
            P.dma("sp", "x_navp", T["navp"].ap()[NHALO:2 * NHALO, :], I["navE"][NT - NHALO:NT, :], reads=[R["navE"]], pwrites=[R["navp"]])
            P.dma("sp", "x_up", T["up"].ap()[:, 0:1], I["uTh"][:, 1:2], reads=[R["uTh"]], pwrites=[R["up"]], allow_slow_non_contiguous=True)
            P.dma("sp", "x_up", T["up"].ap()[:, 1:2], I["uTh"][:, NT:NT + 1], reads=[R["uTh"]], pwrites=[R["up"]], allow_slow_non_contiguous=True)
            for a, g in (("nakp", "nakg"), ("navp", "navg"), ("up", "ug")):
                coll(P, "cc_" + a, T[a].ap().opt(), T[g].ap().opt(), [R[a]], [R[g]])
            for cch in range(4):
                coll(P, "cc_kT", T["kT"].ap()[cch * 256:(cch + 1) * 256, :].opt(), T["kTg"].ap()[cch * 512:(cch + 1) * 512, :].opt(),
                     [R["kT"]], [R["kTg"]] if cch == 0 else (), pw=() if cch == 0 else [R["kTg"]])
            for cch in range(4):
                coll(P, "cc_vE", T["vE"].ap()[cch * 1024:(cch + 1) * 1024, :].opt(), T["vEg"].ap()[cch * 2048:(cch + 1) * 2048, :].opt(),
                     [R["vE"]], [R["vEg"]] if cch == 0 else (), pw=() if cch == 0 else [R["vEg"]])
            ugv = T["ug"].ap().rearrange("(r c) n -> r c n", r=2)
            P.dma("sp", "x_uh", I["uTh"][:, 0:1], ugv[0, :, 1:2], reads=[R["ug"]], pwrites=[R["uTh"]], allow_slow_non_contiguous=True)
            P.dma("sp", "x_uh", I["uTh"][:, NT + 1:NT + 2], ugv[1, :, 0:1], reads=[R["ug"]], pwrites=[R["uTh"]], allow_slow_non_contiguous=True)
            if DBG.get("f_stop") == "xch":
                P.final_wait("sp", [R["uTh"], R["kTg"], R["vEg"], R["nakg"], R["navg"]])
                break
            dl.update(kTg=T["kTg"].ap().rearrange("(c r h p) n -> c r h p n", c=4, r=2, h=2),
                      vEg=T["vEg"].ap().rearrange("(c r t) n -> c r t n", c=4, r=2), qT=I["qT"],
                      naqT=I["naqT"], nakT=I["nakT"], navE=I["navE"],
                      nakg=T["nakg"].ap().rearrange("(r c) n -> r c n", r=2), navg=T["navg"].ap().rearrange("(r t) n -> r t n", r=2),
                      uTh=I["uTh"], gbT=I["gbT"])
            scr = dict(mixT=I["mixT"], wgu_s=I["wgu_s"][l], wd_s=I["wd_s"][l])
            with ExitStack() as es:
                t = dl["_prep"]
                if l == 0:
                    prep_ffn_weights(b, {"w_gu": d["w_gu"][1], "w_down": d["w_down"][1]},
                                     {"wgu_s": I["wgu_s"][1], "wd_s": I["wd_s"][1]}, rw[1])
                phase2_na(b, dl, scr, rw[l], t, R)
                phase2_mla(b, dl, scr, rw[l], t, R)
                phase3(b, dl, {"xoT": x_out}, scr, rw[l], t, R)
            if DBG.get("f_stop") == "l0":
                break
        P.final_wait("sp", [R["xo"]])
        P.replay()
    return nc


_PROGS = {}


def kernel(**inp):
    if "F" not in _PROGS:
        _PROGS["F"] = build_F()
    Ls = [layer_host(inp, l) for l in range(2)]
    shared = {k: np.ascontiguousarray(np.stack([Ls[0][k], Ls[1][k]])) for k in PER_LAYER}
    shared["cmats"] = const_mats()
    shared["rc"] = rope_consts()
    maps = []
    for c in range(8):
        cs = core_static(inp, c)
        half = c % 2
        edge = np.zeros((128, 2), np.float32)
        edge[:, 0] = 1.0 if half == 1 else 0.0
        edge[:, 1] = 1.0 if half == 0 else 0.0
        m = dict(shared)
        m.update(x0T=cs["xT"], ccol=cs["ccol"], pos=cs["pos"], rowmask=cs["rowmask"], edge=edge)
        maps.append(m)
    res = run_bass_kernel_spmd(_PROGS["F"], maps, core_ids=list(range(8))).results
    out = np.empty((4, SEQ, D), np.float32)
    for c in range(8):
        out[c // 2, (c % 2) * NT:(c % 2 + 1) * NT, :] = np.asarray(res[c]["xoT"], np.float32).T
    return out
```

```python
from contextlib import ExitStack
import math
import numpy as np
import ml_dtypes
import concourse.bass as bass
import concourse.mybir as mybir
from concourse.bass_utils import run_bass_kernel_spmd

F32 = mybir.dt.float32
BF16 = mybir.dt.bfloat16
I32 = mybir.dt.int32
ALU = mybir.AluOpType
AF = mybir.ActivationFunctionType

ENGS = ("pe", "act", "dve", "pool", "sp")


class Res:
    __slots__ = ("name", "w", "r")

    def __init__(self, name):
        self.name = name
        self.w = {}
        self.r = []


class Prog:
    def __init__(self, nc, es):
        self.nc = nc
        self.es = es
        self.q = {k: [] for k in ENGS}
        self.sem = {}
        self.cnt = {}
        self.seen = {k: {} for k in ENGS}
        self.cur = {}
        self.pe_pending = False
        self.epoch = 0
        self.new_epoch()

    def mksem(self, name):
        if name in self.sem:
            return name
        h = self.es.enter_context(self.nc.semaphore(name))
        self.sem[name] = h
        self.cnt[name] = 0
        return name

    def new_epoch(self):
        self.epoch += 1
        for k in ("pe", "act", "dve", "pool"):
            self.cur[k] = self.mksem(f"s_{k}_{self.epoch}")

    def _waits(self, eng, reads, writes, pwrites=(), guards=()):
        need = {}

        def add(s, v):
            if need.get(s, 0) < v:
                need[s] = v

        for r in reads:
            for s, v in r.w.items():
                add(s, v)
        for w in tuple(writes) + tuple(guards):
            for s, v in w.w.items():
                add(s, v)
            for s, v in w.r:
                add(s, v)
        for w in pwrites:
            for s, v in w.r:
                add(s, v)
        out = []
        seen = self.seen[eng]
        for s, v in need.items():
            if seen.get(s, 0) >= v:
                continue
            seen[s] = v
            out.append((s, v))
        return out

    def op(self, eng, meth, reads=(), writes=(), sig=True, pwrites=(), guards=(), **kw):
        waits = self._waits(eng, reads, writes, pwrites, guards)
        if eng == "pe":
            waits = [(s, v) for (s, v) in waits if s != self.cur["pe"]]
        s = self.cur[eng]
        if sig:
            self.cnt[s] += 1
            ev = (s, self.cnt[s])
            if eng == "pe":
                self.pe_pending = False
        else:
            assert eng == "pe"
            ev = (s, self.cnt[s] + 1)
            self.pe_pending = True
        wl = [(self.sem[a], b) for a, b in waits]
        sh = self.sem[s]

        def run(e, meth=meth, kw=kw, wl=wl, sig=sig, sh=sh):
            for h, v in wl:
                e.wait_ge(h, v)
            ins = getattr(e, meth)(**kw)
            if sig:
                ins.then_inc(sh, 1)

        self.q[eng].append(run)
        for r in reads:
            r.r.append(ev)
        for w in writes:
            w.w = {ev[0]: ev[1]}
            w.r = []
        for w in pwrites:
            w.w[ev[0]] = ev[1]
        return ev

    def dma(self, qeng, semname, out, in_, reads=(), writes=(), pwrites=(), guards=(), **kw):
        self.mksem(semname)
        waits = self._waits(qeng, reads, writes, pwrites, guards)
        self.cnt[semname] += 16
        ev = (semname, self.cnt[semname])
        wl = [(self.sem[a], b) for a, b in waits]
        sh = self.sem[semname]

        def run(e, wl=wl, sh=sh, out=out, in_=in_, kw=kw):
            for h, v in wl:
                e.wait_ge(h, v)
            e.dma_start(out=out, in_=in_, **kw).then_inc(sh, 16)

        self.q[qeng].append(run)
        for r in reads:
            r.r.append(ev)
        for w in writes:
            w.w = {ev[0]: ev[1]}
            w.r = []
        for w in pwrites:
            w.w[ev[0]] = ev[1]
        return ev

    def final_wait(self, eng, resources):
        waits = self._waits(eng, resources, ())
        wl = [(self.sem[a], b) for a, b in waits]

        def run(e, wl=wl):
            for h, v in wl:
                e.wait_ge(h, v)

        self.q[eng].append(run)

    def replay(self):
        assert not self.pe_pending, "unsignalled PE group at end"
        q = self.q
        with self.nc.Block() as block:
            @block.tensor
            def _(e):
                for f in q["pe"]:
                    f(e)

            @block.scalar
            def _(e):
                for f in q["act"]:
                    f(e)

            @block.vector
            def _(e):
                for f in q["dve"]:
                    f(e)

            @block.gpsimd
            def _(e):
                for f in q["pool"]:
                    f(e)

            @block.sync
            def _(e):
                for f in q["sp"]:
                    f(e)
        self.q = {k: [] for k in ENGS}


D = 1024
NT = 4096
NB = 512
NBLK = NT // NB
SEQ = 8192
DFF = 2816
NJ = DFF // 128
EPS = 1e-6
NAQ, NAK, NAV, CQ, CKV, KR, XIN, GB, GC, WIN = 0, 256, 512, 768, 1152, 1408, 1472, 1728, 1984, 2240
NPAR = 96
P_G1, P_G2, P_NAQ, P_NAK, P_CQ, P_CKV, P_QH, P_KN, P_KR, P_CW, P_CB, P_ON, P_ONC, P_BADA = \
    0, 8, 16, 17, 18, 21, 23, 24, 25, 26, 32, 34, 46, 48
SC_NA = 64 ** -0.5
SC_MLA = 96 ** -0.5
TWO_PI = 2.0 * math.pi
CW1 = 6.28125
CW2 = TWO_PI - CW1
NHALO = 256
DBG = {}


class B:
    def __init__(self, nc, es):
        self.nc = nc
        self.es = es
        self.P = Prog(nc, es)
        self.ps = []
        self.psr = []
        self.psbig = []
        for i in range(4):
            big = es.enter_context(nc.psum_tensor(f"psb{i}", [128, 1024], F32))
            self.psbig.append(big)
            for j in range(2):
                self.ps.append(big[:, j * 512:(j + 1) * 512])
                self.psr.append(Res(f"ps{2 * i + j}"))
        self.psi = 0
        self.uid = 0
        self.rr = {}

    def psum(self):
        i = self.psi
        self.psi = (self.psi + 1) % 8
        return self.ps[i], self.psr[i]

    def sb(self, es, name, shape, dt, nres=1):
        self.uid += 1
        t = es.enter_context(self.nc.sbuf_tensor(f"t{self.uid}_{name}", shape, dt))
        if nres == 1:
            return t, Res(name)
        return t, [Res(f"{name}{i}") for i in range(nres)]

    def pool(self, es, name, n, shape, dt):
        self.rr[name] = [0, [(self.sb(es, f"{name}{i}", shape, dt)) for i in range(n)]]

    def nxt(self, name):
        st = self.rr[name]
        i = st[0]
        st[0] = (i + 1) % len(st[1])
        t, r = st[1][i]
        return t, r, f"{name}{i}"

    def mm(self, out_ap, out_res, items):
        n = len(items)
        for i, (l, r, rd) in enumerate(items):
            first, last = i == 0, i == n - 1
            self.P.op("pe", "matmul", reads=rd,
                      writes=[out_res] if last else (),
                      guards=[out_res] if (first and not last) else (),
                      sig=last, out=out_ap, lhsT=l, rhs=r, start=first, stop=last)


def ext(nc, name, shape, dt, kind):
    return nc.dram_tensor(name, list(shape), dt, kind=kind).ap()


def prep_common(b, es, d, nl=2):
    P = b.P
    cm, r_cm = b.sb(es, "cmats", [128, 6, 128], BF16)
    P.dma("pool", "l_cm", cm[:], d["cmats"][:, :, :], writes=[r_cm])
    one, r_one = b.sb(es, "one", [128, 2], F32)
    P.op("dve", "memset", writes=[r_one], ap=one[:], constant=1.0)
    ccol, r_cc = b.sb(es, "ccol", [128, 8], F32)
    P.dma("sp", "l_cc", ccol[:], d["ccol"][:, :], writes=[r_cc])
    cact, r_ca = b.sb(es, "cact", [128, 8], F32)
    P.op("act", "activation", reads=[r_cc], writes=[r_ca], out=cact[:], in_=ccol[:], func=AF.Silu)
    out = []
    pars = []
    for l in range(nl):
        par, r_par = b.sb(es, f"par{l}", [128, NPAR], F32)
        P.dma("sp", f"l_par{l}", par[:], d["params"][l, :, :], writes=[r_par])
        modc, r_mc = b.sb(es, f"modc{l}", [128, 48], F32)
        pars.append((par, r_par, modc, r_mc))
    with ExitStack() as es2:
        modrow, r_mr = b.sb(es2, "modrow", [1, 6144], F32)
        wa = [b.sb(es2, f"wa{i}", [128, 8, 512], F32) for i in range(2)]
        n = 0
        for l in range(nl):
            par, r_par, modc, r_mc = pars[l]
            w_ada = d["w_ada"][l].rearrange("(kc p) n -> p kc n", p=128)
            for jc in range(12):
                wt, wr = wa[n % 2]
                P.dma("sp", f"l_wa{n % 2}", wt[:], w_ada[:, :, jc * 512:(jc + 1) * 512], writes=[wr])
                n += 1
                ps, pr = b.psum()
                b.mm(ps[0:1, :], pr, [(cact[:, kc:kc + 1], wt[:, kc, :], [r_ca, wr]) for kc in range(8)])
                P.op("dve", "tensor_copy", reads=[pr], pwrites=[r_mr], guards=[r_mr] if jc == 0 else (),
                     out=modrow[0:1, jc * 512:(jc + 1) * 512], in_=ps[0:1, :])
            ps, pr = b.psum()
            for j in range(48):
                P.op("pe", "matmul", reads=[r_mr, r_one], writes=[pr] if j == 47 else (), guards=[pr] if j == 0 else (),
                     sig=(j == 47), out=ps[:, j:j + 1], lhsT=modrow[0:1, j * 128:(j + 1) * 128], rhs=one[0:1, 0:1],
                     start=True, stop=True)
            P.op("dve", "tensor_tensor", reads=[pr, r_par], writes=[r_mc], out=modc[:], in0=ps[:, 0:48],
                 in1=par[:, P_BADA:P_BADA + 48], op=ALU.add)
            out.append(dict(par=par, r_par=r_par, cm=cm, r_cm=r_cm, modc=modc, r_mc=r_mc))
        P.replay()
    return out


A_IN = [("xT", [D, NT], F32), ("params", [128, NPAR], F32), ("cmats", [128, 6, 128], F32),
        ("ccol", [128, 8], F32), ("w_ada", [D, 6144], F32), ("rc", [128, 4], F32),
        ("pos", [1, NT], I32), ("w_in_p", [D, WIN], F32), ("w_uq_p", [384, 1024], F32),
        ("w_ukv_k", [256, 512], F32), ("w_ukv_v", [256, 512], F32)]
A_OUT = [("qT", [8, 128, NT], BF16), ("kT", [8, 128, NT], BF16), ("vE", [NT, 520], BF16),
         ("naqT", [256, NT], BF16), ("nakT", [256, NT], BF16), ("navE", [NT, 260], BF16),
         ("uT", [256, NT], F32), ("gbT", [256, NT], F32)]


def phase1(b, es, d, o, R=None):
    P = b.P
    t = d["_prep"]
    par, r_par, cm, r_cm, modc, r_mc = t["par"], t["r_par"], t["cm"], t["r_cm"], t["modc"], t["r_mc"]
    ONES, B64, BQ, BK, DUP = (cm[:, i, :] for i in range(5))
    a1, r_a1 = b.sb(es, "a1", [128, 8], F32)
    P.op("dve", "scalar_tensor_tensor", reads=[r_mc, r_par], writes=[r_a1], out=a1[:], in0=modc[:, 8:16], scalar=1.0,
         in1=par[:, P_G1:P_G1 + 8], op0=ALU.add, op1=ALU.mult)
    gs, r_gs = b.sb(es, "gs", [128, 2], F32)
    P.op("dve", "tensor_scalar", reads=[r_par], writes=[r_gs], out=gs[:, 0:1], in0=par[:, P_NAQ:P_NAQ + 1],
         scalar1=SC_NA, scalar2=None, op0=ALU.mult)
    P.op("dve", "tensor_scalar", reads=[r_par], pwrites=[r_gs], out=gs[:, 1:2], in0=par[:, P_QH:P_QH + 1],
         scalar1=SC_MLA, scalar2=None, op0=ALU.mult)
    rc, r_rc = b.sb(es, "rc", [128, 4], F32)
    P.dma("sp", "l_rc", rc[:], d["rc"][:, :], writes=[r_rc])
    TT, r_TT = b.sb(es, "TT", [128, NT], F32)
    TK, r_TK = b.sb(es, "TK", [64, NT], F32)
    with ExitStack() as es2:
        posi, r_pi = b.sb(es2, "posi", [128, NT], I32)
        posf, r_pf = b.sb(es2, "posf", [128, NT], F32)
        y, r_y = b.sb(es2, "ry", [128, NT], F32)
        kf, r_kf = b.sb(es2, "rkf", [128, NT], F32)
        ki, r_ki = b.sb(es2, "rki", [128, NT], I32)
        P.dma("sp", "l_pos", posi[:], d["pos"][0:1, :].partition_broadcast(128), writes=[r_pi])
        P.op("dve", "tensor_copy", reads=[r_pi], writes=[r_pf], out=posf[:], in_=posi[:])
        for (tab, r_tab, n_, c0) in ((TT, r_TT, 128, 0), (TK, r_TK, 64, 2)):
            P.op("dve", "tensor_scalar", reads=[r_pf, r_rc], writes=[r_y], out=y[0:n_, :], in0=posf[0:n_, :],
                 scalar1=rc[0:n_, c0:c0 + 1], scalar2=rc[0:n_, c0 + 1:c0 + 2], op0=ALU.mult, op1=ALU.add)
            P.op("dve", "tensor_scalar", reads=[r_y], writes=[r_kf], out=kf[0:n_, :], in0=y[0:n_, :],
                 scalar1=1.0 / TWO_PI, scalar2=None, op0=ALU.mult)
            P.op("dve", "tensor_copy", reads=[r_kf], writes=[r_ki], out=ki[0:n_, :], in_=kf[0:n_, :])
            P.op("dve", "tensor_copy", reads=[r_ki], writes=[r_kf], out=kf[0:n_, :], in_=ki[0:n_, :])
            P.op("dve", "scalar_tensor_tensor", reads=[r_kf, r_y], writes=[r_y], out=y[0:n_, :], in0=kf[0:n_, :],
                 scalar=-CW1, in1=y[0:n_, :], op0=ALU.mult, op1=ALU.add)
            P.op("dve", "scalar_tensor_tensor", reads=[r_kf, r_y], writes=[r_y], out=y[0:n_, :], in0=kf[0:n_, :],
                 scalar=-CW2, in1=y[0:n_, :], op0=ALU.mult, op1=ALU.add)
            P.op("dve", "tensor_scalar", reads=[r_y], writes=[r_y], out=y[0:n_, :], in0=y[0:n_, :],
                 scalar1=-math.pi, scalar2=math.pi, op0=ALU.max, op1=ALU.min)
            P.op("act", "activation", reads=[r_y], writes=[r_tab], out=tab[0:n_, :], in_=y[0:n_, :], func=AF.Sin)
        P.replay()
    if DBG.get("stop") == "rope":
        b.r_out = {"TT": r_TT, "TK": r_TK}
        b.dbg = (TT, TK)
        return
    win, r_win = b.sb(es, "win", [128, 8, WIN], BF16)
    for kc in range(8):
        P.dma("pool", "l_win", win[:, kc, :], d["w_in_p"][kc * 128:(kc + 1) * 128, :],
              writes=[r_win] if kc == 0 else (), pwrites=[r_win] if kc else (), max_dma_last_dim=4096)
    wuq, r_wuq = b.sb(es, "wuq", [128, 3, 1024], BF16)
    P.dma("pool", "l_wuq", wuq[:], d["w_uq_p"].rearrange("(kc p) n -> p kc n", p=128), writes=[r_wuq], max_dma_last_dim=4096)
    wkk, r_wkk = b.sb(es, "wkk", [128, 2, 512], BF16)
    P.dma("pool", "l_wkk", wkk[:], d["w_ukv_k"].rearrange("(kc p) n -> p kc n", p=128), writes=[r_wkk], max_dma_last_dim=4096)
    wkv, r_wkv = b.sb(es, "wkv", [128, 2, 512], BF16)
    P.dma("pool", "l_wkv", wkv[:], d["w_ukv_v"].rearrange("(kc p) n -> p kc n", p=128), writes=[r_wkv], max_dma_last_dim=4096)
    if "_after_w" in d:
        d["_after_w"]()
    if DBG.get("stop") == "w":
        b.r_out = {"a": r_win, "b": r_wuq, "c": r_wkk, "d": r_wkv}
        return
    xs = [b.sb(es, f"x{i}", [128, 8, NB], F32) for i in range(2)]
    sq, r_sq = b.sb(es, "sq", [128, 8, NB], BF16, nres=8)
    hs = [b.sb(es, f"h{i}", [128, 8, NB], BF16, nres=8) for i in range(2)]
    b.pool(es, "rs", 3, [128, NB], F32)
    b.pool(es, "tmp", 3, [128, NB], F32)
    b.pool(es, "sqc", 4, [128, NB], BF16)
    b.pool(es, "ob", 4, [128, NB], BF16)
    b.pool(es, "of", 3, [128, NB], F32)
    cqn, r_cqn = b.sb(es, "cqn", [128, 3, NB], BF16, nres=3)
    ckvn, r_ckvn = b.sb(es, "ckvn", [128, 2, NB], BF16, nres=2)
    vst = [b.sb(es, f"vst{i}", [128, 4, 8, 65], BF16) for i in range(2)]
    nvst = [b.sb(es, f"nvst{i}", [128, 4, 4, 65], BF16) for i in range(2)]
    for (tl, rr) in vst + nvst:
        P.op("pool", "memset", writes=[rr], ap=tl[:], constant=1.0)

    def rstd_from(ps_ap, ps_res, n_, scale, eps=EPS):
        rt, rr, _ = b.nxt("rs")
        P.op("act", "activation", reads=[ps_res], writes=[rr], out=rt[0:n_, :], in_=ps_ap, func=AF.Ln, scale=scale, bias=eps)
        P.op("act", "activation", reads=[rr], writes=[rr], out=rt[0:n_, :], in_=rt[0:n_, :], func=AF.Exp, scale=-0.5)
        return rt, rr

    r_out = R if R is not None else {k: Res("o_" + k) for k in o}
    b.r_out = r_out

    def store(dst_ap, src_ap, src_res, key, nm):
        P.dma("sp", "st_" + nm, dst_ap, src_ap, reads=[src_res], pwrites=[r_out[key]])

    r_x = R["x_in"] if R is not None else Res("xT")
    xT = d["xT"].rearrange("(c p) n -> p c n", p=128)

    def load_x(blk):
        xt, xr = xs[blk % 2]
        P.dma("sp", f"l_x{blk % 2}", xt[:], xT[:, :, blk * NB:(blk + 1) * NB], reads=[r_x], writes=[xr])

    pend = []

    def flush(keep=0):
        while len(pend) > keep:
            pend.pop(0)()

    def head_norm(ps, pr, n_, bmat, gain_ap, gain_res, out_ap, out_res, mult=None, mult_res=None, after=None):
        flush()
        st, sr, _ = b.nxt("sqc")
        P.op("act", "activation", reads=[pr], writes=[sr], out=st[0:n_, :], in_=ps[0:n_, :], func=AF.Square)

        def part_b():
            ps2, pr2 = b.psum()
            b.mm(ps2[0:n_, :], pr2, [(bmat, st[0:n_, :], [r_cm, sr])])
            rt, rr = rstd_from(ps2[0:n_, :], pr2, n_, 1.0)
            if mult is None:
                P.op("dve", "scalar_tensor_tensor", reads=[pr, rr, gain_res], writes=[out_res], out=out_ap, in0=ps[0:n_, :],
                     scalar=gain_ap, in1=rt[0:n_, :], op0=ALU.mult, op1=ALU.mult)
            else:
                tt, tr, _ = b.nxt("tmp")
                P.op("dve", "scalar_tensor_tensor", reads=[pr, rr, gain_res], writes=[tr], out=tt[0:n_, :], in0=ps[0:n_, :],
                     scalar=gain_ap, in1=rt[0:n_, :], op0=ALU.mult, op1=ALU.mult)
                P.op("pool", "tensor_tensor", reads=[tr, mult_res], writes=[out_res], out=out_ap, in0=tt[0:n_, :], in1=mult, op=ALU.mult)
            if after is not None:
                after()

        pend.append(part_b)

    def norm1(blk):
        xt, xr = xs[blk % 2]
        h, r_h = hs[blk % 2]
        for c in range(8):
            P.op("pool", "tensor_tensor", reads=[xr], writes=[r_sq[c]], out=sq[:, c, :], in0=xt[:, c, :], in1=xt[:, c, :], op=ALU.mult)
        ps, pr = b.psum()
        b.mm(ps[:], pr, [(ONES, sq[:, c, :], [r_cm, r_sq[c]]) for c in range(8)])
        rt, rr = rstd_from(ps[:], pr, 128, 1.0 / D)
        for c in range(8):
            tt, tr, _ = b.nxt("tmp")
            P.op("dve", "tensor_tensor", reads=[xr, rr], writes=[tr], out=tt[:], in0=xt[:, c, :], in1=rt[:], op=ALU.mult)
            P.op("act", "activation", reads=[tr, r_a1, r_mc], writes=[r_h[c]], out=h[:, c, :], in_=tt[:], func=AF.Identity,
                 scale=a1[:, c:c + 1], bias=modc[:, c:c + 1])

    nblk1 = DBG.get("nblk", NBLK)
    load_x(0)
    if nblk1 > 1:
        load_x(1)
    norm1(0)
    for blk in range(nblk1):
        tok = slice(blk * NB, (blk + 1) * NB)
        h, r_h = hs[blk % 2]

        def proj(col0, ncol):
            ps, pr = b.psum()
            b.mm(ps[0:ncol, :], pr, [(win[:, k, col0:col0 + ncol], h[:, k, :], [r_win, r_h[k]]) for k in range(8)])
            return ps, pr

        for (col0, gain_ap, gain_res, key) in ((NAQ, gs[:, 0:1], r_gs, "naqT"), (NAK, par[:, P_NAK:P_NAK + 1], r_par, "nakT")):
            for m in range(2):
                ps, pr = proj(col0 + m * 128, 128)
                ot, orr, nm = b.nxt("ob")
                head_norm(ps, pr, 128, B64, gain_ap, gain_res, ot[:], orr,
                          after=lambda key=key, m=m, ot=ot, orr=orr, nm=nm: store(o[key][m * 128:(m + 1) * 128, tok], ot[:], orr, key, nm))
        flush()
        nvt, nvr = nvst[blk % 2]
        for tt_ in range(4):
            ps, pr = b.psum()
            b.mm(ps[:, 0:256], pr, [(h[:, k, tt_ * 128:(tt_ + 1) * 128], win[:, k, NAV:NAV + 256], [r_win, r_h[k]]) for k in range(8)])
            P.op("dve", "tensor_copy", reads=[pr], pwrites=[nvr], guards=[nvr] if tt_ == 0 else (),
                 out=nvt[:, tt_, :, 0:64], in_=ps[:, 0:256].rearrange("p (h d) -> p h d", h=4))
        P.dma("sp", f"st_nv{blk % 2}", o["navE"][tok, :].rearrange("(t p) n -> p t n", p=128),
              nvt[:].rearrange("p t h d -> p t (h d)"), reads=[nvr], pwrites=[r_out["navE"]])
        pss = [proj(CQ + m * 128, 128) for m in range(3)]
        sts = []
        for m in range(3):
            st, sr, _ = b.nxt("sqc")
            P.op("act", "activation", reads=[pss[m][1]], writes=[sr], out=st[:], in_=pss[m][0][:], func=AF.Square)
            sts.append((st, sr))
        ps2, pr2 = b.psum()
        b.mm(ps2[:], pr2, [(ONES, sts[m][0][:], [r_cm, sts[m][1]]) for m in range(3)])
        rt, rr = rstd_from(ps2[:], pr2, 128, 1.0 / 384)
        for m in range(3):
            P.op("dve", "scalar_tensor_tensor", reads=[pss[m][1], rr, r_par], writes=[r_cqn[m]], out=cqn[:, m, :],
                 in0=pss[m][0][:], scalar=par[:, P_CQ + m:P_CQ + m + 1], in1=rt[:], op0=ALU.mult, op1=ALU.mult)
        for hh in range(8):
            ps, pr = b.psum()
            b.mm(ps[:], pr, [(wuq[:, m, hh * 128:(hh + 1) * 128], cqn[:, m, :], [r_wuq, r_cqn[m]]) for m in range(3)])
            ot, orr, nm = b.nxt("ob")
            head_norm(ps, pr, 128, BQ, gs[:, 1:2], r_gs, ot[:], orr, mult=TT[:, tok], mult_res=r_TT,
                      after=lambda hh=hh, ot=ot, orr=orr, nm=nm: store(o["qT"][hh, :, tok], ot[:], orr, "qT", nm))
        flush()
        if blk + 1 < nblk1:
            norm1(blk + 1)
        pss = [proj(CKV + m * 128, 128) for m in range(2)]
        sts = []
        for m in range(2):
            st, sr, _ = b.nxt("sqc")
            P.op("act", "activation", reads=[pss[m][1]], writes=[sr], out=st[:], in_=pss[m][0][:], func=AF.Square)
            sts.append((st, sr))
        ps2, pr2 = b.psum()
        b.mm(ps2[:], pr2, [(ONES, sts[m][0][:], [r_cm, sts[m][1]]) for m in range(2)])
        rt, rr = rstd_from(ps2[:], pr2, 128, 1.0 / 256)
        for m in range(2):
            P.op("dve", "scalar_tensor_tensor", reads=[pss[m][1], rr, r_par], writes=[r_ckvn[m]], out=ckvn[:, m, :],
                 in0=pss[m][0][:], scalar=par[:, P_CKV + m:P_CKV + m + 1], in1=rt[:], op0=ALU.mult, op1=ALU.mult)
        for c in range(4):
            ps, pr = b.psum()
            b.mm(ps[:], pr, [(wkk[:, m, c * 128:(c + 1) * 128], ckvn[:, m, :], [r_wkk, r_ckvn[m]]) for m in range(2)])
            ot, orr, nm = b.nxt("ob")
            def st_k(c=c, ot=ot, orr=orr, nm=nm):
                store(o["kT"][2 * c, 0:64, tok], ot[0:64, :], orr, "kT", nm)
                store(o["kT"][2 * c + 1, 0:64, tok], ot[64:128, :], orr, "kT", nm)

            head_norm(ps, pr, 128, B64, par[:, P_KN:P_KN + 1], r_par, ot[:], orr, after=st_k)
        flush()
        vt, vr = vst[blk % 2]
        for tt_ in range(4):
            ps, pr = b.psum()
            b.mm(ps[:], pr, [(ckvn[:, m, tt_ * 128:(tt_ + 1) * 128], wkv[:, m, :], [r_wkv, r_ckvn[m]]) for m in range(2)])
            P.op("dve", "tensor_copy", reads=[pr], pwrites=[vr], guards=[vr] if tt_ == 0 else (),
                 out=vt[:, tt_, :, 0:64], in_=ps[:].rearrange("p (h d) -> p h d", h=8))
        P.dma("sp", f"st_v{blk % 2}", o["vE"][tok, :].rearrange("(t p) n -> p t n", p=128),
              vt[:].rearrange("p t h d -> p t (h d)"), reads=[vr], pwrites=[r_out["vE"]])
        ps, pr = proj(KR, 64)
        kt_, ktr, _ = b.nxt("ob")
        head_norm(ps, pr, 64, BK[0:64, 0:64], par[0:64, P_KR:P_KR + 1], r_par, kt_[0:64, :], ktr, mult=TK[:, tok], mult_res=r_TK)
        flush()
        ps2, pr2 = b.psum()
        b.mm(ps2[0:64, :], pr2, [(DUP[0:64, 0:64], kt_[0:64, :], [r_cm, ktr])])
        ot, orr, nm = b.nxt("ob")
        P.op("act", "activation", reads=[pr2], writes=[orr], out=ot[0:64, :], in_=ps2[0:64, :], func=AF.Copy)
        for hh in range(8):
            store(o["kT"][hh, 64:128, tok], ot[0:64, :], orr, "kT", nm)
        for m in range(2):
            psx, prx = proj(XIN + m * 128, 128)
            tt, tr, _ = b.nxt("tmp")
            P.op("act", "activation", reads=[prx], writes=[tr], out=tt[:], in_=psx[:], func=AF.Copy)
            psc, prc = proj(GC + m * 128, 128)
            ot, orr, nm = b.nxt("of")
            P.op("dve", "tensor_tensor", reads=[prc, tr], writes=[orr], out=ot[:], in0=psc[:], in1=tt[:], op=ALU.mult)
            if "uTh" in o:
                store(o["uTh"][m * 128:(m + 1) * 128, 1 + blk * NB:1 + (blk + 1) * NB], ot[:], orr, "uTh", nm)
            else:
                store(o["uT"][m * 128:(m + 1) * 128, tok], ot[:], orr, "uT", nm)
            psb, prb = proj(GB + m * 128, 128)
            ot, orr, nm = b.nxt("of")
            P.op("act", "activation", reads=[prb], writes=[orr], out=ot[:], in_=psb[:], func=AF.Copy)
            store(o["gbT"][m * 128:(m + 1) * 128, tok], ot[:], orr, "gbT", nm)
        flush()
        if blk + 2 < nblk1:
            load_x(blk + 2)


def const_mats():
    cm = np.zeros((128, 6, 128), np.float32)
    cm[:, 0, :] = 1.0
    for g in range(2):
        cm[g * 64:(g + 1) * 64, 1, g * 64:(g + 1) * 64] = 1.0 / 64
    cm[0:64, 2, 0:64] = 1.0 / 64
    cm[64:96, 2, 64:96] = 1.0 / 32
    cm[96:128, 2, 96:128] = 1.0 / 32
    cm[0:32, 3, 0:32] = 1.0 / 32
    cm[32:64, 3, 32:64] = 1.0 / 32
    for i in range(64):
        cm[i, 4, i % 32] = 1.0
        cm[i, 4, 32 + i % 32] = 1.0
    cm[0:64, 5, 0:64] = 1.0 / 64
    cm[64, 5, 0:64] = EPS
    return cm


def rope_consts():
    inv = (10000.0 ** (-np.arange(0, 32, 2, dtype=np.float32) / 32)).astype(np.float32)
    rc = np.zeros((128, 4), np.float32)
    rc[:, 1] = math.pi / 2
    for p in range(64, 128):
        rc[p, 0] = inv[(p - 64) % 16]
        rc[p, 1] = math.pi / 2 if p < 96 else (math.pi if p < 112 else 0.0)
    for p in range(64):
        rc[p, 2] = inv[p % 16]
        rc[p, 3] = math.pi / 2 if p < 32 else (math.pi if p < 48 else 0.0)
    return rc


def layer_host(inp, l):
    f = lambda k: np.asarray(inp[k][l], np.float32)
    w_in = f("w_in")
    kr = w_in[:, 1408:1440]
    w_in_p = np.concatenate([w_in[:, 0:1440], kr[:, 16:32], kr[:, 0:16], w_in[:, 1440:2208]], axis=1)
    wuq = f("mla_w_uq").reshape(384, 8, 96)
    w_uq_p = np.concatenate([wuq[:, :, 0:96], wuq[:, :, 80:96], wuq[:, :, 64:80]], axis=2).reshape(384, 1024)
    wukv = f("mla_w_ukv").reshape(256, 8, 128)
    w_ukv_k = wukv[:, :, 0:64].reshape(256, 512)
    w_ukv_v = wukv[:, :, 64:128].reshape(256, 512)
    par = np.zeros((128, NPAR), np.float32)
    par[:, P_G1:P_G1 + 8] = f("norm1_g").reshape(8, 128).T
    par[:, P_G2:P_G2 + 8] = f("norm2_g").reshape(8, 128).T
    par[:, P_NAQ] = np.tile(f("na_q_g"), 2)
    par[:, P_NAK] = np.tile(f("na_k_g"), 2)
    par[:, P_CQ:P_CQ + 3] = f("mla_q_a_g").reshape(3, 128).T
    par[:, P_CKV:P_CKV + 2] = f("mla_kv_a_g").reshape(2, 128).T
    qr = f("mla_qr_g")
    par[:, P_QH] = np.concatenate([f("mla_qn_g"), qr, qr[16:32], qr[0:16]])
    par[:, P_KN] = np.tile(f("mla_kn_g"), 2)
    krg = f("mla_kr_g")
    par[0:64, P_KR] = np.concatenate([krg, krg[16:32], krg[0:16]])
    cw = f("conv_w")
    for k in range(3):
        for c in range(2):
            par[:, P_CW + k * 2 + c] = cw[k, c * 128:(c + 1) * 128]
    par[:, P_CB:P_CB + 2] = f("conv_b").reshape(2, 128).T
    og = f("out_norm_g")
    par[0:64, P_ON:P_ON + 12] = og[0:768].reshape(12, 64).T
    par[:, P_ONC:P_ONC + 2] = og[768:1024].reshape(2, 128).T
    par[:, P_BADA:P_BADA + 48] = f("b_ada").reshape(48, 128).T
    rpb = f("na_rpb")
    wq = np.arange(64)[None, :]
    wk = np.arange(64)[:, None]
    cs = np.clip(wq - 8, 0, 48)
    col_ok = (wk >= cs) & (wk < cs + 16)
    dc = np.clip(wk - wq + 15, 0, 30)
    tb = np.full((2, 64, 4, 22, 64), -30000.0, np.float32)
    for jr in range(2):
        for s in range(22):
            dr = 10 + jr - s
            if -7 <= dr <= 7:
                for hh in range(4):
                    g = rpb[hh, dr + 7][dc]
                    tb[jr, :, hh, s, :] = np.where(col_ok, g, np.float32(-30000.0))
    return dict(w_in_p=np.ascontiguousarray(w_in_p), w_uq_p=np.ascontiguousarray(w_uq_p),
                w_ukv_k=np.ascontiguousarray(w_ukv_k), w_ukv_v=np.ascontiguousarray(w_ukv_v),
                params=par, w_ada=f("w_ada"), TB2=tb.reshape(128, 4 * 22 * 64),
                w_out=f("w_out"), w_gu=f("w_gu"), w_down=f("w_down"))


def row_mask(half):
    m = np.zeros((128, 3, 8, 8), np.float32)
    for ty, qb in enumerate((0, 1, 7)):
        for kt in range(8):
            for jr in range(2):
                for qr in range(8):
                    qrow = half * 64 + qb * 8 + qr
                    krow = half * 64 + qb * 8 - 4 + 2 * kt + jr
                    rs = min(max(qrow - 4, 0), 120)
                    ok = (0 <= krow < 128) and (rs <= krow < rs + 8)
                    m[jr * 64:(jr + 1) * 64, ty, kt, qr] = 1.0 if ok else 0.0
    return m.reshape(128, 192)


def core_static(inp, core):
    bi, half = core // 2, core % 2
    x = np.asarray(inp["x"], np.float32)
    sl = slice(half * NT, (half + 1) * NT)
    return dict(xT=np.ascontiguousarray(x[bi, sl, :].T),
                ccol=np.ascontiguousarray(np.asarray(inp["c"], np.float32)[bi].reshape(8, 128).T),
                pos=np.ascontiguousarray(np.asarray(inp["positions"], np.int32)[bi, sl].reshape(1, NT)),
                rowmask=row_mask(half))


NTH = NT + 2 * NHALO
B_IN = [("xT", [D, NT], F32), ("params", [128, NPAR], F32), ("cmats", [128, 6, 128], F32),
        ("ccol", [128, 8], F32), ("w_ada", [D, 6144], F32),
        ("qT", [8, 128, NT], BF16), ("kTf", [8, 128, SEQ], BF16), ("vEf", [SEQ, 520], BF16),
        ("naqT", [256, NT], BF16), ("nakTh", [256, NTH], BF16), ("navEh", [NTH, 260], BF16),
        ("uTh", [256, NT + 2], F32), ("gbT", [256, NT], F32), ("TB2", [128, 5632], F32),
        ("rowmask", [128, 192], F32), ("w_out", [D, D], F32), ("w_gu", [D, 2 * DFF], F32),
        ("w_down", [DFF, D], F32)]
B_OUT = [("xoT", [D, NT], F32)]


def prep_ffn_weights(b, d, scr, r):
    P = b.P
    for j in range(NJ):
        for half in range(2):
            P.dma("pool", "c_wgu", scr["wgu_s"][j, :, half * 1024:(half + 1) * 1024].rearrange("p (kc n) -> p kc n", kc=8),
                  d["w_gu"][:, half * DFF + j * 128: half * DFF + (j + 1) * 128].rearrange("(kc p) n -> p kc n", p=128),
                  pwrites=[r["wgus"]])
    for oc in range(8):
        P.dma("pool", "c_wd", scr["wd_s"][oc].rearrange("p (j n) -> p j n", j=NJ),
              d["w_down"][:, oc * 128:(oc + 1) * 128].rearrange("(j p) n -> p j n", p=128), pwrites=[r["wds"]])


def attn_finish(b, st8, psO, prO, g, tok, scr, r, t, nbank=None):
    P = b.P
    par, r_par, cm, r_cm = t["par"], t["r_par"], t["cm"], t["r_cm"]
    i = st8["f"]
    st8["f"] += 1
    ot, orr = st8["osb"][i % 2]
    P.op("dve", "tensor_copy", reads=[prO], writes=[orr], out=ot[:], in_=psO[0:65, :])
    sq, sr = st8["osq"][i % 2]
    P.op("pool", "tensor_tensor", reads=[orr], writes=[sr], out=sq[:], in0=ot[:], in1=ot[:], op=ALU.mult)

    def part_b():
        nb_ = (6 + i % 2) if nbank is None else nbank
        ps2, pr2 = b.ps[nb_], b.psr[nb_]
        b.mm(ps2[0:64, :], pr2, [(cm[0:65, 5, 0:64], sq[0:65, :], [r_cm, sr])])
        rt, rr = st8["ors"][i % 2]
        P.op("act", "activation", reads=[pr2], writes=[rr], out=rt[:], in_=ps2[0:64, :], func=AF.Ln, scale=2.0 ** -20)
        P.op("act", "activation", reads=[rr], writes=[rr], out=rt[:], in_=rt[:], func=AF.Exp, scale=-0.5, bias=-10.0 * math.log(2.0))
        mt, mr = st8["mixo"][i % 3]
        P.op("dve", "scalar_tensor_tensor", reads=[orr, rr, r_par], writes=[mr], out=mt[:], in0=ot[0:64, :],
             scalar=par[0:64, P_ON + g:P_ON + g + 1], in1=rt[:], op0=ALU.mult, op1=ALU.mult)
        P.dma("sp", f"st_mixo{i % 3}", scr["mixT"][g, :, tok], mt[:], reads=[mr], pwrites=[r["mix"]])

    return part_b


def attn_state(b, es):
    return dict(f=0, s=0, p=0, it=0,
                osb=[b.sb(es, f"osb{i}", [65, NB], F32) for i in range(2)],
                osq=[b.sb(es, f"osq{i}", [65, NB], BF16) for i in range(2)],
                ors=[b.sb(es, f"ors{i}", [64, NB], F32) for i in range(2)],
                mixo=[b.sb(es, f"mixo{i}", [64, NB], BF16) for i in range(3)])


def phase2_mla(b, d, scr, r, t, R):
    P = b.P
    with ExitStack() as es:
        st8 = attn_state(b, es)
        V, r_V = b.sb(es, "Vall", [128, 64, 520], BF16)
        first = True
        for cch in range(4):
            for rk in range(2):
                t0 = rk * 32 + cch * 8
                P.dma("sp", "l_V", V[:, t0:t0 + 8, :], d["vEg"][cch, rk].rearrange("(t p) n -> p t n", p=128), reads=[R["vEg"]],
                      writes=[r_V] if first else (), pwrites=() if first else [r_V])
                first = False
        Kh = [b.sb(es, f"Kh{i}", [128, SEQ], BF16) for i in range(2)]
        Qh = [b.sb(es, f"Qh{i}", [128, NT], BF16) for i in range(2)]
        pT = [b.sb(es, f"pT{i}", [128, 2 * NB], BF16) for i in range(4)]

        def load_h(hh):
            P.dma("sp", f"l_K{hh % 2}", Kh[hh % 2][0][:, 0:NT], d["kTg"][hh // 2, 0, hh % 2, :, :], reads=[R["kTg"]], writes=[Kh[hh % 2][1]])
            P.dma("sp", f"l_K{hh % 2}", Kh[hh % 2][0][:, NT:SEQ], d["kTg"][hh // 2, 1, hh % 2, :, :], reads=[R["kTg"]], pwrites=[Kh[hh % 2][1]])
            P.dma("sp", f"l_Q{hh % 2}", Qh[hh % 2][0][:], d["qT"][hh, :, :], reads=[R["qT"]], writes=[Qh[hh % 2][1]])

        nh = DBG.get("mla_heads", 8)
        deferred = []
        load_h(0)
        for hh in range(nh):
            if hh + 1 < nh:
                load_h(hh + 1)
            kt_, kr_ = Kh[hh % 2]
            qt_, qr_ = Qh[hh % 2]
            for qb in range(NBLK):
                tok = slice(qb * NB, (qb + 1) * NB)
                psO, prO = b.ps[4], b.psr[4]

                def S2(j):
                    pb = (0, 1, 3)[st8["s"] % 3]
                    st8["s"] += 1
                    for u in range(2):
                        kt = 2 * j + u
                        P.op("pe", "matmul", reads=[kr_, qr_], writes=[b.psr[2 * pb + u]], out=b.ps[2 * pb + u],
                             lhsT=kt_[:, kt * 128:(kt + 1) * 128], rhs=qt_[:, tok], start=True, stop=True)
                    return pb

                pend = {0: S2(0), 1: S2(1), 2: S2(2)}
                for j in range(32):
                    if j == 3 and deferred:
                        deferred.pop()()
                    pb = pend.pop(j)
                    pt, ptr = pT[st8["p"] % 4]
                    st8["p"] += 1
                    P.op("act", "activation", reads=[b.psr[2 * pb], b.psr[2 * pb + 1]], writes=[ptr], out=pt[:],
                         in_=b.psbig[pb][:], func=AF.Exp)
                    for u in range(2):
                        kt = 2 * j + u
                        P.op("pe", "matmul", reads=[r_V, ptr], writes=[prO] if kt == 63 else (), guards=[prO] if kt == 0 else (),
                             sig=(kt == 63), out=psO[0:65, :], lhsT=V[:, kt, hh * 65:(hh + 1) * 65], rhs=pt[:, u * NB:(u + 1) * NB],
                             start=(kt == 0), stop=(kt == 63))
                    if j + 3 < 32:
                        pend[j + 3] = S2(j + 3)
                deferred.append(attn_finish(b, st8, psO, prO, 4 + hh, tok, scr, r, t, nbank=5))
        while deferred:
            deferred.pop()()
        P.replay()


def phase2_na(b, d, scr, r, t, R):
    P = b.P
    with ExitStack() as es:
        st8 = attn_state(b, es)
        naq, r_naq = b.sb(es, "naqz", [128, 4, NT], BF16)
        P.op("pool", "memset", writes=[r_naq], ap=naq[:], constant=0.0)
        for hh in range(4):
            po = (hh % 2) * 64
            P.dma("sp", "l_naq", naq[po:po + 64, hh, :], d["naqT"][hh * 64:(hh + 1) * 64, :], reads=[R["naqT"]],
                  pwrites=[r_naq], guards=[r_naq])
        nak, r_nak = b.sb(es, "nak", [128, 2, NTH], BF16)
        P.dma("sp", "l_nak", nak[:, :, NHALO:NHALO + NT], d["nakT"].rearrange("(c p) n -> p c n", p=128), reads=[R["nakT"]], writes=[r_nak])
        P.dma("sp", "l_nak", nak[:, :, 0:NHALO], d["nakg"][0, :, NHALO:2 * NHALO].rearrange("(c p) n -> p c n", p=128),
              reads=[R["nakg"]], pwrites=[r_nak])
        P.dma("sp", "l_nak", nak[:, :, NHALO + NT:NTH], d["nakg"][1, :, 0:NHALO].rearrange("(c p) n -> p c n", p=128),
              reads=[R["nakg"]], pwrites=[r_nak])
        nav, r_nav = b.sb(es, "nav", [128, NTH // 128, 260], BF16)
        P.dma("sp", "l_nav", nav[:, 2:2 + NT // 128, :], d["navE"].rearrange("(t p) n -> p t n", p=128), reads=[R["navE"]], writes=[r_nav])
        P.dma("sp", "l_nav", nav[:, 0:2, :], d["navg"][0, NHALO:2 * NHALO, :].rearrange("(t p) n -> p t n", p=128),
              reads=[R["navg"]], pwrites=[r_nav])
        P.dma("sp", "l_nav", nav[:, 2 + NT // 128:NTH // 128, :], d["navg"][1, 0:NHALO, :].rearrange("(t p) n -> p t n", p=128),
              reads=[R["navg"]], pwrites=[r_nav])
        rm, r_rm = b.sb(es, "rm", [128, 3, 8, 8], F32)
        P.dma("sp", "l_rm", rm[:].rearrange("p a b c -> p (a b c)"), d["rowmask"][:, :], writes=[r_rm])
        tbf, r_tbf = b.sb(es, "tbf", [128, 5632], F32)
        P.dma("sp", "l_tb", tbf[:], d["TB2"][:, :], writes=[r_tbf])
        Et, r_Et = b.sb(es, "Etab", [128, 4, 22, 64], BF16)
        P.op("act", "activation", reads=[r_tbf], writes=[r_Et], out=Et[:].rearrange("p a b c -> p (a b c)"), in_=tbf[:], func=AF.Exp)
        e1 = [b.sb(es, f"e1_{i}", [128, 2 * NB], BF16) for i in range(3)]
        pp = [b.sb(es, f"pp_{i}", [128, 2 * NB], BF16) for i in range(3)]
        tE = [b.sb(es, f"tE_{i}", [128, 2, NB], BF16) for i in range(3)]
        EM, r_EM = b.sb(es, "EM", [128, 4, 8, NB], BF16)
        first = True
        for hh in range(4):
            for kt in range(8):
                P.op("pool", "tensor_tensor", reads=[r_Et, r_rm], writes=[r_EM] if first else (), pwrites=() if first else [r_EM],
                     out=EM[:, hh, kt, :].rearrange("p (s w) -> p s w", s=8), in0=Et[:, hh, 14 - 2 * kt:22 - 2 * kt, :],
                     in1=rm[:, 1, kt, :].unsqueeze(2).to_broadcast([128, 8, 64]), op=ALU.mult)
                first = False
        seq = [(qb, hh, j) for qb in range(NBLK) for hh in range(4) for j in range(4)]
        LOOK = 2
        bufs = {}

        def S2(i):
            qb, hh, j = seq[i]
            c, po = hh // 2, (hh % 2) * 64
            pb = (0, 1, 3)[st8["s"] % 3]
            st8["s"] += 1
            for u in range(2):
                kt = 2 * j + u
                tok0 = (qb * 8 + 2 * kt) * 64
                P.op("pe", "matmul", reads=[r_nak, r_naq], writes=[b.psr[2 * pb + u]], out=b.ps[2 * pb + u],
                     lhsT=nak[:, c, tok0:tok0 + 128], rhs=naq[:, hh, qb * NB:(qb + 1) * NB], start=True, stop=True)
            bufs[i] = pb

        for i in range(LOOK):
            S2(i)
        deferred = []
        psO, prO = b.ps[4], b.psr[4]
        for i, (qb, hh, j) in enumerate(seq):
            tok = slice(qb * NB, (qb + 1) * NB)
            ty = 0 if qb == 0 else (2 if qb == NBLK - 1 else 1)
            pb = bufs.pop(i)
            a1_, a1r = e1[i % 3]
            a3_, a3r = pp[i % 3]
            if ty == 1:
                emul, emr = EM[:, hh, 2 * j:2 * j + 2, :].rearrange("p k n -> p (k n)"), r_EM
            else:
                te, ter = tE[i % 3]
                for u in range(2):
                    kt = 2 * j + u
                    P.op("pool", "tensor_tensor", reads=[r_Et, r_rm], writes=[ter] if u == 0 else (), pwrites=[ter] if u else (),
                         out=te[:, u, :].rearrange("p (s w) -> p s w", s=8), in0=Et[:, hh, 14 - 2 * kt:22 - 2 * kt, :],
                         in1=rm[:, ty, kt, :].unsqueeze(2).to_broadcast([128, 8, 64]), op=ALU.mult)
                emul, emr = te[:].rearrange("p k n -> p (k n)"), ter
            P.op("act", "activation", reads=[b.psr[2 * pb], b.psr[2 * pb + 1]], writes=[a1r], out=a1_[:], in_=b.psbig[pb][:], func=AF.Exp)
            P.op("dve", "tensor_tensor", reads=[a1r, emr], writes=[a3r], out=a3_[:], in0=a1_[:], in1=emul, op=ALU.mult)
            for u in range(2):
                kt = 2 * j + u
                P.op("pe", "matmul", reads=[r_nav, a3r], writes=[prO] if kt == 7 else (), guards=[prO] if kt == 0 else (),
                     sig=(kt == 7), out=psO[0:65, :], lhsT=nav[:, qb * 4 + kt, hh * 65:(hh + 1) * 65], rhs=a3_[:, u * NB:(u + 1) * NB],
                     start=(kt == 0), stop=(kt == 7))
            if i + LOOK < len(seq):
                S2(i + LOOK)
            if j == 3:
                deferred.append(attn_finish(b, st8, psO, prO, hh, tok, scr, r, t, nbank=5))
            if j == 1 and deferred:
                deferred.pop()()
        while deferred:
            deferred.pop()()
        P.replay()


def phase3(b, d, o, scr, r, t, R):
    P = b.P
    par, r_par, cm, r_cm, modc, r_mc = t["par"], t["r_par"], t["cm"], t["r_cm"], t["modc"], t["r_mc"]
    ONES, B64 = cm[:, 0, :], cm[:, 1, :]
    with ExitStack() as es:
        a2, r_a2 = b.sb(es, "a2", [128, 8], F32)
        P.op("dve", "scalar_tensor_tensor", reads=[r_mc, r_par], writes=[r_a2], out=a2[:], in0=modc[:, 32:40], scalar=1.0,
             in1=par[:, P_G2:P_G2 + 8], op0=ALU.add, op1=ALU.mult)
        woa, r_woa = b.sb(es, "woa", [128, 6, D], BF16)
        P.dma("pool", "l_woa", woa[:], d["w_out"][0:768, :].rearrange("(c p) n -> p c n", p=128), writes=[r_woa], max_dma_last_dim=4096)
        woc, r_woc = b.sb(es, "woc", [128, 2, D], BF16)
        P.dma("pool", "l_woc", woc[:], d["w_out"][768:1024, :].rearrange("(c p) n -> p c n", p=128), writes=[r_woc], max_dma_last_dim=4096)
        xs = [b.sb(es, f"x3_{i}", [128, 8, NB], F32, nres=8) for i in range(2)]
        mixs = [b.sb(es, f"mix{i}", [128, 6, NB], BF16) for i in range(2)]
        us = [b.sb(es, f"u{i}", [128, 2, NB + 2], F32) for i in range(2)]
        gbs = [b.sb(es, f"gb{i}", [128, 2, NB], F32) for i in range(2)]
        b.pool(es, "cv", 3, [128, NB], F32)
        b.pool(es, "rs3", 2, [128, NB], F32)
        b.pool(es, "sg", 2, [128, NB], F32)
        b.pool(es, "sqv", 2, [128, NB], BF16)
        mixc, r_mixc = b.sb(es, "mixc", [128, 2, NB], BF16, nres=2)
        sq, r_sq = b.sb(es, "sq3", [128, 8, NB], BF16, nres=8)
        h2, r_h2 = b.sb(es, "h2", [128, 8, NB], BF16, nres=8)
        aT, r_aT = b.sb(es, "aT", [128, NJ, NB], BF16, nres=NJ)
        wgu = [b.sb(es, f"wgu{i}", [128, 2048], BF16) for i in range(3)]
        wd = [b.sb(es, f"wd{i}", [128, NJ, 128], BF16) for i in range(2)]
        xT = d["xT"].rearrange("(c p) n -> p c n", p=128)
        xoT = o["xoT"].rearrange("(c p) n -> p c n", p=128)
        r_xin = R["x_in"]
        r_d = {"u": R["uTh"], "gb": R["gbT"]}
        edge, r_edge = b.sb(es, "edge", [128, 2], F32)
        P.dma("sp", "l_edge", edge[:], d["edge"][:, :], writes=[r_edge])
        nw = {"gu": 0, "d": 0}

        def rstd_from(ps_ap, ps_res, scale):
            rt, rr, _ = b.nxt("rs3")
            P.op("act", "activation", reads=[ps_res], writes=[rr], out=rt[:], in_=ps_ap, func=AF.Ln, scale=scale, bias=EPS)
            P.op("act", "activation", reads=[rr], writes=[rr], out=rt[:], in_=rt[:], func=AF.Exp, scale=-0.5)
            return rt, rr

        def loads(blk):
            tok = slice(blk * NB, (blk + 1) * NB)
            i = blk % 2
            P.dma("sp", f"l_x3{i}", xs[i][0][:], xT[:, :, tok], reads=[r_xin], writes=xs[i][1])
            P.dma("sp", f"l_mix{i}", mixs[i][0][:], scr["mixT"][:, :, tok].rearrange("(c g) p n -> (g p) c n", g=2), reads=[r["mix"]], writes=[mixs[i][1]])
            P.dma("sp", f"l_u{i}", us[i][0][:], d["uTh"][:, blk * NB:blk * NB + NB + 2].rearrange("(c p) n -> p c n", p=128),
                  reads=[r_d["u"]], writes=[us[i][1]])
            P.dma("sp", f"l_gb{i}", gbs[i][0][:], d["gbT"][:, tok].rearrange("(c p) n -> p c n", p=128), reads=[r_d["gb"]], writes=[gbs[i][1]])
            if blk == 0:
                P.op("dve", "tensor_scalar", reads=[r_edge], pwrites=[us[i][1]], guards=[us[i][1]], out=us[i][0][:, :, 0:1],
                     in0=us[i][0][:, :, 0:1], scalar1=edge[:, 0:1], scalar2=None, op0=ALU.mult)
            if blk == NBLK - 1:
                P.op("dve", "tensor_scalar", reads=[r_edge], pwrites=[us[i][1]], guards=[us[i][1]], out=us[i][0][:, :, NB + 1:NB + 2],
                     in0=us[i][0][:, :, NB + 1:NB + 2], scalar1=edge[:, 1:2], scalar2=None, op0=ALU.mult)

        nb3 = DBG.get("nblk3", NBLK)
        loads(0)
        for blk in range(nb3):
            tok = slice(blk * NB, (blk + 1) * NB)
            if blk + 1 < nb3:
                loads(blk + 1)
            i = blk % 2
            xt, xr = xs[i]
            mt, mr = mixs[i]
            ut, ur = us[i]
            gt, gr = gbs[i]
            for c in range(2):
                cv, cr, _ = b.nxt("cv")
                P.op("dve", "tensor_scalar", reads=[ur, r_par], writes=[cr], out=cv[:], in0=ut[:, c, 0:NB],
                     scalar1=par[:, P_CW + c:P_CW + c + 1], scalar2=None, op0=ALU.mult)
                P.op("dve", "scalar_tensor_tensor", reads=[ur, r_par, cr], writes=[cr], out=cv[:], in0=ut[:, c, 1:NB + 1],
                     scalar=par[:, P_CW + 2 + c:P_CW + 3 + c], in1=cv[:], op0=ALU.mult, op1=ALU.add)
                P.op("dve", "scalar_tensor_tensor", reads=[ur, r_par, cr], writes=[cr], out=cv[:], in0=ut[:, c, 2:NB + 2],
                     scalar=par[:, P_CW + 4 + c:P_CW + 5 + c], in1=cv[:], op0=ALU.mult, op1=ALU.add)
                P.op("dve", "scalar_tensor_tensor", reads=[gr, r_par, cr], writes=[cr], out=cv[:], in0=cv[:],
                     scalar=par[:, P_CB + c:P_CB + c + 1], in1=gt[:, c, :], op0=ALU.add, op1=ALU.mult)
                sv, svr, _ = b.nxt("sqv")
                P.op("pool", "tensor_tensor", reads=[cr], writes=[svr], out=sv[:], in0=cv[:], in1=cv[:], op=ALU.mult)
                ps2, pr2 = b.psum()
                b.mm(ps2[:], pr2, [(B64, sv[:], [r_cm, svr])])
                rt, rr = rstd_from(ps2[:], pr2, 1.0)
                P.op("dve", "scalar_tensor_tensor", reads=[cr, rr, r_par], writes=[r_mixc[c]], out=mixc[:, c, :], in0=cv[:],
                     scalar=par[:, P_ONC + c:P_ONC + c + 1], in1=rt[:], op0=ALU.mult, op1=ALU.mult)
            for oc in range(8):
                ps, pr = b.psum()
                items = [(woa[:, g, oc * 128:(oc + 1) * 128], mt[:, g, :], [r_woa, mr]) for g in range(6)]
                items += [(woc[:, c, oc * 128:(oc + 1) * 128], mixc[:, c, :], [r_woc, r_mixc[c]]) for c in range(2)]
                b.mm(ps[:], pr, items)
                P.op("dve", "scalar_tensor_tensor", reads=[pr, r_mc], writes=[xr[oc]], out=xt[:, oc, :], in0=ps[:],
                     scalar=modc[:, 16 + oc:17 + oc], in1=xt[:, oc, :], op0=ALU.mult, op1=ALU.add)
            for c in range(8):
                P.op("pool", "tensor_tensor", reads=[xr[c]], writes=[r_sq[c]], out=sq[:, c, :], in0=xt[:, c, :], in1=xt[:, c, :], op=ALU.mult)
            ps, pr = b.psum()
            b.mm(ps[:], pr, [(ONES, sq[:, c, :], [r_cm, r_sq[c]]) for c in range(8)])
            rt, rr = rstd_from(ps[:], pr, 1.0 / D)
            for c in range(8):
                cv, cr, _ = b.nxt("cv")
                P.op("dve", "tensor_tensor", reads=[xr[c], rr], writes=[cr], out=cv[:], in0=xt[:, c, :], in1=rt[:], op=ALU.mult)
                P.op("act", "activation", reads=[cr, r_a2, r_mc], writes=[r_h2[c]], out=h2[:, c, :], in_=cv[:], func=AF.Identity,
                     scale=a2[:, c:c + 1], bias=modc[:, 24 + c:25 + c])
            for j in range(NJ):
                wt, wr = wgu[nw["gu"] % 3]
                P.dma("sp", f"l_wgu{nw['gu'] % 3}", wt[:], scr["wgu_s"][j, :, :], reads=[r["wgus"]], writes=[wr])
                nw["gu"] += 1
                psg, prg = b.psum()
                b.mm(psg[:], prg, [(wt[:, k * 128:(k + 1) * 128], h2[:, k, :], [wr, r_h2[k]]) for k in range(8)])
                psu, pru = b.psum()
                b.mm(psu[:], pru, [(wt[:, 1024 + k * 128:1024 + (k + 1) * 128], h2[:, k, :], [wr, r_h2[k]]) for k in range(8)])
                sg, sgr, _ = b.nxt("sg")
                P.op("act", "activation", reads=[prg], writes=[sgr], out=sg[:], in_=psg[:], func=AF.Silu)
                P.op("dve", "tensor_tensor", reads=[sgr, pru], writes=[r_aT[j]], out=aT[:, j, :], in0=psu[:], in1=sg[:], op=ALU.mult)
            for oc in range(8):
                wt, wr = wd[nw["d"] % 2]
                P.dma("sp", f"l_wd{nw['d'] % 2}", wt[:].rearrange("p j n -> p (j n)"), scr["wd_s"][oc, :, :], reads=[r["wds"]], writes=[wr])
                nw["d"] += 1
                ps, pr = b.psum()
                b.mm(ps[:], pr, [(wt[:, j, :], aT[:, j, :], [wr, r_aT[j]]) for j in range(NJ)])
                P.op("dve", "scalar_tensor_tensor", reads=[pr, r_mc], writes=[xr[oc]], out=xt[:, oc, :], in0=ps[:],
                     scalar=modc[:, 40 + oc:41 + oc], in1=xt[:, oc, :], op0=ALU.mult, op1=ALU.add)
            P.dma("sp", f"st_x3{i}", xoT[:, :, tok], xt[:], reads=xr, pwrites=[R["x_out"]])
        P.replay()


PAIRS = [[0, 1], [2, 3], [4, 5], [6, 7]]
F_IN = [("x0T", [D, NT], F32), ("ccol", [128, 8], F32), ("pos", [1, NT], I32), ("rowmask", [128, 192], F32),
        ("edge", [128, 2], F32), ("cmats", [128, 6, 128], F32), ("rc", [128, 4], F32),
        ("params", [2, 128, NPAR], F32), ("w_ada", [2, D, 6144], F32), ("w_in_p", [2, D, WIN], F32),
        ("w_uq_p", [2, 384, 1024], F32), ("w_ukv_k", [2, 256, 512], F32), ("w_ukv_v", [2, 256, 512], F32),
        ("TB2", [2, 128, 5632], F32), ("w_out", [2, D, D], F32), ("w_gu", [2, D, 2 * DFF], F32),
        ("w_down", [2, DFF, D], F32)]
PER_LAYER = ("params", "w_ada", "w_in_p", "w_uq_p", "w_ukv_k", "w_ukv_v", "TB2", "w_out", "w_gu", "w_down")


def coll(P, semname, in_ap, out_ap, reads, writes, pw=()):
    P.mksem(semname)
    waits = P._waits("pool", reads, writes, pw)
    P.cnt[semname] += 1
    ev = (semname, P.cnt[semname])
    wl = [(P.sem[a], v) for a, v in waits]
    sh = P.sem[semname]

    def run(e, wl=wl, sh=sh):
        for h, v in wl:
            e.wait_ge(h, v)
        e.collective_compute("AllGather", ALU.bypass, replica_groups=PAIRS, ins=[in_ap], outs=[out_ap]).then_inc(sh)

    P.q["pool"].append(run)
    for r_ in reads:
        r_.r.append(ev)
    for w in writes:
        w.w = {ev[0]: ev[1]}
        w.r = []
    for w in pw:
        w.w[ev[0]] = ev[1]


def build_F():
    nc = bass.Bass("TRN2", target_bir_lowering=False)
    d = {n: ext(nc, n, s, dt, "ExternalInput") for n, s, dt in F_IN}
    xoT = ext(nc, "xoT", [D, NT], F32, "ExternalOutput")
    T = {}
    for n, s, dt in [("kT", [1024, NT], BF16), ("kTg", [2048, NT], BF16), ("vE", [NT, 520], BF16), ("vEg", [SEQ, 520], BF16),
                     ("nakp", [256, 2 * NHALO], BF16), ("nakg", [512, 2 * NHALO], BF16),
                     ("navp", [2 * NHALO, 260], BF16), ("navg", [4 * NHALO, 260], BF16),
                     ("up", [256, 2], F32), ("ug", [512, 2], F32)]:
        T[n] = nc.dram_tensor(n, s, dt)
    I = {}
    for n, s, dt in [("qT", [8, 128, NT], BF16), ("naqT", [256, NT], BF16), ("nakT", [256, NT], BF16), ("navE", [NT, 260], BF16),
                     ("uTh", [256, NT + 2], F32), ("gbT", [256, NT], F32), ("x1T", [D, NT], F32),
                     ("mixT", [12, 64, NT], BF16), ("wgu_s", [2, NJ, 128, 2048], BF16), ("wd_s", [2, 8, 128, NJ * 128], BF16)]:
        I[n] = ext(nc, n, s, dt, "Internal")
    R = {k: Res("R_" + k) for k in ("kT", "kTg", "vE", "vEg", "nakp", "nakg", "navp", "navg", "up", "ug", "qT", "naqT",
                                    "nakT", "navE", "uTh", "gbT", "x0", "x1", "xo", "mix")}
    with ExitStack() as es0:
        b = B(nc, es0)
        P = b.P
        rw = [dict(wgus=Res("wgus0"), wds=Res("wds0"), mix=R["mix"]), dict(wgus=Res("wgus1"), wds=Res("wds1"), mix=R["mix"])]
        preps = prep_common(b, es0, d)
        for l in range(2):
            dl = {k: (d[k][l] if k in PER_LAYER else d[k]) for k in d}
            dl["_prep"] = preps[l]
            if l == 0:
                dl["_after_w"] = lambda: prep_ffn_weights(b, {"w_gu": d["w_gu"][0], "w_down": d["w_down"][0]},
                                                          {"wgu_s": I["wgu_s"][0], "wd_s": I["wd_s"][0]}, rw[0])
            x_in, x_out = (d["x0T"], I["x1T"]) if l == 0 else (I["x1T"], xoT)
            R["x_in"], R["x_out"] = (R["x0"], R["x1"]) if l == 0 else (R["x1"], R["xo"])
            dl["xT"] = x_in
            o1 = dict(qT=I["qT"], kT=T["kT"].ap().rearrange("(h p) n -> h p n", p=128), vE=T["vE"].ap(), naqT=I["naqT"],
                      nakT=I["nakT"], navE=I["navE"], uTh=I["uTh"], gbT=I["gbT"])
            with ExitStack() as es:
                phase1(b, es, dl, o1, R)
                P.replay()
            if DBG.get("f_stop") == "p1":
                break
            P.dma("sp", "x_nakp", T["nakp"].ap()[:, 0:NHALO], I["nakT"][:, 0:NHALO], reads=[R["nakT"]], pwrites=[R["nakp"]])
            P.dma("sp", "x_nakp", T["nakp"].ap()[:, NHALO:2 * NHALO], I["nakT"][:, NT - NHALO:NT], reads=[R["nakT"]], pwrites=[R["nakp"]])
            P.dma("sp", "x_navp", T["navp"].ap()[0:NHALO, :], I["navE"][0:NHALO, :], reads=[R["navE"]], pwrites=[R["navp"]])
            P.dma("sp", "x_navp", T["navp"].ap()[NHALO:2 * NHALO, :], I["navE"][NT - NHALO:NT, :], reads=[R["navE"]], pwrites=[R["navp"]])
            P.dma("sp", "x_up", T["up"].ap()[:, 0:1], I["uTh"][:, 1:2], reads=[R["uTh"]], pwrites=[R["up"]], allow_slow_non_contiguous=True)
            P.dma("sp", "x_up", T["up"].ap()[:, 1:2], I["uTh"][:, NT:NT + 1], reads=[R["uTh"]], pwrites=[R["up"]], allow_slow_non_contiguous=True)
            for a, g in (("nakp", "nakg"), ("navp", "navg"), ("up", "ug")):
                coll(P, "cc_" + a, T[a].ap().opt(), T[g].ap().opt(), [R[a]], [R[g]])
            for cch in range(4):
                coll(P, "cc_kT", T["kT"].ap()[cch * 256:(cch + 1) * 256, :].opt(), T["kTg"].ap()[cch * 512:(cch + 1) * 512, :].opt(),
                     [R["kT"]], [R["kTg"]] if cch == 0 else (), pw=() if cch == 0 else [R["kTg"]])
            for cch in range(4):
                coll(P, "cc_vE", T["vE"].ap()[cch * 1024:(cch + 1) * 1024, :].opt(), T["vEg"].ap()[cch * 2048:(cch + 1) * 2048, :].opt(),
                     [R["vE"]], [R["vEg"]] if cch == 0 else (), pw=() if cch == 0 else [R["vEg"]])
            ugv = T["ug"].ap().rearrange("(r c) n -> r c n", r=2)
            P.dma("sp", "x_uh", I["uTh"][:, 0:1], ugv[0, :, 1:2], reads=[R["ug"]], pwrites=[R["uTh"]], allow_slow_non_contiguous=True)
            P.dma("sp", "x_uh", I["uTh"][:, NT + 1:NT + 2], ugv[1, :, 0:1], reads=[R["ug"]], pwrites=[R["uTh"]], allow_slow_non_contiguous=True)
            if DBG.get("f_stop") == "xch":
                P.final_wait("sp", [R["uTh"], R["kTg"], R["vEg"], R["nakg"], R["navg"]])
                break
            dl.update(kTg=T["kTg"].ap().rearrange("(c r h p) n -> c r h p n", c=4, r=2, h=2),
                      vEg=T["vEg"].ap().rearrange("(c r t) n -> c r t n", c=4, r=2), qT=I["qT"],
                      naqT=I["naqT"], nakT=I["nakT"], navE=I["navE"],
                      nakg=T["nakg"].ap().rearrange("(r c) n -> r c n", r=2), navg=T["navg"].ap().rearrange("(r t) n -> r t n", r=2),
                      uTh=I["uTh"], gbT=I["gbT"])
            scr = dict(mixT=I["mixT"], wgu_s=I["wgu_s"][l], wd_s=I["wd_s"][l])
            with ExitStack() as es:
                t = dl["_prep"]
                if l == 0:
                    prep_ffn_weights(b, {"w_gu": d["w_gu"][1], "w_down": d["w_down"][1]},
                                     {"wgu_s": I["wgu_s"][1], "wd_s": I["wd_s"][1]}, rw[1])
                phase2_na(b, dl, scr, rw[l], t, R)
                phase2_mla(b, dl, scr, rw[l], t, R)
                phase3(b, dl, {"xoT": x_out}, scr, rw[l], t, R)
            if DBG.get("f_stop") == "l0":
                break
        P.final_wait("sp", [R["xo"]])
        P.replay()
    return nc


_PROGS = {}


def kernel(**inp):
    if "F" not in _PROGS:
        _PROGS["F"] = build_F()
    Ls = [layer_host(inp, l) for l in range(2)]
    shared = {k: np.ascontiguousarray(np.stack([Ls[0][k], Ls[1][k]])) for k in PER_LAYER}
    shared["cmats"] = const_mats()
    shared["rc"] = rope_consts()
    maps = []
    for c in range(8):
        cs = core_static(inp, c)
        half = c % 2
        edge = np.zeros((128, 2), np.float32)
        edge[:, 0] = 1.0 if half == 1 else 0.0
        edge[:, 1] = 1.0 if half == 0 else 0.0
        m = dict(shared)
        m.update(x0T=cs["xT"], ccol=cs["ccol"], pos=cs["pos"], rowmask=cs["rowmask"], edge=edge)
        maps.append(m)
    res = run_bass_kernel_spmd(_PROGS["F"], maps, core_ids=list(range(8))).results
    out = np.empty((4, SEQ, D), np.float32)
    for c in range(8):
        out[c // 2, (c % 2) * NT:(c % 2 + 1) * NT, :] = np.asarray(res[c]["xoT"], np.float32).T
    return out
```

```python
from contextlib import ExitStack
import math
import numpy as np
import ml_dtypes
import concourse.bass as bass
import concourse.mybir as mybir
from concourse.bass_utils import run_bass_kernel_spmd

F32 = mybir.dt.float32
BF16 = mybir.dt.bfloat16
I32 = mybir.dt.int32
ALU = mybir.AluOpType
AF = mybir.ActivationFunctionType

ENGS = ("pe", "act", "dve", "pool", "sp")


class Res:
    __slots__ = ("name", "w", "r")

    def __init__(self, name):
        self.name = name
        self.w = {}
        self.r = []


class Prog:
    def __init__(self, nc, es):
        self.nc = nc
        self.es = es
        self.q = {k: [] for k in ENGS}
        self.sem = {}
        self.cnt = {}
        self.seen = {k: {} for k in ENGS}
        self.cur = {}
        self.pe_pending = False
        self.epoch = 0
        self.new_epoch()

    def mksem(self, name):
        if name in self.sem:
            return name
        h = self.es.enter_context(self.nc.semaphore(name))
        self.sem[name] = h
        self.cnt[name] = 0
        return name

    def new_epoch(self):
        self.epoch += 1
        for k in ("pe", "act", "dve", "pool"):
            self.cur[k] = self.mksem(f"s_{k}_{self.epoch}")

    def _waits(self, eng, reads, writes, pwrites=(), guards=()):
        need = {}

        def add(s, v):
            if need.get(s, 0) < v:
                need[s] = v

        for r in reads:
            for s, v in r.w.items():
                add(s, v)
        for w in tuple(writes) + tuple(guards):
            for s, v in w.w.items():
                add(s, v)
            for s, v in w.r:
                add(s, v)
        for w in pwrites:
            for s, v in w.r:
                add(s, v)
        out = []
        seen = self.seen[eng]
        for s, v in need.items():
            if seen.get(s, 0) >= v:
                continue
            seen[s] = v
            out.append((s, v))
        return out

    def op(self, eng, meth, reads=(), writes=(), sig=True, pwrites=(), guards=(), **kw):
        waits = self._waits(eng, reads, writes, pwrites, guards)
        if eng == "pe":
            waits = [(s, v) for (s, v) in waits if s != self.cur["pe"]]
        s = self.cur[eng]
        if sig:
            self.cnt[s] += 1
            ev = (s, self.cnt[s])
            if eng == "pe":
                self.pe_pending = False
        else:
            assert eng == "pe"
            ev = (s, self.cnt[s] + 1)
            self.pe_pending = True
        wl = [(self.sem[a], b) for a, b in waits]
        sh = self.sem[s]

        def run(e, meth=meth, kw=kw, wl=wl, sig=sig, sh=sh):
            for h, v in wl:
                e.wait_ge(h, v)
            ins = getattr(e, meth)(**kw)
            if sig:
                ins.then_inc(sh, 1)

        self.q[eng].append(run)
        for r in reads:
            r.r.append(ev)
        for w in writes:
            w.w = {ev[0]: ev[1]}
            w.r = []
        for w in pwrites:
            w.w[ev[0]] = ev[1]
        return ev

    def dma(self, qeng, semname, out, in_, reads=(), writes=(), pwrites=(), guards=(), **kw):
        self.mksem(semname)
        waits = self._waits(qeng, reads, writes, pwrites, guards)
        self.cnt[semname] += 16
        ev = (semname, self.cnt[semname])
        wl = [(self.sem[a], b) for a, b in waits]
        sh = self.sem[semname]

        def run(e, wl=wl, sh=sh, out=out, in_=in_, kw=kw):
            for h, v in wl:
                e.wait_ge(h, v)
            e.dma_start(out=out, in_=in_, **kw).then_inc(sh, 16)

        self.q[qeng].append(run)
        for r in reads:
            r.r.append(ev)
        for w in writes:
            w.w = {ev[0]: ev[1]}
            w.r = []
        for w in pwrites:
            w.w[ev[0]] = ev[1]
        return ev

    def final_wait(self, eng, resources):
        waits = self._waits(eng, resources, ())
        wl = [(self.sem[a], b) for a, b in waits]

        def run(e, wl=wl):
            for h, v in wl:
                e.wait_ge(h, v)

        self.q[eng].append(run)

    def replay(self):
        assert not self.pe_pending, "unsignalled PE group at end"
        q = self.q
        with self.nc.Block() as block:
            @block.tensor
            def _(e):
                for f in q["pe"]:
                    f(e)

            @block.scalar
            def _(e):
                for f in q["act"]:
                    f(e)

            @block.vector
            def _(e):
                for f in q["dve"]:
                    f(e)

            @block.gpsimd
            def _(e):
                for f in q["pool"]:
                    f(e)

            @block.sync
            def _(e):
                for f in q["sp"]:
                    f(e)
        self.q = {k: [] for k in ENGS}


D = 1024
NT = 4096
NB = 512
NBLK = NT // NB
SEQ = 8192
DFF = 2816
NJ = DFF // 128
EPS = 1e-6
NAQ, NAK, NAV, CQ, CKV, KR, XIN, GB, GC, WIN = 0, 256, 512, 768, 1152, 1408, 1472, 1728, 1984, 2240
NPAR = 96
P_G1, P_G2, P_NAQ, P_NAK, P_CQ, P_CKV, P_QH, P_KN, P_KR, P_CW, P_CB, P_ON, P_ONC, P_BADA = \
    0, 8, 16, 17, 18, 21, 23, 24, 25, 26, 32, 34, 46, 48
SC_NA = 64 ** -0.5
SC_MLA = 96 ** -0.5
TWO_PI = 2.0 * math.pi
CW1 = 6.28125
CW2 = TWO_PI - CW1
NHALO = 256
DBG = {}


class B:
    def __init__(self, nc, es):
        self.nc = nc
        self.es = es
        self.P = Prog(nc, es)
        self.ps = []
        self.psr = []
        self.psbig = []
        for i in range(4):
            big = es.enter_context(nc.psum_tensor(f"psb{i}", [128, 1024], F32))
            self.psbig.append(big)
            for j in range(2):
                self.ps.append(big[:, j * 512:(j + 1) * 512])
                self.psr.append(Res(f"ps{2 * i + j}"))
        self.psi = 0
        self.uid = 0
        self.rr = {}

    def psum(self):
        i = self.psi
        self.psi = (self.psi + 1) % 8
        return self.ps[i], self.psr[i]

    def sb(self, es, name, shape, dt, nres=1):
        self.uid += 1
        t = es.enter_context(self.nc.sbuf_tensor(f"t{self.uid}_{name}", shape, dt))
        if nres == 1:
            return t, Res(name)
        return t, [Res(f"{name}{i}") for i in range(nres)]

    def pool(self, es, name, n, shape, dt):
        self.rr[name] = [0, [(self.sb(es, f"{name}{i}", shape, dt)) for i in range(n)]]

    def nxt(self, name):
        st = self.rr[name]
        i = st[0]
        st[0] = (i + 1) % len(st[1])
        t, r = st[1][i]
        return t, r, f"{name}{i}"

    def mm(self, out_ap, out_res, items):
        n = len(items)
        for i, (l, r, rd) in enumerate(items):
            first, last = i == 0, i == n - 1
            self.P.op("pe", "matmul", reads=rd,
                      writes=[out_res] if last else (),
                      guards=[out_res] if (first and not last) else (),
                      sig=last, out=out_ap, lhsT=l, rhs=r, start=first, stop=last)


def ext(nc, name, shape, dt, kind):
    return nc.dram_tensor(name, list(shape), dt, kind=kind).ap()


def prep_common(b, es, d, nl=2):
    P = b.P
    cm, r_cm = b.sb(es, "cmats", [128, 6, 128], BF16)
    P.dma("pool", "l_cm", cm[:], d["cmats"][:, :, :], writes=[r_cm])
    one, r_one = b.sb(es, "one", [128, 2], F32)
    P.op("dve", "memset", writes=[r_one], ap=one[:], constant=1.0)
    ccol, r_cc = b.sb(es, "ccol", [128, 8], F32)
    P.dma("sp", "l_cc", ccol[:], d["ccol"][:, :], writes=[r_cc])
    cact, r_ca = b.sb(es, "cact", [128, 8], F32)
    P.op("act", "activation", reads=[r_cc], writes=[r_ca], out=cact[:], in_=ccol[:], func=AF.Silu)
    out = []
    pars = []
    for l in range(nl):
        par, r_par = b.sb(es, f"par{l}", [128, NPAR], F32)
        P.dma("sp", f"l_par{l}", par[:], d["params"][l, :, :], writes=[r_par])
        modc, r_mc = b.sb(es, f"modc{l}", [128, 48], F32)
        pars.append((par, r_par, modc, r_mc))
    with ExitStack() as es2:
        modrow, r_mr = b.sb(es2, "modrow", [1, 6144], F32)
        wa = [b.sb(es2, f"wa{i}", [128, 8, 512], F32) for i in range(2)]
        n = 0
        for l in range(nl):
            par, r_par, modc, r_mc = pars[l]
            w_ada = d["w_ada"][l].rearrange("(kc p) n -> p kc n", p=128)
            for jc in range(12):
                wt, wr = wa[n % 2]
                P.dma("sp", f"l_wa{n % 2}", wt[:], w_ada[:, :, jc * 512:(jc + 1) * 512], writes=[wr])
                n += 1
                ps, pr = b.psum()
                b.mm(ps[0:1, :], pr, [(cact[:, kc:kc + 1], wt[:, kc, :], [r_ca, wr]) for kc in range(8)])
                P.op("dve", "tensor_copy", reads=[pr], pwrites=[r_mr], guards=[r_mr] if jc == 0 else (),
                     out=modrow[0:1, jc * 512:(jc + 1) * 512], in_=ps[0:1, :])
            ps, pr = b.psum()
            for j in range(48):
                P.op("pe", "matmul", reads=[r_mr, r_one], writes=[pr] if j == 47 else (), guards=[pr] if j == 0 else (),
                     sig=(j == 47), out=ps[:, j:j + 1], lhsT=modrow[0:1, j * 128:(j + 1) * 128], rhs=one[0:1, 0:1],
                     start=True, stop=True)
            P.op("dve", "tensor_tensor", reads=[pr, r_par], writes=[r_mc], out=modc[:], in0=ps[:, 0:48],
                 in1=par[:, P_BADA:P_BADA + 48], op=ALU.add)
            out.append(dict(par=par, r_par=r_par, cm=cm, r_cm=r_cm, modc=modc, r_mc=r_mc))
        P.replay()
    return out


A_IN = [("xT", [D, NT], F32), ("params", [128, NPAR], F32), ("cmats", [128, 6, 128], F32),
        ("ccol", [128, 8], F32), ("w_ada", [D, 6144], F32), ("rc", [128, 4], F32),
        ("pos", [1, NT], I32), ("w_in_p", [D, WIN], F32), ("w_uq_p", [384, 1024], F32),
        ("w_ukv_k", [256, 512], F32), ("w_ukv_v", [256, 512], F32)]
A_OUT = [("qT", [8, 128, NT], BF16), ("kT", [8, 128, NT], BF16), ("vE", [NT, 520], BF16),
         ("naqT", [256, NT], BF16), ("nakT", [256, NT], BF16), ("navE", [NT, 260], BF16),
         ("uT", [256, NT], F32), ("gbT", [256, NT], F32)]


def phase1(b, es, d, o, R=None):
    P = b.P
    t = d["_prep"]
    par, r_par, cm, r_cm, modc, r_mc = t["par"], t["r_par"], t["cm"], t["r_cm"], t["modc"], t["r_mc"]
    ONES, B64, BQ, BK, DUP = (cm[:, i, :] for i in range(5))
    a1, r_a1 = b.sb(es, "a1", [128, 8], F32)
    P.op("dve", "scalar_tensor_tensor", reads=[r_mc, r_par], writes=[r_a1], out=a1[:], in0=modc[:, 8:16], scalar=1.0,
         in1=par[:, P_G1:P_G1 + 8], op0=ALU.add, op1=ALU.mult)
    gs, r_gs = b.sb(es, "gs", [128, 2], F32)
    P.op("dve", "tensor_scalar", reads=[r_par], writes=[r_gs], out=gs[:, 0:1], in0=par[:, P_NAQ:P_NAQ + 1],
         scalar1=SC_NA, scalar2=None, op0=ALU.mult)
    P.op("dve", "tensor_scalar", reads=[r_par], pwrites=[r_gs], out=gs[:, 1:2], in0=par[:, P_QH:P_QH + 1],
         scalar1=SC_MLA, scalar2=None, op0=ALU.mult)
    rc, r_rc = b.sb(es, "rc", [128, 4], F32)
    P.dma("sp", "l_rc", rc[:], d["rc"][:, :], writes=[r_rc])
    TT, r_TT = b.sb(es, "TT", [128, NT], F32)
    TK, r_TK = b.sb(es, "TK", [64, NT], F32)
    with ExitStack() as es2:
        posi, r_pi = b.sb(es2, "posi", [128, NT], I32)
        posf, r_pf = b.sb(es2, "posf", [128, NT], F32)
        y, r_y = b.sb(es2, "ry", [128, NT], F32)
        kf, r_kf = b.sb(es2, "rkf", [128, NT], F32)
        ki, r_ki = b.sb(es2, "rki", [128, NT], I32)
        P.dma("sp", "l_pos", posi[:], d["pos"][0:1, :].partition_broadcast(128), writes=[r_pi])
        P.op("dve", "tensor_copy", reads=[r_pi], writes=[r_pf], out=posf[:], in_=posi[:])
        for (tab, r_tab, n_, c0) in ((TT, r_TT, 128, 0), (TK, r_TK, 64, 2)):
            P.op("dve", "tensor_scalar", reads=[r_pf, r_rc], writes=[r_y], out=y[0:n_, :], in0=posf[0:n_, :],
                 scalar1=rc[0:n_, c0:c0 + 1], scalar2=rc[0:n_, c0 + 1:c0 + 2], op0=ALU.mult, op1=ALU.add)
            P.op("dve", "tensor_scalar", reads=[r_y], writes=[r_kf], out=kf[0:n_, :], in0=y[0:n_, :],
                 scalar1=1.0 / TWO_PI, scalar2=None, op0=ALU.mult)
            P.op("dve", "tensor_copy", reads=[r_kf], writes=[r_ki], out=ki[0:n_, :], in_=kf[0:n_, :])
            P.op("dve", "tensor_copy", reads=[r_ki], writes=[r_kf], out=kf[0:n_, :], in_=ki[0:n_, :])
            P.op("dve", "scalar_tensor_tensor", reads=[r_kf, r_y], writes=[r_y], out=y[0:n_, :], in0=kf[0:n_, :],
                 scalar=-CW1, in1=y[0:n_, :], op0=ALU.mult, op1=ALU.add)
            P.op("dve", "scalar_tensor_tensor", reads=[r_kf, r_y], writes=[r_y], out=y[0:n_, :], in0=kf[0:n_, :],
                 scalar=-CW2, in1=y[0:n_, :], op0=ALU.mult, op1=ALU.add)
            P.op("dve", "tensor_scalar", reads=[r_y], writes=[r_y], out=y[0:n_, :], in0=y[0:n_, :],
                 scalar1=-math.pi, scalar2=math.pi, op0=ALU.max, op1=ALU.min)
            P.op("act", "activation", reads=[r_y], writes=[r_tab], out=tab[0:n_, :], in_=y[0:n_, :], func=AF.Sin)
        P.replay()
    if DBG.get("stop") == "rope":
        b.r_out = {"TT": r_TT, "TK": r_TK}
        b.dbg = (TT, TK)
        return
    win, r_win = b.sb(es, "win", [128, 8, WIN], BF16)
    for kc in range(8):
        P.dma("pool", "l_win", win[:, kc, :], d["w_in_p"][kc * 128:(kc + 1) * 128, :],
              writes=[r_win] if kc == 0 else (), pwrites=[r_win] if kc else (), max_dma_last_dim=4096)
    wuq, r_wuq = b.sb(es, "wuq", [128, 3, 1024], BF16)
    P.dma("pool", "l_wuq", wuq[:], d["w_uq_p"].rearrange("(kc p) n -> p kc n", p=128), writes=[r_wuq], max_dma_last_dim=4096)
    wkk, r_wkk = b.sb(es, "wkk", [128, 2, 512], BF16)
    P.dma("pool", "l_wkk", wkk[:], d["w_ukv_k"].rearrange("(kc p) n -> p kc n", p=128), writes=[r_wkk], max_dma_last_dim=4096)
    wkv, r_wkv = b.sb(es, "wkv", [128, 2, 512], BF16)
    P.dma("pool", "l_wkv", wkv[:], d["w_ukv_v"].rearrange("(kc p) n -> p kc n", p=128), writes=[r_wkv], max_dma_last_dim=4096)
    if "_after_w" in d:
        d["_after_w"]()
    if DBG.get("stop") == "w":
        b.r_out = {"a": r_win, "b": r_wuq, "c": r_wkk, "d": r_wkv}
        return
    xs = [b.sb(es, f"x{i}", [128, 8, NB], F32) for i in range(2)]
    sq, r_sq = b.sb(es, "sq", [128, 8, NB], BF16, nres=8)
    hs = [b.sb(es, f"h{i}", [128, 8, NB], BF16, nres=8) for i in range(2)]
    b.pool(es, "rs", 3, [128, NB], F32)
    b.pool(es, "tmp", 3, [128, NB], F32)
    b.pool(es, "sqc", 4, [128, NB], BF16)
    b.pool(es, "ob", 4, [128, NB], BF16)
    b.pool(es, "of", 3, [128, NB], F32)
    cqn, r_cqn = b.sb(es, "cqn", [128, 3, NB], BF16, nres=3)
    ckvn, r_ckvn = b.sb(es, "ckvn", [128, 2, NB], BF16, nres=2)
    vst = [b.sb(es, f"vst{i}", [128, 4, 8, 65], BF16) for i in range(2)]
    nvst = [b.sb(es, f"nvst{i}", [128, 4, 4, 65], BF16) for i in range(2)]
    for (tl, rr) in vst + nvst:
        P.op("pool", "memset", writes=[rr], ap=tl[:], constant=1.0)

    def rstd_from(ps_ap, ps_res, n_, scale, eps=EPS):
        rt, rr, _ = b.nxt("rs")
        P.op("act", "activation", reads=[ps_res], writes=[rr], out=rt[0:n_, :], in_=ps_ap, func=AF.Ln, scale=scale, bias=eps)
        P.op("act", "activation", reads=[rr], writes=[rr], out=rt[0:n_, :], in_=rt[0:n_, :], func=AF.Exp, scale=-0.5)
        return rt, rr

    r_out = R if R is not None else {k: Res("o_" + k) for k in o}
    b.r_out = r_out

    def store(dst_ap, src_ap, src_res, key, nm):
        P.dma("sp", "st_" + nm, dst_ap, src_ap, reads=[src_res], pwrites=[r_out[key]])

    r_x = R["x_in"] if R is not None else Res("xT")
    xT = d["xT"].rearrange("(c p) n -> p c n", p=128)

    def load_x(blk):
        xt, xr = xs[blk % 2]
        P.dma("sp", f"l_x{blk % 2}", xt[:], xT[:, :, blk * NB:(blk + 1) * NB], reads=[r_x], writes=[xr])

    pend = []

    def flush(keep=0):
        while len(pend) > keep:
            pend.pop(0)()

    def head_norm(ps, pr, n_, bmat, gain_ap, gain_res, out_ap, out_res, mult=None, mult_res=None, after=None):
        assert len(pend) <= 1
        st, sr, _ = b.nxt("sqc")
        P.op("act", "activation", reads=[pr], writes=[sr], out=st[0:n_, :], in_=ps[0:n_, :], func=AF.Square)
        flush()

        def part_b():
            ps2, pr2 = b.psum()
            b.mm(ps2[0:n_, :], pr2, [(bmat, st[0:n_, :], [r_cm, sr])])
            rt, rr = rstd_from(ps2[0:n_, :], pr2, n_, 1.0)
            if mult is None:
                P.op("dve", "scalar_tensor_tensor", reads=[pr, rr, gain_res], writes=[out_res], out=out_ap, in0=ps[0:n_, :],
                     scalar=gain_ap, in1=rt[0:n_, :], op0=ALU.mult, op1=ALU.mult)
            else:
                tt, tr, _ = b.nxt("tmp")
                P.op("dve", "scalar_tensor_tensor", reads=[pr, rr, gain_res], writes=[tr], out=tt[0:n_, :], in0=ps[0:n_, :],
                     scalar=gain_ap, in1=rt[0:n_, :], op0=ALU.mult, op1=ALU.mult)
                P.op("pool", "tensor_tensor", reads=[tr, mult_res], writes=[out_res], out=out_ap, in0=tt[0:n_, :], in1=mult, op=ALU.mult)
            if after is not None:
                after()

        pend.append(part_b)

    def norm1(blk):
        xt, xr = xs[blk % 2]
        h, r_h = hs[blk % 2]
        for c in range(8):
            P.op("pool", "tensor_tensor", reads=[xr], writes=[r_sq[c]], out=sq[:, c, :], in0=xt[:, c, :], in1=xt[:, c, :], op=ALU.mult)
        ps, pr = b.psum()
        b.mm(ps[:], pr, [(ONES, sq[:, c, :], [r_cm, r_sq[c]]) for c in range(8)])
        rt, rr = rstd_from(ps[:], pr, 128, 1.0 / D)
        for c in range(8):
            tt, tr, _ = b.nxt("tmp")
            P.op("dve", "tensor_tensor", reads=[xr, rr], writes=[tr], out=tt[:], in0=xt[:, c, :], in1=rt[:], op=ALU.mult)
            P.op("act", "activation", reads=[tr, r_a1, r_mc], writes=[r_h[c]], out=h[:, c, :], in_=tt[:], func=AF.Identity,
                 scale=a1[:, c:c + 1], bias=modc[:, c:c + 1])

    nblk1 = DBG.get("nblk", NBLK)
    load_x(0)
    if nblk1 > 1:
        load_x(1)
    norm1(0)
    for blk in range(nblk1):
        tok = slice(blk * NB, (blk + 1) * NB)
        h, r_h = hs[blk % 2]

        def proj(col0, ncol):
            ps, pr = b.psum()
            b.mm(ps[0:ncol, :], pr, [(win[:, k, col0:col0 + ncol], h[:, k, :], [r_win, r_h[k]]) for k in range(8)])
            return ps, pr

        for (col0, gain_ap, gain_res, key) in ((NAQ, gs[:, 0:1], r_gs, "naqT"), (NAK, par[:, P_NAK:P_NAK + 1], r_par, "nakT")):
            for m in range(2):
                ps, pr = proj(col0 + m * 128, 128)
                ot, orr, nm = b.nxt("ob")
                head_norm(ps, pr, 128, B64, gain_ap, gain_res, ot[:], orr,
                          after=lambda key=key, m=m, ot=ot, orr=orr, nm=nm: store(o[key][m * 128:(m + 1) * 128, tok], ot[:], orr, key, nm))
        flush()
        nvt, nvr = nvst[blk % 2]
        for tt_ in range(4):
            ps, pr = b.psum()
            b.mm(ps[:, 0:256], pr, [(h[:, k, tt_ * 128:(tt_ + 1) * 128], win[:, k, NAV:NAV + 256], [r_win, r_h[k]]) for k in range(8)])
            P.op("dve", "tensor_copy", reads=[pr], pwrites=[nvr], guards=[nvr] if tt_ == 0 else (),
                 out=nvt[:, tt_, :, 0:64], in_=ps[:, 0:256].rearrange("p (h d) -> p h d", h=4))
        P.dma("sp", f"st_nv{blk % 2}", o["navE"][tok, :].rearrange("(t p) n -> p t n", p=128),
              nvt[:].rearrange("p t h d -> p t (h d)"), reads=[nvr], pwrites=[r_out["navE"]])
        pss = [proj(CQ + m * 128, 128) for m in range(3)]
        sts = []
        for m in range(3):
            st, sr, _ = b.nxt("sqc")
            P.op("act", "activation", reads=[pss[m][1]], writes=[sr], out=st[:], in_=pss[m][0][:], func=AF.Square)
            sts.append((st, sr))
        ps2, pr2 = b.psum()
        b.mm(ps2[:], pr2, [(ONES, sts[m][0][:], [r_cm, sts[m][1]]) for m in range(3)])
        rt, rr = rstd_from(ps2[:], pr2, 128, 1.0 / 384)
        for m in range(3):
            P.op("dve", "scalar_tensor_tensor", reads=[pss[m][1], rr, r_par], writes=[r_cqn[m]], out=cqn[:, m, :],
                 in0=pss[m][0][:], scalar=par[:, P_CQ + m:P_CQ + m + 1], in1=rt[:], op0=ALU.mult, op1=ALU.mult)
        for hh in range(8):
            ps, pr = b.psum()
            b.mm(ps[:], pr, [(wuq[:, m, hh * 128:(hh + 1) * 128], cqn[:, m, :], [r_wuq, r_cqn[m]]) for m in range(3)])
            ot, orr, nm = b.nxt("ob")
            head_norm(ps, pr, 128, BQ, gs[:, 1:2], r_gs, ot[:], orr, mult=TT[:, tok], mult_res=r_TT,
                      after=lambda hh=hh, ot=ot, orr=orr, nm=nm: store(o["qT"][hh, :, tok], ot[:], orr, "qT", nm))
        flush()
        if blk + 1 < nblk1:
            norm1(blk + 1)
        pss = [proj(CKV + m * 128, 128) for m in range(2)]
        sts = []
        for m in range(2):
            st, sr, _ = b.nxt("sqc")
            P.op("act", "activation", reads=[pss[m][1]], writes=[sr], out=st[:], in_=pss[m][0][:], func=AF.Square)
            sts.append((st, sr))
        ps2, pr2 = b.psum()
        b.mm(ps2[:], pr2, [(ONES, sts[m][0][:], [r_cm, sts[m][1]]) for m in range(2)])
        rt, rr = rstd_from(ps2[:], pr2, 128, 1.0 / 256)
        for m in range(2):
            P.op("dve", "scalar_tensor_tensor", reads=[pss[m][1], rr, r_par], writes=[r_ckvn[m]], out=ckvn[:, m, :],
                 in0=pss[m][0][:], scalar=par[:, P_CKV + m:P_CKV + m + 1], in1=rt[:], op0=ALU.mult, op1=ALU.mult)
        for c in range(4):
            ps, pr = b.psum()
            b.mm(ps[:], pr, [(wkk[:, m, c * 128:(c + 1) * 128], ckvn[:, m, :], [r_wkk, r_ckvn[m]]) for m in range(2)])
            ot, orr, nm = b.nxt("ob")
            def st_k(c=c, ot=ot, orr=orr, nm=nm):
                store(o["kT"][2 * c, 0:64, tok], ot[0:64, :], orr, "kT", nm)
                store(o["kT"][2 * c + 1, 0:64, tok], ot[64:128, :], orr, "kT", nm)

            head_norm(ps, pr, 128, B64, par[:, P_KN:P_KN + 1], r_par, ot[:], orr, after=st_k)
        flush()
        vt, vr = vst[blk % 2]
        for tt_ in range(4):
            ps, pr = b.psum()
            b.mm(ps[:], pr, [(ckvn[:, m, tt_ * 128:(tt_ + 1) * 128], wkv[:, m, :], [r_wkv, r_ckvn[m]]) for m in range(2)])
            P.op("dve", "tensor_copy", reads=[pr], pwrites=[vr], guards=[vr] if tt_ == 0 else (),
                 out=vt[:, tt_, :, 0:64], in_=ps[:].rearrange("p (h d) -> p h d", h=8))
        P.dma("sp", f"st_v{blk % 2}", o["vE"][tok, :].rearrange("(t p) n -> p t n", p=128),
              vt[:].rearrange("p t h d -> p t (h d)"), reads=[vr], pwrites=[r_out["vE"]])
        ps, pr = proj(KR, 64)
        kt_, ktr, _ = b.nxt("ob")
        head_norm(ps, pr, 64, BK[0:64, 0:64], par[0:64, P_KR:P_KR + 1], r_par, kt_[0:64, :], ktr, mult=TK[:, tok], mult_res=r_TK)
        flush()
        ps2, pr2 = b.psum()
        b.mm(ps2[0:64, :], pr2, [(DUP[0:64, 0:64], kt_[0:64, :], [r_cm, ktr])])
        ot, orr, nm = b.nxt("ob")
        P.op("act", "activation", reads=[pr2], writes=[orr], out=ot[0:64, :], in_=ps2[0:64, :], func=AF.Copy)
        for hh in range(8):
            store(o["kT"][hh, 64:128, tok], ot[0:64, :], orr, "kT", nm)
        for m in range(2):
            psx, prx = proj(XIN + m * 128, 128)
            tt, tr, _ = b.nxt("tmp")
            P.op("act", "activation", reads=[prx], writes=[tr], out=tt[:], in_=psx[:], func=AF.Copy)
            psc, prc = proj(GC + m * 128, 128)
            ot, orr, nm = b.nxt("of")
            P.op("dve", "tensor_tensor", reads=[prc, tr], writes=[orr], out=ot[:], in0=psc[:], in1=tt[:], op=ALU.mult)
            if "uTh" in o:
                store(o["uTh"][m * 128:(m + 1) * 128, 1 + blk * NB:1 + (blk + 1) * NB], ot[:], orr, "uTh", nm)
            else:
                store(o["uT"][m * 128:(m + 1) * 128, tok], ot[:], orr, "uT", nm)
            psb, prb = proj(GB + m * 128, 128)
            ot, orr, nm = b.nxt("of")
            P.op("act", "activation", reads=[prb], writes=[orr], out=ot[:], in_=psb[:], func=AF.Copy)
            store(o["gbT"][m * 128:(m + 1) * 128, tok], ot[:], orr, "gbT", nm)
        flush()
        if blk + 2 < nblk1:
            load_x(blk + 2)


def const_mats():
    cm = np.zeros((128, 6, 128), np.float32)
    cm[:, 0, :] = 1.0
    for g in range(2):
        cm[g * 64:(g + 1) * 64, 1, g * 64:(g + 1) * 64] = 1.0 / 64
    cm[0:64, 2, 0:64] = 1.0 / 64
    cm[64:96, 2, 64:96] = 1.0 / 32
    cm[96:128, 2, 96:128] = 1.0 / 32
    cm[0:32, 3, 0:32] = 1.0 / 32
    cm[32:64, 3, 32:64] = 1.0 / 32
    for i in range(64):
        cm[i, 4, i % 32] = 1.0
        cm[i, 4, 32 + i % 32] = 1.0
    cm[0:64, 5, 0:64] = 1.0 / 64
    cm[64, 5, 0:64] = EPS
    return cm


def rope_consts():
    inv = (10000.0 ** (-np.arange(0, 32, 2, dtype=np.float32) / 32)).astype(np.float32)
    rc = np.zeros((128, 4), np.float32)
    rc[:, 1] = math.pi / 2
    for p in range(64, 128):
        rc[p, 0] = inv[(p - 64) % 16]
        rc[p, 1] = math.pi / 2 if p < 96 else (math.pi if p < 112 else 0.0)
    for p in range(64):
        rc[p, 2] = inv[p % 16]
        rc[p, 3] = math.pi / 2 if p < 32 else (math.pi if p < 48 else 0.0)
    return rc


def layer_host(inp, l):
    f = lambda k: np.asarray(inp[k][l], np.float32)
    w_in = f("w_in")
    kr = w_in[:, 1408:1440]
    w_in_p = np.concatenate([w_in[:, 0:1440], kr[:, 16:32], kr[:, 0:16], w_in[:, 1440:2208]], axis=1)
    wuq = f("mla_w_uq").reshape(384, 8, 96)
    w_uq_p = np.concatenate([wuq[:, :, 0:96], wuq[:, :, 80:96], wuq[:, :, 64:80]], axis=2).reshape(384, 1024)
    wukv = f("mla_w_ukv").reshape(256, 8, 128)
    w_ukv_k = wukv[:, :, 0:64].reshape(256, 512)
    w_ukv_v = wukv[:, :, 64:128].reshape(256, 512)
    par = np.zeros((128, NPAR), np.float32)
    par[:, P_G1:P_G1 + 8] = f("norm1_g").reshape(8, 128).T
    par[:, P_G2:P_G2 + 8] = f("norm2_g").reshape(8, 128).T
    par[:, P_NAQ] = np.tile(f("na_q_g"), 2)
    par[:, P_NAK] = np.tile(f("na_k_g"), 2)
    par[:, P_CQ:P_CQ + 3] = f("mla_q_a_g").reshape(3, 128).T
    par[:, P_CKV:P_CKV + 2] = f("mla_kv_a_g").reshape(2, 128).T
    qr = f("mla_qr_g")
    par[:, P_QH] = np.concatenate([f("mla_qn_g"), qr, qr[16:32], qr[0:16]])
    par[:, P_KN] = np.tile(f("mla_kn_g"), 2)
    krg = f("mla_kr_g")
    par[0:64, P_KR] = np.concatenate([krg, krg[16:32], krg[0:16]])
    cw = f("conv_w")
    for k in range(3):
        for c in range(2):
            par[:, P_CW + k * 2 + c] = cw[k, c * 128:(c + 1) * 128]
    par[:, P_CB:P_CB + 2] = f("conv_b").reshape(2, 128).T
    og = f("out_norm_g")
    par[0:64, P_ON:P_ON + 12] = og[0:768].reshape(12, 64).T
    par[:, P_ONC:P_ONC + 2] = og[768:1024].reshape(2, 128).T
    par[:, P_BADA:P_BADA + 48] = f("b_ada").reshape(48, 128).T
    rpb = f("na_rpb")
    wq = np.arange(64)[None, :]
    wk = np.arange(64)[:, None]
    cs = np.clip(wq - 8, 0, 48)
    col_ok = (wk >= cs) & (wk < cs + 16)
    dc = np.clip(wk - wq + 15, 0, 30)
    tb = np.full((2, 64, 4, 22, 64), -30000.0, np.float32)
    for jr in range(2):
        for s in range(22):
            dr = 10 + jr - s
            if -7 <= dr <= 7:
                for hh in range(4):
                    g = rpb[hh, dr + 7][dc]
                    tb[jr, :, hh, s, :] = np.where(col_ok, g, np.float32(-30000.0))
    return dict(w_in_p=np.ascontiguousarray(w_in_p), w_uq_p=np.ascontiguousarray(w_uq_p),
                w_ukv_k=np.ascontiguousarray(w_ukv_k), w_ukv_v=np.ascontiguousarray(w_ukv_v),
                params=par, w_ada=f("w_ada"), TB2=tb.reshape(128, 4 * 22 * 64),
                w_out=f("w_out"), w_gu=f("w_gu"), w_down=f("w_down"))


def row_mask(half):
    m = np.zeros((128, 3, 8, 8), np.float32)
    for ty, qb in enumerate((0, 1, 7)):
        for kt in range(8):
            for jr in range(2):
                for qr in range(8):
                    qrow = half * 64 + qb * 8 + qr
                    krow = half * 64 + qb * 8 - 4 + 2 * kt + jr
                    rs = min(max(qrow - 4, 0), 120)
                    ok = (0 <= krow < 128) and (rs <= krow < rs + 8)
                    m[jr * 64:(jr + 1) * 64, ty, kt, qr] = 1.0 if ok else 0.0
    return m.reshape(128, 192)


def core_static(inp, core):
    bi, half = core // 2, core % 2
    x = np.asarray(inp["x"], np.float32)
    sl = slice(half * NT, (half + 1) * NT)
    return dict(xT=np.ascontiguousarray(x[bi, sl, :].T),
                ccol=np.ascontiguousarray(np.asarray(inp["c"], np.float32)[bi].reshape(8, 128).T),
                pos=np.ascontiguousarray(np.asarray(inp["positions"], np.int32)[bi, sl].reshape(1, NT)),
                rowmask=row_mask(half))


NTH = NT + 2 * NHALO
B_IN = [("xT", [D, NT], F32), ("params", [128, NPAR], F32), ("cmats", [128, 6, 128], F32),
        ("ccol", [128, 8], F32), ("w_ada", [D, 6144], F32),
        ("qT", [8, 128, NT], BF16), ("kTf", [8, 128, SEQ], BF16), ("vEf", [SEQ, 520], BF16),
        ("naqT", [256, NT], BF16), ("nakTh", [256, NTH], BF16), ("navEh", [NTH, 260], BF16),
        ("uTh", [256, NT + 2], F32), ("gbT", [256, NT], F32), ("TB2", [128, 5632], F32),
        ("rowmask", [128, 192], F32), ("w_out", [D, D], F32), ("w_gu", [D, 2 * DFF], F32),
        ("w_down", [DFF, D], F32)]
B_OUT = [("xoT", [D, NT], F32)]


def prep_ffn_weights(b, d, scr, r):
    P = b.P
    for j in range(NJ):
        for half in range(2):
            P.dma("pool", "c_wgu", scr["wgu_s"][j, :, half * 1024:(half + 1) * 1024].rearrange("p (kc n) -> p kc n", kc=8),
                  d["w_gu"][:, half * DFF + j * 128: half * DFF + (j + 1) * 128].rearrange("(kc p) n -> p kc n", p=128),
                  pwrites=[r["wgus"]])
    for oc in range(8):
        P.dma("pool", "c_wd", scr["wd_s"][oc].rearrange("p (j n) -> p j n", j=NJ),
              d["w_down"][:, oc * 128:(oc + 1) * 128].rearrange("(j p) n -> p j n", p=128), pwrites=[r["wds"]])


def attn_finish(b, st8, psO, prO, g, tok, scr, r, t, nbank=None):
    P = b.P
    par, r_par, cm, r_cm = t["par"], t["r_par"], t["cm"], t["r_cm"]
    i = st8["f"]
    st8["f"] += 1
    ot, orr = st8["osb"][i % 2]
    P.op("dve", "tensor_copy", reads=[prO], writes=[orr], out=ot[:], in_=psO[0:65, :])
    sq, sr = st8["osq"][i % 2]
    P.op("pool", "tensor_tensor", reads=[orr], writes=[sr], out=sq[:], in0=ot[:], in1=ot[:], op=ALU.mult)

    def part_b():
        nb_ = (6 + i % 2) if nbank is None else nbank
        ps2, pr2 = b.ps[nb_], b.psr[nb_]
        b.mm(ps2[0:64, :], pr2, [(cm[0:65, 5, 0:64], sq[0:65, :], [r_cm, sr])])
        rt, rr = st8["ors"][i % 2]
        P.op("act", "activation", reads=[pr2], writes=[rr], out=rt[:], in_=ps2[0:64, :], func=AF.Ln, scale=2.0 ** -20)
        P.op("act", "activation", reads=[rr], writes=[rr], out=rt[:], in_=rt[:], func=AF.Exp, scale=-0.5, bias=-10.0 * math.log(2.0))
        mt, mr = st8["mixo"][i % 3]
        P.op("dve", "scalar_tensor_tensor", reads=[orr, rr, r_par], writes=[mr], out=mt[:], in0=ot[0:64, :],
             scalar=par[0:64, P_ON + g:P_ON + g + 1], in1=rt[:], op0=ALU.mult, op1=ALU.mult)
        P.dma("sp", f"st_mixo{i % 3}", scr["mixT"][g, :, tok], mt[:], reads=[mr], pwrites=[r["mix"]])

    return part_b


def attn_state(b, es):
    return dict(f=0, s=0, p=0, it=0,
                osb=[b.sb(es, f"osb{i}", [65, NB], F32) for i in range(2)],
                osq=[b.sb(es, f"osq{i}", [65, NB], BF16) for i in range(2)],
                ors=[b.sb(es, f"ors{i}", [64, NB], F32) for i in range(2)],
                mixo=[b.sb(es, f"mixo{i}", [64, NB], BF16) for i in range(3)])


def phase2_mla(b, d, scr, r, t, R):
    P = b.P
    with ExitStack() as es:
        st8 = attn_state(b, es)
        V, r_V = b.sb(es, "Vall", [128, 64, 520], BF16)
        first = True
        for cch in range(4):
            for rk in range(2):
                t0 = rk * 32 + cch * 8
                P.dma("sp", "l_V", V[:, t0:t0 + 8, :], d["vEg"][cch, rk].rearrange("(t p) n -> p t n", p=128), reads=[R["vEg"]],
                      writes=[r_V] if first else (), pwrites=() if first else [r_V])
                first = False
        Kh = [b.sb(es, f"Kh{i}", [128, SEQ], BF16) for i in range(2)]
        Qh = [b.sb(es, f"Qh{i}", [128, NT], BF16) for i in range(2)]
        pT = [b.sb(es, f"pT{i}", [128, 2 * NB], BF16) for i in range(4)]

        def load_h(hh):
            P.dma("sp", f"l_K{hh % 2}", Kh[hh % 2][0][:, 0:NT], d["kTg"][hh // 2, 0, hh % 2, :, :], reads=[R["kTg"]], writes=[Kh[hh % 2][1]])
            P.dma("sp", f"l_K{hh % 2}", Kh[hh % 2][0][:, NT:SEQ], d["kTg"][hh // 2, 1, hh % 2, :, :], reads=[R["kTg"]], pwrites=[Kh[hh % 2][1]])
            P.dma("sp", f"l_Q{hh % 2}", Qh[hh % 2][0][:], d["qT"][hh, :, :], reads=[R["qT"]], writes=[Qh[hh % 2][1]])

        nh = DBG.get("mla_heads", 8)
        deferred = []
        load_h(0)
        for hh in range(nh):
            if hh + 1 < nh:
                load_h(hh + 1)
            kt_, kr_ = Kh[hh % 2]
            qt_, qr_ = Qh[hh % 2]
            for qb in range(NBLK):
                tok = slice(qb * NB, (qb + 1) * NB)
                psO, prO = b.ps[4], b.psr[4]

                def S2(j):
                    pb = (0, 1, 3)[st8["s"] % 3]
                    st8["s"] += 1
                    for u in range(2):
                        kt = 2 * j + u
                        P.op("pe", "matmul", reads=[kr_, qr_], writes=[b.psr[2 * pb + u]], out=b.ps[2 * pb + u],
                             lhsT=kt_[:, kt * 128:(kt + 1) * 128], rhs=qt_[:, tok], start=True, stop=True)
                    return pb

                pend = {0: S2(0), 1: S2(1), 2: S2(2)}
                for j in range(32):
                    if j == 3 and deferred:
                        deferred.pop()()
                    pb = pend.pop(j)
                    pt, ptr = pT[st8["p"] % 4]
                    st8["p"] += 1
                    P.op("act", "activation", reads=[b.psr[2 * pb], b.psr[2 * pb + 1]], writes=[ptr], out=pt[:],
                         in_=b.psbig[pb][:], func=AF.Exp)
                    for u in range(2):
                        kt = 2 * j + u
                        P.op("pe", "matmul", reads=[r_V, ptr], writes=[prO] if kt == 63 else (), guards=[prO] if kt == 0 else (),
                             sig=(kt == 63), out=psO[0:65, :], lhsT=V[:, kt, hh * 65:(hh + 1) * 65], rhs=pt[:, u * NB:(u + 1) * NB],
                             start=(kt == 0), stop=(kt == 63))
                    if j + 3 < 32:
                        pend[j + 3] = S2(j + 3)
                deferred.append(attn_finish(b, st8, psO, prO, 4 + hh, tok, scr, r, t, nbank=5))
        while deferred:
            deferred.pop()()
        P.replay()


def phase2_na(b, d, scr, r, t, R):
    P = b.P
    with ExitStack() as es:
        st8 = attn_state(b, es)
        naq, r_naq = b.sb(es, "naqz", [128, 4, NT], BF16)
        P.op("pool", "memset", writes=[r_naq], ap=naq[:], constant=0.0)
        for hh in range(4):
            po = (hh % 2) * 64
            P.dma("sp", "l_naq", naq[po:po + 64, hh, :], d["naqT"][hh * 64:(hh + 1) * 64, :], reads=[R["naqT"]],
                  pwrites=[r_naq], guards=[r_naq])
        nak, r_nak = b.sb(es, "nak", [128, 2, NTH], BF16)
        P.dma("sp", "l_nak", nak[:, :, NHALO:NHALO + NT], d["nakT"].rearrange("(c p) n -> p c n", p=128), reads=[R["nakT"]], writes=[r_nak])
        P.dma("sp", "l_nak", nak[:, :, 0:NHALO], d["nakg"][0, :, NHALO:2 * NHALO].rearrange("(c p) n -> p c n", p=128),
              reads=[R["nakg"]], pwrites=[r_nak])
        P.dma("sp", "l_nak", nak[:, :, NHALO + NT:NTH], d["nakg"][1, :, 0:NHALO].rearrange("(c p) n -> p c n", p=128),
              reads=[R["nakg"]], pwrites=[r_nak])
        nav, r_nav = b.sb(es, "nav", [128, NTH // 128, 260], BF16)
        P.dma("sp", "l_nav", nav[:, 2:2 + NT // 128, :], d["navE"].rearrange("(t p) n -> p t n", p=128), reads=[R["navE"]], writes=[r_nav])
        P.dma("sp", "l_nav", nav[:, 0:2, :], d["navg"][0, NHALO:2 * NHALO, :].rearrange("(t p) n -> p t n", p=128),
              reads=[R["navg"]], pwrites=[r_nav])
        P.dma("sp", "l_nav", nav[:, 2 + NT // 128:NTH // 128, :], d["navg"][1, 0:NHALO, :].rearrange("(t p) n -> p t n", p=128),
              reads=[R["navg"]], pwrites=[r_nav])
        rm, r_rm = b.sb(es, "rm", [128, 3, 8, 8], F32)
        P.dma("sp", "l_rm", rm[:].rearrange("p a b c -> p (a b c)"), d["rowmask"][:, :], writes=[r_rm])
        tbf, r_tbf = b.sb(es, "tbf", [128, 5632], F32)
        P.dma("sp", "l_tb", tbf[:], d["TB2"][:, :], writes=[r_tbf])
        Et, r_Et = b.sb(es, "Etab", [128, 4, 22, 64], BF16)
        P.op("act", "activation", reads=[r_tbf], writes=[r_Et], out=Et[:].rearrange("p a b c -> p (a b c)"), in_=tbf[:], func=AF.Exp)
        e1 = [b.sb(es, f"e1_{i}", [128, 2 * NB], BF16) for i in range(3)]
        pp = [b.sb(es, f"pp_{i}", [128, 2 * NB], BF16) for i in range(3)]
        tE = [b.sb(es, f"tE_{i}", [128, 2, NB], BF16) for i in range(3)]
        EM, r_EM = b.sb(es, "EM", [128, 4, 8, NB], BF16)
        first = True
        for hh in range(4):
            for kt in range(8):
                P.op("pool", "tensor_tensor", reads=[r_Et, r_rm], writes=[r_EM] if first else (), pwrites=() if first else [r_EM],
                     out=EM[:, hh, kt, :].rearrange("p (s w) -> p s w", s=8), in0=Et[:, hh, 14 - 2 * kt:22 - 2 * kt, :],
                     in1=rm[:, 1, kt, :].unsqueeze(2).to_broadcast([128, 8, 64]), op=ALU.mult)
                first = False
        seq = [(qb, hh, j) for qb in range(NBLK) for hh in range(4) for j in range(4)]
        LOOK = 2
        bufs = {}

        def S2(i):
            qb, hh, j = seq[i]
            c, po = hh // 2, (hh % 2) * 64
            pb = (0, 1, 3)[st8["s"] % 3]
            st8["s"] += 1
            for u in range(2):
                kt = 2 * j + u
                tok0 = (qb * 8 + 2 * kt) * 64
                P.op("pe", "matmul", reads=[r_nak, r_naq], writes=[b.psr[2 * pb + u]], out=b.ps[2 * pb + u],
                     lhsT=nak[:, c, tok0:tok0 + 128], rhs=naq[:, hh, qb * NB:(qb + 1) * NB], start=True, stop=True)
            bufs[i] = pb

        for i in range(LOOK):
            S2(i)
        deferred = []
        psO, prO = b.ps[4], b.psr[4]
        for i, (qb, hh, j) in enumerate(seq):
            tok = slice(qb * NB, (qb + 1) * NB)
            ty = 0 if qb == 0 else (2 if qb == NBLK - 1 else 1)
            pb = bufs.pop(i)
            a1_, a1r = e1[i % 3]
            a3_, a3r = pp[i % 3]
            if ty == 1:
                emul, emr = EM[:, hh, 2 * j:2 * j + 2, :].rearrange("p k n -> p (k n)"), r_EM
            else:
                te, ter = tE[i % 3]
                for u in range(2):
                    kt = 2 * j + u
                    P.op("pool", "tensor_tensor", reads=[r_Et, r_rm], writes=[ter] if u == 0 else (), pwrites=[ter] if u else (),
                         out=te[:, u, :].rearrange("p (s w) -> p s w", s=8), in0=Et[:, hh, 14 - 2 * kt:22 - 2 * kt, :],
                         in1=rm[:, ty, kt, :].unsqueeze(2).to_broadcast([128, 8, 64]), op=ALU.mult)
                emul, emr = te[:].rearrange("p k n -> p (k n)"), ter
            P.op("act", "activation", reads=[b.psr[2 * pb], b.psr[2 * pb + 1]], writes=[a1r], out=a1_[:], in_=b.psbig[pb][:], func=AF.Exp)
            P.op("dve", "tensor_tensor", reads=[a1r, emr], writes=[a3r], out=a3_[:], in0=a1_[:], in1=emul, op=ALU.mult)
            for u in range(2):
                kt = 2 * j + u
                P.op("pe", "matmul", reads=[r_nav, a3r], writes=[prO] if kt == 7 else (), guards=[prO] if kt == 0 else (),
                     sig=(kt == 7), out=psO[0:65, :], lhsT=nav[:, qb * 4 + kt, hh * 65:(hh + 1) * 65], rhs=a3_[:, u * NB:(u + 1) * NB],
                     start=(kt == 0), stop=(kt == 7))
            if i + LOOK < len(seq):
                S2(i + LOOK)
            if j == 3:
                deferred.append(attn_finish(b, st8, psO, prO, hh, tok, scr, r, t, nbank=5))
            if j == 1 and deferred:
                deferred.pop()()
        while deferred:
            deferred.pop()()
        P.replay()


def phase3(b, d, o, scr, r, t, R):
    P = b.P
    par, r_par, cm, r_cm, modc, r_mc = t["par"], t["r_par"], t["cm"], t["r_cm"], t["modc"], t["r_mc"]
    ONES, B64 = cm[:, 0, :], cm[:, 1, :]
    with ExitStack() as es:
        a2, r_a2 = b.sb(es, "a2", [128, 8], F32)
        P.op("dve", "scalar_tensor_tensor", reads=[r_mc, r_par], writes=[r_a2], out=a2[:], in0=modc[:, 32:40], scalar=1.0,
             in1=par[:, P_G2:P_G2 + 8], op0=ALU.add, op1=ALU.mult)
        woa, r_woa = b.sb(es, "woa", [128, 6, D], BF16)
        P.dma("pool", "l_woa", woa[:], d["w_out"][0:768, :].rearrange("(c p) n -> p c n", p=128), writes=[r_woa], max_dma_last_dim=4096)
        woc, r_woc = b.sb(es, "woc", [128, 2, D], BF16)
        P.dma("pool", "l_woc", woc[:], d["w_out"][768:1024, :].rearrange("(c p) n -> p c n", p=128), writes=[r_woc], max_dma_last_dim=4096)
        xs = [b.sb(es, f"x3_{i}", [128, 8, NB], F32, nres=8) for i in range(2)]
        mixs = [b.sb(es, f"mix{i}", [128, 6, NB], BF16) for i in range(2)]
        us = [b.sb(es, f"u{i}", [128, 2, NB + 2], F32) for i in range(2)]
        gbs = [b.sb(es, f"gb{i}", [128, 2, NB], F32) for i in range(2)]
        b.pool(es, "cv", 3, [128, NB], F32)
        b.pool(es, "rs3", 2, [128, NB], F32)
        b.pool(es, "sg", 2, [128, NB], F32)
        b.pool(es, "sqv", 2, [128, NB], BF16)
        mixc, r_mixc = b.sb(es, "mixc", [128, 2, NB], BF16, nres=2)
        sq, r_sq = b.sb(es, "sq3", [128, 8, NB], BF16, nres=8)
        h2, r_h2 = b.sb(es, "h2", [128, 8, NB], BF16, nres=8)
        aT, r_aT = b.sb(es, "aT", [128, NJ, NB], BF16, nres=NJ)
        wgu = [b.sb(es, f"wgu{i}", [128, 2048], BF16) for i in range(3)]
        wd = [b.sb(es, f"wd{i}", [128, NJ, 128], BF16) for i in range(2)]
        xT = d["xT"].rearrange("(c p) n -> p c n", p=128)
        xoT = o["xoT"].rearrange("(c p) n -> p c n", p=128)
        r_xin = R["x_in"]
        r_d = {"u": R["uTh"], "gb": R["gbT"]}
        edge, r_edge = b.sb(es, "edge", [128, 2], F32)
        P.dma("sp", "l_edge", edge[:], d["edge"][:, :], writes=[r_edge])
        nw = {"gu": 0, "d": 0}

        def rstd_from(ps_ap, ps_res, scale):
            rt, rr, _ = b.nxt("rs3")
            P.op("act", "activation", reads=[ps_res], writes=[rr], out=rt[:], in_=ps_ap, func=AF.Ln, scale=scale, bias=EPS)
            P.op("act", "activation", reads=[rr], writes=[rr], out=rt[:], in_=rt[:], func=AF.Exp, scale=-0.5)
            return rt, rr

        def loads(blk):
            tok = slice(blk * NB, (blk + 1) * NB)
            i = blk % 2
            P.dma("sp", f"l_x3{i}", xs[i][0][:], xT[:, :, tok], reads=[r_xin], writes=xs[i][1])
            P.dma("sp", f"l_mix{i}", mixs[i][0][:], scr["mixT"][:, :, tok].rearrange("(c g) p n -> (g p) c n", g=2), reads=[r["mix"]], writes=[mixs[i][1]])
            P.dma("sp", f"l_u{i}", us[i][0][:], d["uTh"][:, blk * NB:blk * NB + NB + 2].rearrange("(c p) n -> p c n", p=128),
                  reads=[r_d["u"]], writes=[us[i][1]])
            P.dma("sp", f"l_gb{i}", gbs[i][0][:], d["gbT"][:, tok].rearrange("(c p) n -> p c n", p=128), reads=[r_d["gb"]], writes=[gbs[i][1]])
            if blk == 0:
                P.op("dve", "tensor_scalar", reads=[r_edge], pwrites=[us[i][1]], guards=[us[i][1]], out=us[i][0][:, :, 0:1],
                     in0=us[i][0][:, :, 0:1], scalar1=edge[:, 0:1], scalar2=None, op0=ALU.mult)
            if blk == NBLK - 1:
                P.op("dve", "tensor_scalar", reads=[r_edge], pwrites=[us[i][1]], guards=[us[i][1]], out=us[i][0][:, :, NB + 1:NB + 2],
                     in0=us[i][0][:, :, NB + 1:NB + 2], scalar1=edge[:, 1:2], scalar2=None, op0=ALU.mult)

        nb3 = DBG.get("nblk3", NBLK)
        loads(0)
        for blk in range(nb3):
            tok = slice(blk * NB, (blk + 1) * NB)
            if blk + 1 < nb3:
                loads(blk + 1)
            i = blk % 2
            xt, xr = xs[i]
            mt, mr = mixs[i]
            ut, ur = us[i]
            gt, gr = gbs[i]
            for c in range(2):
                cv, cr, _ = b.nxt("cv")
                P.op("dve", "tensor_scalar", reads=[ur, r_par], writes=[cr], out=cv[:], in0=ut[:, c, 0:NB],
                     scalar1=par[:, P_CW + c:P_CW + c + 1], scalar2=None, op0=ALU.mult)
                P.op("dve", "scalar_tensor_tensor", reads=[ur, r_par, cr], writes=[cr], out=cv[:], in0=ut[:, c, 1:NB + 1],
                     scalar=par[:, P_CW + 2 + c:P_CW + 3 + c], in1=cv[:], op0=ALU.mult, op1=ALU.add)
                P.op("dve", "scalar_tensor_tensor", reads=[ur, r_par, cr], writes=[cr], out=cv[:], in0=ut[:, c, 2:NB + 2],
                     scalar=par[:, P_CW + 4 + c:P_CW + 5 + c], in1=cv[:], op0=ALU.mult, op1=ALU.add)
                P.op("dve", "scalar_tensor_tensor", reads=[gr, r_par, cr], writes=[cr], out=cv[:], in0=cv[:],
                     scalar=par[:, P_CB + c:P_CB + c + 1], in1=gt[:, c, :], op0=ALU.add, op1=ALU.mult)
                sv, svr, _ = b.nxt("sqv")
                P.op("pool", "tensor_tensor", reads=[cr], writes=[svr], out=sv[:], in0=cv[:], in1=cv[:], op=ALU.mult)
                ps2, pr2 = b.psum()
                b.mm(ps2[:], pr2, [(B64, sv[:], [r_cm, svr])])
                rt, rr = rstd_from(ps2[:], pr2, 1.0)
                P.op("dve", "scalar_tensor_tensor", reads=[cr, rr, r_par], writes=[r_mixc[c]], out=mixc[:, c, :], in0=cv[:],
                     scalar=par[:, P_ONC + c:P_ONC + c + 1], in1=rt[:], op0=ALU.mult, op1=ALU.mult)
            for oc in range(8):
                ps, pr = b.psum()
                items = [(woa[:, g, oc * 128:(oc + 1) * 128], mt[:, g, :], [r_woa, mr]) for g in range(6)]
                items += [(woc[:, c, oc * 128:(oc + 1) * 128], mixc[:, c, :], [r_woc, r_mixc[c]]) for c in range(2)]
                b.mm(ps[:], pr, items)
                P.op("dve", "scalar_tensor_tensor", reads=[pr, r_mc], writes=[xr[oc]], out=xt[:, oc, :], in0=ps[:],
                     scalar=modc[:, 16 + oc:17 + oc], in1=xt[:, oc, :], op0=ALU.mult, op1=ALU.add)
            for c in range(8):
                P.op("pool", "tensor_tensor", reads=[xr[c]], writes=[r_sq[c]], out=sq[:, c, :], in0=xt[:, c, :], in1=xt[:, c, :], op=ALU.mult)
            ps, pr = b.psum()
            b.mm(ps[:], pr, [(ONES, sq[:, c, :], [r_cm, r_sq[c]]) for c in range(8)])
            rt, rr = rstd_from(ps[:], pr, 1.0 / D)
            for c in range(8):
                cv, cr, _ = b.nxt("cv")
                P.op("dve", "tensor_tensor", reads=[xr[c], rr], writes=[cr], out=cv[:], in0=xt[:, c, :], in1=rt[:], op=ALU.mult)
                P.op("act", "activation", reads=[cr, r_a2, r_mc], writes=[r_h2[c]], out=h2[:, c, :], in_=cv[:], func=AF.Identity,
                     scale=a2[:, c:c + 1], bias=modc[:, 24 + c:25 + c])
            for j in range(NJ):
                wt, wr = wgu[nw["gu"] % 3]
                P.dma("sp", f"l_wgu{nw['gu'] % 3}", wt[:], scr["wgu_s"][j, :, :], reads=[r["wgus"]], writes=[wr])
                nw["gu"] += 1
                psg, prg = b.psum()
                b.mm(psg[:], prg, [(wt[:, k * 128:(k + 1) * 128], h2[:, k, :], [wr, r_h2[k]]) for k in range(8)])
                psu, pru = b.psum()
                b.mm(psu[:], pru, [(wt[:, 1024 + k * 128:1024 + (k + 1) * 128], h2[:, k, :], [wr, r_h2[k]]) for k in range(8)])
                sg, sgr, _ = b.nxt("sg")
                P.op("act", "activation", reads=[prg], writes=[sgr], out=sg[:], in_=psg[:], func=AF.Silu)
                P.op("dve", "tensor_tensor", reads=[sgr, pru], writes=[r_aT[j]], out=aT[:, j, :], in0=psu[:], in1=sg[:], op=ALU.mult)
            for oc in range(8):
                wt, wr = wd[nw["d"] % 2]
                P.dma("sp", f"l_wd{nw['d'] % 2}", wt[:].rearrange("p j n -> p (j n)"), scr["wd_s"][oc, :, :], reads=[r["wds"]], writes=[wr])
                nw["d"] += 1
                ps, pr = b.psum()
                b.mm(ps[:], pr, [(wt[:, j, :], aT[:, j, :], [wr, r_aT[j]]) for j in range(NJ)])
                P.op("dve", "scalar_tensor_tensor", reads=[pr, r_mc], writes=[xr[oc]], out=xt[:, oc, :], in0=ps[:],
                     scalar=modc[:, 40 + oc:41 + oc], in1=xt[:, oc, :], op0=ALU.mult, op1=ALU.add)
            P.dma("sp", f"st_x3{i}", xoT[:, :, tok], xt[:], reads=xr, pwrites=[R["x_out"]])
        P.replay()


PAIRS = [[0, 1], [2, 3], [4, 5], [6, 7]]
F_IN = [("x0T", [D, NT], F32), ("ccol", [128, 8], F32), ("pos", [1, NT], I32), ("rowmask", [128, 192], F32),
        ("edge", [128, 2], F32), ("cmats", [128, 6, 128], F32), ("rc", [128, 4], F32),
        ("params", [2, 128, NPAR], F32), ("w_ada", [2, D, 6144], F32), ("w_in_p", [2, D, WIN], F32),
        ("w_uq_p", [2, 384, 1024], F32), ("w_ukv_k", [2, 256, 512], F32), ("w_ukv_v", [2, 256, 512], F32),
        ("TB2", [2, 128, 5632], F32), ("w_out", [2, D, D], F32), ("w_gu", [2, D, 2 * DFF], F32),
        ("w_down", [2, DFF, D], F32)]
PER_LAYER = ("params", "w_ada", "w_in_p", "w_uq_p", "w_ukv_k", "w_ukv_v", "TB2", "w_out", "w_gu", "w_down")


def coll(P, semname, in_ap, out_ap, reads, writes, pw=()):
    P.mksem(semname)
    waits = P._waits("pool", reads, writes, pw)
    P.cnt[semname] += 1
    ev = (semname, P.cnt[semname])
    wl = [(P.sem[a], v) for a, v in waits]
    sh = P.sem[semname]

    def run(e, wl=wl, sh=sh):
        for h, v in wl:
            e.wait_ge(h, v)
        e.collective_compute("AllGather", ALU.bypass, replica_groups=PAIRS, ins=[in_ap], outs=[out_ap]).then_inc(sh)

    P.q["pool"].append(run)
    for r_ in reads:
        r_.r.append(ev)
    for w in writes:
        w.w = {ev[0]: ev[1]}
        w.r = []
    for w in pw:
        w.w[ev[0]] = ev[1]


def build_F():
    nc = bass.Bass("TRN2", target_bir_lowering=False)
    d = {n: ext(nc, n, s, dt, "ExternalInput") for n, s, dt in F_IN}
    xoT = ext(nc, "xoT", [D, NT], F32, "ExternalOutput")
    T = {}
    for n, s, dt in [("kT", [1024, NT], BF16), ("kTg", [2048, NT], BF16), ("vE", [NT, 520], BF16), ("vEg", [SEQ, 520], BF16),
                     ("nakp", [256, 2 * NHALO], BF16), ("nakg", [512, 2 * NHALO], BF16),
                     ("navp", [2 * NHALO, 260], BF16), ("navg", [4 * NHALO, 260], BF16),
                     ("up", [256, 2], F32), ("ug", [512, 2], F32)]:
        T[n] = nc.dram_tensor(n, s, dt)
    I = {}
    for n, s, dt in [("qT", [8, 128, NT], BF16), ("naqT", [256, NT], BF16), ("nakT", [256, NT], BF16), ("navE", [NT, 260], BF16),
                     ("uTh", [256, NT + 2], F32), ("gbT", [256, NT], F32), ("x1T", [D, NT], F32),
                     ("mixT", [12, 64, NT], BF16), ("wgu_s", [2, NJ, 128, 2048], BF16), ("wd_s", [2, 8, 128, NJ * 128], BF16)]:
        I[n] = ext(nc, n, s, dt, "Internal")
    R = {k: Res("R_" + k) for k in ("kT", "kTg", "vE", "vEg", "nakp", "nakg", "navp", "navg", "up", "ug", "qT", "naqT",
                                    "nakT", "navE", "uTh", "gbT", "x0", "x1", "xo", "mix")}
    with ExitStack() as es0:
        b = B(nc, es0)
        P = b.P
        rw = [dict(wgus=Res("wgus0"), wds=Res("wds0"), mix=R["mix"]), dict(wgus=Res("wgus1"), wds=Res("wds1"), mix=R["mix"])]
        preps = prep_common(b, es0, d)
        for l in range(2):
            dl = {k: (d[k][l] if k in PER_LAYER else d[k]) for k in d}
            dl["_prep"] = preps[l]
            if l == 0:
                dl["_after_w"] = lambda: prep_ffn_weights(b, {"w_gu": d["w_gu"][0], "w_down": d["w_down"][0]},
                                                          {"wgu_s": I["wgu_s"][0], "wd_s": I["wd_s"][0]}, rw[0])
            x_in, x_out = (d["x0T"], I["x1T"]) if l == 0 else (I["x1T"], xoT)
            R["x_in"], R["x_out"] = (R["x0"], R["x1"]) if l == 0 else (R["x1"], R["xo"])
            dl["xT"] = x_in
            o1 = dict(qT=I["qT"], kT=T["kT"].ap().rearrange("(h p) n -> h p n", p=128), vE=T["vE"].ap(), naqT=I["naqT"],
                      nakT=I["nakT"], navE=I["navE"], uTh=I["uTh"], gbT=I["gbT"])
            with ExitStack() as es:
                phase1(b, es, dl, o1, R)
                P.replay()
            if DBG.get("f_stop") == "p1":
                break
            P.dma("sp", "x_nakp", T["nakp"].ap()[:, 0:NHALO], I["nakT"][:, 0:NHALO], reads=[R["nakT"]], pwrites=[R["nakp"]])
            P.dma("sp", "x_nakp", T["nakp"].ap()[:, NHALO:2 * NHALO], I["nakT"][:, NT - NHALO:NT], reads=[R["nakT"]], pwrites=[R["nakp"]])
            P.dma("sp", "x_navp", T["navp"].ap()[0:NHALO, :], I["navE"][0:NHALO, :], reads=[R["navE"]], pwrites=[R["navp"]])
            P.dma("sp", "x_navp", T["navp"].ap()[NHALO:2 * NHALO, :], I["navE"][NT - NHALO:NT, :], reads=[R["navE"]], pwrites=[R["navp"]])
            P.dma("sp", "x_up", T["up"].ap()[:, 0:1], I["uTh"][:, 1:2], reads=[R["uTh"]], pwrites=[R["up"]], allow_slow_non_contiguous=True)
            P.dma("sp", "x_up", T["up"].ap()[:, 1:2], I["uTh"][:, NT:NT + 1], reads=[R["uTh"]], pwrites=[R["up"]], allow_slow_non_contiguous=True)
            for a, g in (("nakp", "nakg"), ("navp", "navg"), ("up", "ug")):
                coll(P, "cc_" + a, T[a].ap().opt(), T[g].ap().opt(), [R[a]], [R[g]])
            for cch in range(4):
                coll(P, "cc_kT", T["kT"].ap()[cch * 256:(cch + 1) * 256, :].opt(), T["kTg"].ap()[cch * 512:(cch + 1) * 512, :].opt(),
                     [R["kT"]], [R["kTg"]] if cch == 0 else (), pw=() if cch == 0 else [R["kTg"]])
            for cch in range(4):
                coll(P, "cc_vE", T["vE"].ap()[cch * 1024:(cch + 1) * 1024, :].opt(), T["vEg"].ap()[cch * 2048:(cch + 1) * 2048, :].opt(),
                     [R["vE"]], [R["vEg"]] if cch == 0 else (), pw=() if cch == 0 else [R["vEg"]])
            ugv = T["ug"].ap().rearrange("(r c) n -> r c n", r=2)
            P.dma("sp", "x_uh", I["uTh"][:, 0:1], ugv[0, :, 1:2], reads=[R["ug"]], pwrites=[R["uTh"]], allow_slow_non_contiguous=True)
            P.dma("sp", "x_uh", I["uTh"][:, NT + 1:NT + 2], ugv[1, :, 0:1], reads=[R["ug"]], pwrites=[R["uTh"]], allow_slow_non_contiguous=True)
            if DBG.get("f_stop") == "xch":
                P.final_wait("sp", [R["uTh"], R["kTg"], R["vEg"], R["nakg"], R["navg"]])
                break
            dl.update(kTg=T["kTg"].ap().rearrange("(c r h p) n -> c r h p n", c=4, r=2, h=2),
                      vEg=T["vEg"].ap().rearrange("(c r t) n -> c r t n", c=4, r=2), qT=I["qT"],
                      naqT=I["naqT"], nakT=I["nakT"], navE=I["navE"],
                      nakg=T["nakg"].ap().rearrange("(r c) n -> r c n", r=2), navg=T["navg"].ap().rearrange("(r t) n -> r t n", r=2),
                      uTh=I["uTh"], gbT=I["gbT"])
            scr = dict(mixT=I["mixT"], wgu_s=I["wgu_s"][l], wd_s=I["wd_s"][l])
            with ExitStack() as es:
                t = dl["_prep"]
                if l == 0:
                    prep_ffn_weights(b, {"w_gu": d["w_gu"][1], "w_down": d["w_down"][1]},
                                     {"wgu_s": I["wgu_s"][1], "wd_s": I["wd_s"][1]}, rw[1])
                phase2_na(b, dl, scr, rw[l], t, R)
                phase2_mla(b, dl, scr, rw[l], t, R)
                phase3(b, dl, {"xoT": x_out}, scr, rw[l], t, R)
            if DBG.get("f_stop") == "l0":
                break
        P.final_wait("sp", [R["xo"]])
        P.replay()
    return nc


_PROGS = {}


def kernel(**inp):
    if "F" not in _PROGS:
        _PROGS["F"] = build_F()
    Ls = [layer_host(inp, l) for l in range(2)]
    shared = {k: np.ascontiguousarray(np.stack([Ls[0][k], Ls[1][k]])) for k in PER_LAYER}
    shared["cmats"] = const_mats()
    shared["rc"] = rope_consts()
    maps = []
    for c in range(8):
        cs = core_static(inp, c)
        half = c % 2
        edge = np.zeros((128, 2), np.float32)
        edge[:, 0] = 1.0 if half == 1 else 0.0
        edge[:, 1] = 1.0 if half == 0 else 0.0
        m = dict(shared)
        m.update(x0T=cs["xT"], ccol=cs["ccol"], pos=cs["pos"], rowmask=cs["rowmask"], edge=edge)
        maps.append(m)
    res = run_bass_kernel_spmd(_PROGS["F"], maps, core_ids=list(range(8))).results
    out = np.empty((4, SEQ, D), np.float32)
    for c in range(8):
        out[c // 2, (c % 2) * NT:(c % 2 + 1) * NT, :] = np.asarray(res[c]["xoT"], np.float32).T
    return out
```
